# Optimizing a Trainium2 kernel written in Bass

```python
import jax, jax.numpy as jnp
from jax import lax
import numpy as np

D_MODEL = 1024
BATCH = 16
SEQ = 2048
DEPTH = 2

CHUNK = 64
Q_BLOCK = 128
PLE_DIM = 256
D_FF = 2816
EPS = 1e-6
LB_FLOOR = 1e-30
MLA_HEADS = 8
MLA_NOPE = 64
MLA_ROPE = 32
MLA_V = 64
MLA_Q_LORA = 384
MLA_KV_LORA = 256
ROPE_BASE = 10000.0
SB_HEADS = 8
SB_HEAD_DIM = 64
HG_HEADS = 4
HG_KEY = 128
HG_VAL = 128
HG_SUB = 16
N_BRANCH = 3

MLA_WIDTH = MLA_HEADS * MLA_V
SB_WIDTH = SB_HEADS * SB_HEAD_DIM
HG_KEY_WIDTH = HG_HEADS * HG_KEY
HG_VAL_WIDTH = HG_HEADS * HG_VAL
IN_SPLITS = (MLA_Q_LORA, MLA_KV_LORA, MLA_ROPE,
             SB_WIDTH, SB_WIDTH, SB_WIDTH,
             HG_KEY_WIDTH, HG_KEY_WIDTH, HG_VAL_WIDTH, HG_VAL_WIDTH,
             N_BRANCH * D_MODEL)
IN_WIDTH = sum(IN_SPLITS)

kernel_name = "hybrid_mla_stickbreak_hgrn2_macaron"


def rms_norm(x, w):
    x32 = x.astype(jnp.float32)
    y = x32 * lax.rsqrt(jnp.mean(x32 * x32, axis=-1, keepdims=True) + EPS)
    return (y * w.astype(jnp.float32)).astype(x.dtype)


def split_cols(t, sizes):
    out, start = [], 0
    for n in sizes:
        out.append(t[..., start:start + n])
        start += n
    return out


def swiglu(x, w_in, w_out):
    g, up = jnp.split(x @ w_in, 2, axis=-1)
    return (jax.nn.silu(g) * up) @ w_out


def rope_tables(positions):
    half = MLA_ROPE // 2
    inv = ROPE_BASE ** (-jnp.arange(half, dtype=jnp.float32) / half)
    ang = positions.astype(jnp.float32)[..., None] * inv
    return jnp.cos(ang)[:, :, None, :], jnp.sin(ang)[:, :, None, :]


def apply_rope(x, cos, sin):
    half = MLA_ROPE // 2
    x32 = x.astype(jnp.float32)
    x1, x2 = x32[..., :half], x32[..., half:]
    return jnp.concatenate([x1 * cos - x2 * sin, x2 * cos + x1 * sin], axis=-1).astype(x.dtype)


def mla_attention(c_q, c_kv, k_rope, q_norm, w_uq, kv_norm, w_ukv, cos, sin):
    B, S, _ = c_q.shape
    q = (rms_norm(c_q, q_norm) @ w_uq).reshape(B, S, MLA_HEADS, MLA_NOPE + MLA_ROPE)
    q_nope, q_rope = q[..., :MLA_NOPE], apply_rope(q[..., MLA_NOPE:], cos, sin)
    kv = (rms_norm(c_kv, kv_norm) @ w_ukv).reshape(B, S, MLA_HEADS, MLA_NOPE + MLA_V)
    k_nope, v = kv[..., :MLA_NOPE], kv[..., MLA_NOPE:]
    k_r = apply_rope(k_rope[:, :, None, :], cos, sin)[:, :, 0, :]
    scale = (MLA_NOPE + MLA_ROPE) ** -0.5
    chunk_id = jnp.arange(S) // CHUNK
    outs = []
    for q0 in range(0, S, Q_BLOCK):
        q1 = q0 + Q_BLOCK
        s = (jnp.einsum('bqhd,bkhd->bhqk', q_nope[:, q0:q1], k_nope[:, :q1])
             + jnp.einsum('bqhr,bkr->bhqk', q_rope[:, q0:q1], k_r[:, :q1]))
        s = s.astype(jnp.float32) * scale
        mask = chunk_id[q0:q1, None] >= chunk_id[None, :q1]
        pr = jax.nn.softmax(jnp.where(mask, s, -jnp.inf), axis=-1).astype(v.dtype)
        outs.append(jnp.einsum('bhqk,bkhd->bqhd', pr, v[:, :q1]))
    return jnp.concatenate(outs, axis=1).reshape(B, S, MLA_WIDTH)


def stick_breaking_attention(q, k, v):
    B, S, _ = q.shape
    q = q.reshape(B, S, SB_HEADS, SB_HEAD_DIM)
    k = k.reshape(B, S, SB_HEADS, SB_HEAD_DIM)
    v = v.reshape(B, S, SB_HEADS, SB_HEAD_DIM)
    scale = SB_HEAD_DIM ** -0.5
    pos = jnp.arange(S)
    outs = []
    for q0 in range(0, S, Q_BLOCK):
        q1 = q0 + Q_BLOCK
        z = jnp.einsum('bqhd,bkhd->bhqk', q[:, q0:q1], k[:, :q1]).astype(jnp.float32) * scale
        mask = pos[None, :q1] < pos[q0:q1, None]
        log_beta = jax.nn.log_sigmoid(z)
        log_keep = jnp.where(mask, jax.nn.log_sigmoid(-z), 0.0)
        log_rest = lax.cumsum(log_keep, axis=3, reverse=True) - log_keep
        a = jnp.where(mask, jnp.exp(jnp.minimum(log_beta + log_rest, 0.0)), 0.0).astype(v.dtype)
        outs.append(jnp.einsum('bhqk,bkhd->bqhd', a, v[:, :q1]))
    return jnp.concatenate(outs, axis=1).reshape(B, S, SB_WIDTH)


def hgrn2_recurrence(q_raw, f_raw, i_in, g_raw, lb, norm_w):
    B, S, _ = q_raw.shape
    N = S // CHUNK
    NS = CHUNK // HG_SUB
    xf = f_raw.astype(jnp.float32)
    lb = jnp.clip(lb.astype(jnp.float32), 0.0, 1.0 - 1e-6)
    log_f = jnp.logaddexp(jnp.log(jnp.maximum(lb, LB_FLOOR)), jnp.log1p(-lb) + jax.nn.log_sigmoid(xf))
    log_f = jnp.minimum(log_f, 0.0)
    k = (1.0 - lb) * jax.nn.sigmoid(-xf)
    q = jax.nn.silu(q_raw.astype(jnp.float32))
    v = i_in.astype(jnp.float32)

    def to_chunks(t, d):
        return t.reshape(B, N, CHUNK, HG_HEADS, d).transpose(0, 3, 1, 2, 4)

    q, k, log_f = to_chunks(q, HG_KEY), to_chunks(k, HG_KEY), to_chunks(log_f, HG_KEY)
    v = to_chunks(v, HG_VAL)
    b = jnp.cumsum(log_f, axis=3)

    b_ref = jnp.concatenate([jnp.zeros_like(b[:, :, :, :1]),
                             b[:, :, :, HG_SUB - 1:CHUNK - 1:HG_SUB]], axis=3)
    b_sub = b.reshape(B, HG_HEADS, N, NS, HG_SUB, HG_KEY)
    qf = q.reshape(B, HG_HEADS, N, NS, HG_SUB, HG_KEY) * jnp.exp(b_sub - b_ref[:, :, :, :, None, :])
    s_idx = jnp.arange(CHUNK)
    valid = s_idx[None, :] < ((jnp.arange(NS) + 1) * HG_SUB)[:, None]
    expo = jnp.where(valid[:, :, None], b_ref[:, :, :, :, None, :] - b[:, :, :, None, :, :], -jnp.inf)
    kf = k[:, :, :, None] * jnp.exp(expo)
    a = jnp.einsum('bhnilk,bhnisk->bhnils', qf, kf)
    t_idx = jnp.arange(CHUNK).reshape(NS, HG_SUB)
    causal = s_idx[None, None, :] <= t_idx[:, :, None]
    a = jnp.where(causal, a, 0.0)
    o_intra = jnp.einsum('bhnils,bhnsv->bhnilv', a, v).reshape(B, HG_HEADS, N, CHUNK, HG_VAL)

    b_last = b[:, :, :, -1, :]
    qd = q * jnp.exp(b)
    kd = k * jnp.exp(b_last[:, :, :, None, :] - b)
    decay = jnp.exp(b_last)

    def step(state, xs):
        qd_n, kd_n, v_n, decay_n = xs
        o_n = jnp.einsum('bhck,bhkv->bhcv', qd_n, state)
        state = decay_n[..., None] * state + jnp.einsum('bhck,bhcv->bhkv', kd_n, v_n)
        return state, o_n

    xs = (jnp.moveaxis(qd, 2, 0), jnp.moveaxis(kd, 2, 0), jnp.moveaxis(v, 2, 0), jnp.moveaxis(decay, 2, 0))
    state0 = jnp.zeros((B, HG_HEADS, HG_KEY, HG_VAL), jnp.float32)
    _, o_inter = lax.scan(step, state0, xs)
    o = o_intra + jnp.moveaxis(o_inter, 0, 2)
    o = o.transpose(0, 2, 3, 1, 4).reshape(B, S, HG_HEADS, HG_VAL)
    o = o * lax.rsqrt(jnp.mean(o * o, axis=-1, keepdims=True) + EPS)
    o = o * norm_w.astype(jnp.float32).reshape(HG_HEADS, HG_VAL)
    o = o.reshape(B, S, HG_VAL_WIDTH) * jax.nn.silu(g_raw.astype(jnp.float32))
    return o.astype(q_raw.dtype)


def setup_inputs(seed: int = 0) -> dict:
    key = jax.random.key(seed)
    ks = jax.random.split(key, 32)
    f32 = jnp.float32

    def w(k, shape, fan_in):
        return jax.random.normal(k, shape, f32) * (fan_in ** -0.5)

    def gain(k, shape):
        return 1.0 + 0.05 * jax.random.normal(k, shape, f32)

    x = jax.random.normal(ks[0], (BATCH, SEQ, D_MODEL), f32)
    p = jax.random.normal(ks[1], (DEPTH, BATCH, SEQ, PLE_DIM), f32)
    offsets = jax.random.randint(ks[2], (BATCH, 1), 0, 64) * CHUNK
    positions = (offsets + jnp.arange(SEQ, dtype=jnp.int32)[None, :]).astype(jnp.int32)
    return {
        "x": x,
        "p": p,
        "positions": positions,
        "ffn_a_norm": gain(ks[3], (DEPTH, D_MODEL)),
        "ffn_a_w_in": w(ks[4], (DEPTH, D_MODEL, 2 * D_FF), D_MODEL),
        "ffn_a_w_out": w(ks[5], (DEPTH, D_FF, D_MODEL), D_FF),
        "mix_norm": gain(ks[6], (DEPTH, D_MODEL)),
        "w_in": w(ks[7], (DEPTH, D_MODEL, IN_WIDTH), D_MODEL),
        "mla_q_norm": gain(ks[8], (DEPTH, MLA_Q_LORA)),
        "mla_w_uq": w(ks[9], (DEPTH, MLA_Q_LORA, MLA_HEADS * (MLA_NOPE + MLA_ROPE)), MLA_Q_LORA),
        "mla_kv_norm": gain(ks[10], (DEPTH, MLA_KV_LORA)),
        "mla_w_ukv": w(ks[11], (DEPTH, MLA_KV_LORA, MLA_HEADS * (MLA_NOPE + MLA_V)), MLA_KV_LORA),
        "hgrn_lower_bounds": 0.5 * jax.random.normal(ks[12], (DEPTH, HG_KEY_WIDTH), f32),
        "hgrn_out_norm": gain(ks[13], (DEPTH, HG_VAL_WIDTH)),
        "w_br_mla": w(ks[14], (DEPTH, MLA_WIDTH, D_MODEL), MLA_WIDTH),
        "w_br_sb": w(ks[15], (DEPTH, SB_WIDTH, D_MODEL), SB_WIDTH),
        "w_br_hgrn": w(ks[16], (DEPTH, HG_VAL_WIDTH, D_MODEL), HG_VAL_WIDTH),
        "w_out": w(ks[17], (DEPTH, D_MODEL, D_MODEL), D_MODEL),
        "ffn_b_norm": gain(ks[18], (DEPTH, D_MODEL)),
        "ffn_b_w_in": w(ks[19], (DEPTH, D_MODEL, 2 * D_FF), D_MODEL),
        "ffn_b_w_out": w(ks[20], (DEPTH, D_FF, D_MODEL), D_FF),
        "ple_norm": gain(ks[21], (DEPTH, D_MODEL)),
        "w_ple_gate": w(ks[22], (DEPTH, D_MODEL, D_MODEL), D_MODEL),
        "w_ple_proj": w(ks[23], (DEPTH, PLE_DIM, D_MODEL), PLE_DIM),
        "final_norm": gain(ks[24], (D_MODEL,)),
    }


def reference(x, p, positions, ffn_a_norm, ffn_a_w_in, ffn_a_w_out, mix_norm, w_in,
              mla_q_norm, mla_w_uq, mla_kv_norm, mla_w_ukv, hgrn_lower_bounds, hgrn_out_norm,
              w_br_mla, w_br_sb, w_br_hgrn, w_out, ffn_b_norm, ffn_b_w_in, ffn_b_w_out,
              ple_norm, w_ple_gate, w_ple_proj, final_norm):
    B, S, D = x.shape
    cos, sin = rope_tables(positions)
    lb_sm = jax.nn.softmax(hgrn_lower_bounds.astype(jnp.float32), axis=0)
    lb_all = jnp.concatenate([jnp.zeros_like(lb_sm[:1]), jnp.cumsum(lb_sm[1:], axis=0)], axis=0)
    h = x
    for i in range(DEPTH):
        h = h + 0.5 * swiglu(rms_norm(h, ffn_a_norm[i]), ffn_a_w_in[i], ffn_a_w_out[i])
        u = rms_norm(h, mix_norm[i])
        (c_q, c_kv, k_rope, sb_q, sb_k, sb_v,
         hg_q, hg_f, hg_i, hg_g, gate_logits) = split_cols(u @ w_in[i], IN_SPLITS)
        y_a = mla_attention(c_q, c_kv, k_rope, mla_q_norm[i], mla_w_uq[i],
                            mla_kv_norm[i], mla_w_ukv[i], cos, sin)
        y_b = stick_breaking_attention(sb_q, sb_k, sb_v)
        y_c = hgrn2_recurrence(hg_q, hg_f, hg_i, hg_g, lb_all[i], hgrn_out_norm[i])
        gates = jax.nn.sigmoid(gate_logits.astype(jnp.float32)).astype(h.dtype).reshape(B, S, N_BRANCH, D)
        merged = (gates[:, :, 0] * (y_a @ w_br_mla[i])
                  + gates[:, :, 1] * (y_b @ w_br_sb[i])
                  + gates[:, :, 2] * (y_c @ w_br_hgrn[i]))
        h = h + merged @ w_out[i]
        h = h + 0.5 * swiglu(rms_norm(h, ffn_b_norm[i]), ffn_b_w_in[i], ffn_b_w_out[i])
        h = h + (p[i] @ w_ple_proj[i]) * jax.nn.sigmoid(rms_norm(h, ple_norm[i]) @ w_ple_gate[i])
    return rms_norm(h, final_norm)
```

```python
import numpy as np
import concourse.bass as bass
import concourse.mybir as mybir
from concourse.bass_utils import run_bass_kernel_spmd

F32 = mybir.dt.float32
BF = mybir.dt.bfloat16
I32 = mybir.dt.int32
AF = mybir.ActivationFunctionType
ALU = mybir.AluOpType

D = 1024
DFF = 2816
NJ = DFF // 128
DEPTH = 2
PLE = 256
EPS = 1e-6
IN_W = 7328
O_CQ, O_CKV, O_KR = 0, 384, 640
O_SBQ, O_SBK, O_SBV = 672, 1184, 1696
O_HQ, O_HF, O_HI, O_HG = 2208, 2720, 3232, 3744
O_GATE = 4256

ENGS = ('pe', 'act', 'dve', 'pool', 'sp')
NDS = 8


class Op:
    __slots__ = ('eng', 'fn', 'deps', 'tok', 'dma', 'prev_dma')


class Prog:
    def __init__(self, nc):
        self.nc = nc
        self.ops = []
        self.last_w = {}
        self.readers = {}
        self.cnt = {e: 0 for e in ENGS}
        self.dcnt = {'sp': 0, 'pool': 0}
        self.dma_hist = {'sp': [], 'pool': []}
        self.last_real = {e: None for e in ENGS}
        self.pending_dma = []
        self.marks = []
        self.pe_count_at_op = {}

    def mark(self, name):
        self.marks.append((name, len(self.ops)))

    def add(self, eng, fn, r=(), w=(), dma=False):
        i = len(self.ops)
        deps = {}
        for k in r:
            lw = self.last_w.get(k)
            if lw is not None:
                deps[lw] = True
        for k in w:
            lw = self.last_w.get(k)
            if lw is not None:
                deps[lw] = True
            for rd in self.readers.get(k, ()):
                if rd not in deps:
                    deps[rd] = False
        for k in r:
            self.readers.setdefault(k, []).append(i)
        for k in w:
            self.last_w[k] = i
            self.readers[k] = []
        deps.pop(i, None)
        op = Op()
        op.eng, op.fn, op.deps, op.dma, op.prev_dma = eng, fn, deps, dma, None
        if fn is None:
            op.tok = None
        elif dma:
            n = self.dcnt[eng]
            self.dcnt[eng] += 1
            op.tok = ('d', eng, n % NDS, 16 * (n // NDS + 1))
            hist = self.dma_hist[eng]
            if n >= NDS:
                op.prev_dma = hist[n - NDS]
            hist.append(i)
            self.pending_dma.append(i)
        else:
            self.cnt[eng] += 1
            op.tok = ('c', eng, self.cnt[eng])
        if fn is not None:
            self.last_real[eng] = i
        self.ops.append(op)
        return i

    def fence(self):
        targets = [v for v in self.last_real.values() if v is not None] + list(self.pending_dma)
        self.pending_dma = []
        for e in ENGS:
            i = self.add(e, None)
            self.ops[i].deps = {t: True for t in targets}

    def emit(self):
        nc = self.nc
        sems = {e: nc.alloc_semaphore('s_' + e) for e in ENGS}
        dsems = {q: [nc.alloc_semaphore('d_%s%d' % (q, i)) for i in range(NDS)] for q in ('sp', 'pool')}
        per_eng = {e: [] for e in ENGS}
        for i, op in enumerate(self.ops):
            per_eng[op.eng].append(i)
        ops = self.ops

        def tok_sem(tok):
            if tok[0] == 'c':
                return sems[tok[1]], tok[2]
            return dsems[tok[1]][tok[2]], tok[3]

        pe_n = [0]

        class _Cnt:
            def matmul(self_, *a, **k):
                pe_n[0] += 1
                return nc.tensor.matmul(*a, **k)

            def transpose(self_, *a, **k):
                pe_n[0] += 1
                return nc.tensor.transpose(*a, **k)
        cnt_proxy = _Cnt()

        def run(ename, e):
            waited = {}
            for i in per_eng[ename]:
                op = ops[i]
                dl = sorted(op.deps.items())
                if op.prev_dma is not None:
                    dl.append((op.prev_dma, True))
                for j, true_dep in dl:
                    dj = ops[j]
                    if dj.tok is None:
                        continue
                    if (not dj.dma) and dj.eng == ename:
                        if ename in ('pe', 'sp'):
                            continue
                    s, v = tok_sem(dj.tok)
                    if waited.get(s.num, 0) >= v:
                        continue
                    e.wait_ge(s, v)
                    waited[s.num] = v
                if op.fn is None:
                    continue
                if ename == 'pe':
                    self.pe_count_at_op[i] = pe_n[0]
                    ins = op.fn(cnt_proxy)
                else:
                    ins = op.fn(e)
                s, v = tok_sem(op.tok)
                ins.then_inc(s, 16 if op.dma else 1)

        with nc.Block() as block:
            block.tensor(lambda e: run('pe', e))
            block.scalar(lambda e: run('act', e))
            block.vector(lambda e: run('dve', e))
            block.gpsimd(lambda e: run('pool', e))
            block.sync(lambda e: run('sp', e))
        self.pe_total = pe_n[0]


class Arena:
    def __init__(self, nc, nbytes):
        self.t = nc.alloc_sbuf_tensor('arena', [128, nbytes // 4], F32)
        self.size = nbytes
        self.off = 0

    def reset(self, off=0):
        self.off = off

    def alloc(self, free_shape, dtype):
        esz = 2 if dtype == BF else 4
        n = 1
        for s in free_shape:
            n *= s
        nb = (n * esz + 63) // 64 * 64
        assert self.off + nb <= self.size, ('arena overflow', self.off, nb, self.size)
        ap = self.t[:, self.off // 4:(self.off + nb) // 4]
        self.off += nb
        if dtype != F32:
            ap = ap.bitcast(dtype)
        ap = ap[:, 0:n]
        if len(free_shape) == 2:
            ap = ap.rearrange('p (a b) -> p a b', a=free_shape[0])
        elif len(free_shape) == 3:
            ap = ap.rearrange('p (a b c) -> p a b c', a=free_shape[0], b=free_shape[1])
        elif len(free_shape) == 4:
            ap = ap.rearrange('p (a b c d) -> p a b c d', a=free_shape[0], b=free_shape[1], c=free_shape[2])
        return ap


def build(nc, S, NSEQ, layers=(0, 1), phases=('ffa', 'mix', 'ffb', 'ple'), final_norm=True, dbg=None):
    NT = S // 512
    NB = S // 128
    P = Prog(nc)

    def din(name, shape, dt=F32):
        return nc.dram_tensor(name, list(shape), dt, kind='ExternalInput').ap()

    x_d = din('x', [NSEQ, S, D])
    p_d = din('p', [DEPTH, NSEQ, S, PLE])
    pos_d = din('positions', [NSEQ, S], I32)
    W = {}
    for nm, shp in [('ffn_a_norm', [DEPTH, D]), ('ffn_a_w_in', [DEPTH, D, 2 * DFF]), ('ffn_a_w_out', [DEPTH, DFF, D]),
                    ('mix_norm', [DEPTH, D]), ('w_in', [DEPTH, D, IN_W]), ('mla_q_norm', [DEPTH, 384]),
                    ('mla_w_uq', [DEPTH, 384, 768]), ('mla_kv_norm', [DEPTH, 256]), ('mla_w_ukv', [DEPTH, 256, 1024]),
                    ('hgrn_lower_bounds', [DEPTH, 512]), ('hgrn_out_norm', [DEPTH, 512]),
                    ('w_br_mla', [DEPTH, 512, D]), ('w_br_sb', [DEPTH, 512, D]), ('w_br_hgrn', [DEPTH, 512, D]),
                    ('w_out', [DEPTH, D, D]), ('ffn_b_norm', [DEPTH, D]), ('ffn_b_w_in', [DEPTH, D, 2 * DFF]),
                    ('ffn_b_w_out', [DEPTH, DFF, D]), ('ple_norm', [DEPTH, D]), ('w_ple_gate', [DEPTH, D, D]),
                    ('w_ple_proj', [DEPTH, PLE, D]), ('final_norm', [D])]:
        W[nm] = din(nm, shp)
    cst_d = din('cst', [128, NCST])
    out_d = nc.dram_tensor('out', [NSEQ, S, D], F32, kind='ExternalOutput').ap()

    hT = nc.alloc_sbuf_tensor('hT', [128, 8 * S], F32)[:, :].rearrange('p (c s) -> p c s', c=8)
    cstf = nc.alloc_sbuf_tensor('cstf', [128, NCST], F32)
    cstb = nc.alloc_sbuf_tensor('cstb', [128, NCST], BF)
    NW = 41
    nw = nc.alloc_sbuf_tensor('nw', [128, DEPTH * NW + 16], F32)
    lbt = nc.alloc_sbuf_tensor('lbt', [128, 16], F32)
    cosT = nc.alloc_sbuf_tensor('cosT', [128, S], F32)
    sinT = nc.alloc_sbuf_tensor('sinT', [128, S], F32)
    ps = [nc.alloc_psum_tensor('ps%d' % b, [128, 512], F32) for b in range(8)]
    arena_bytes = nc.sbuf_bytes_remaining - 1024
    arena_bytes = min(arena_bytes, 124 * 1024) // 64 * 64
    AR = Arena(nc, arena_bytes)

    ident_f = cstf[:, C_ID:C_ID + 128]
    ident_b = cstb[:, C_ID:C_ID + 128]
    ones_b = cstb[:, C_ONES:C_ONES + 128]
    uinc_b = cstb[:, C_UINC:C_UINC + 128]

    bank_rr = [0]

    def bank():
        b = bank_rr[0]
        bank_rr[0] = (b + 1) % 8
        return b

    def PSK(b):
        return ('ps', b)

    def dma(q, out, in_, r=(), w=()):
        eng = 'pool' if q == 'pool' else 'sp'
        return P.add(eng, lambda e: e.dma_start(out=out, in_=in_), r=r, w=w, dma=True)

    def mm(b_out, pairs, r=(), w=()):
        def fn(e):
            n = len(pairs)
            ins = None
            for i, (l, rr) in enumerate(pairs):
                ins = e.matmul(b_out, l, rr, start=(i == 0), stop=(i == n - 1))
            return ins
        return P.add('pe', fn, r=r, w=w)

    def act(out, in_, func, r=(), w=(), scale=1.0, bias=0.0):
        return P.add('act', lambda e: e.activation(out, in_, func, bias=bias, scale=scale), r=r, w=w)

    def ts(out, in0, s1, s2, op0, op1=None, r=(), w=(), eng='dve'):
        if op1 is None:
            return P.add(eng, lambda e: e.tensor_scalar(out, in0, s1, None, op0), r=r, w=w)
        return P.add(eng, lambda e: e.tensor_scalar(out, in0, s1, s2, op0, op1), r=r, w=w)

    def tt(out, in0, in1, op, r=(), w=(), eng='dve'):
        return P.add(eng, lambda e: e.tensor_tensor(out, in0, in1, op), r=r, w=w)

    def stt(out, in0, sc, in1, op0, op1, r=(), w=()):
        return P.add('dve', lambda e: e.scalar_tensor_tensor(out, in0, sc, in1, op0, op1), r=r, w=w)

    def cp(out, in_, r=(), w=(), eng='dve'):
        if eng == 'act':
            return P.add('act', lambda e: e.copy(out, in_), r=r, w=w)
        return P.add(eng, lambda e: e.tensor_copy(out, in_), r=r, w=w)

    dma('sp', cstf[:, :], cst_d, w=['cstf'])
    cp(cstb[:, :], cstf[:, :], r=['cstf'], w=['cstb'])
    nwk = 'nw'
    nwst = nc.alloc_sbuf_tensor('nwst', [128, 128], F32)
    P.add('dve', lambda e: e.memset(nwst[:, :], 0.0), w=['nwst'])
    for l in range(DEPTH):
        base = l * NW
        for nm, off, nch in [('ffn_a_norm', 0, 8), ('mix_norm', 8, 8), ('ffn_b_norm', 16, 8), ('ple_norm', 24, 8),
                             ('mla_q_norm', 32, 3), ('mla_kv_norm', 35, 2), ('hgrn_out_norm', 37, 4)]:
            dma('sp', nwst[base + off:base + off + nch, :], W[nm][l].rearrange('(c p) -> c p', p=128), w=['nwst'])
    dma('sp', nwst[DEPTH * NW:DEPTH * NW + 8, :], W['final_norm'].rearrange('(c p) -> c p', p=128), w=['nwst'])
    for l in range(DEPTH):
        r0 = DEPTH * NW + 8 + l * 4
        dma('sp', nwst[r0:r0 + 4, :], W['hgrn_lower_bounds'][l].rearrange('(c p) -> c p', p=128), w=['nwst'])
    P.add('pe', lambda e: e.transpose(ps[0][:, 0:128], nwst[:, :], ident_f), r=['nwst', 'cstf'], w=[PSK(0)])
    cp(nw[:, :], ps[0][:, 0:DEPTH * NW + 16], r=[PSK(0)], w=[nwk])

    def nwcol(l, off, c):
        i = l * NW + off + c
        return nw[:, i:i + 1]

    def rmsnorm_gen(src_fn, nch, dim, wcol_fn, dst_fn, rkeys, wkeys, tmp):
        sq = tmp['sq']
        kp = tmp.get('kp', '')
        b = tmp['bank'] if 'bank' in tmp else bank()
        rsk = kp + 'rs'
        for c in range(nch):
            s = c % 2
            act(sq[:, s, :], src_fn(c), AF.Square, r=list(rkeys), w=[(kp + 'sq', s)])
            P.add('pe', lambda e, c=c, s=s: e.matmul(ps[b][:, :], ones_b, sq[:, s, :], start=(c == 0), stop=(c == nch - 1)),
                  r=[(kp + 'sq', s), 'cstb'], w=[PSK(b)])
            yield
        rs = tmp['rs']
        act(rs, ps[b][:, :], AF.Ln, r=[PSK(b)], w=[rsk], scale=1.0 / dim, bias=tmp['eps'])
        act(rs, rs, AF.Exp, r=[rsk], w=[rsk], scale=-0.5)
        yield
        for c in range(nch):
            stt(dst_fn(c), src_fn(c), wcol_fn(c), rs, ALU.mult, ALU.mult, r=list(rkeys) + [rsk, nwk], w=list(wkeys))
            yield

    def rmsnorm_tile(src_fn, nch, dim, wcol_fn, dst_fn, rkeys, wkeys, tmp, src_is_psum=False):
        for _ in rmsnorm_gen(src_fn, nch, dim, wcol_fn, dst_fn, rkeys, wkeys, tmp):
            pass

    def load_seq(sq_i):
        AR.reset()
        stage = AR.alloc([2, D], F32)
        for tb in range(NB):
            s = tb % 2
            t = tb // 4
            dma('sp', stage[:, s, :], x_d[sq_i, tb * 128:(tb + 1) * 128, :], w=[('stg', s)])
            for half in range(2):
                b = bank()
                def fn(e, s=s, half=half, b=b):
                    ins = None
                    for i in range(4):
                        c = half * 4 + i
                        ins = e.transpose(ps[b][:, i * 128:(i + 1) * 128], stage[:, s, c * 128:(c + 1) * 128], ident_f)
                    return ins
                P.add('pe', fn, r=[('stg', s), 'cstf'], w=[PSK(b)])
                cp(hT[:, half * 4:half * 4 + 4, tb * 128:(tb + 1) * 128],
                   ps[b][:, :].rearrange('p (a b) -> p a b', a=4), r=[PSK(b)], w=[('h', t)],
                   eng=('act' if half else 'dve'))
        P.fence()

    def store_seq(sq_i):
        AR.reset()
        tmp = {'sq': AR.alloc([2, 512], BF), 'rs': AR.alloc([512], F32), 'eps': epsc}
        yT = AR.alloc([8, 512], F32)
        stage = AR.alloc([2, D], F32)
        for t in range(NT):
            tok = slice(t * 512, (t + 1) * 512)
            if final_norm:
                rmsnorm_tile(lambda c: hT[:, c, tok], 8, D, lambda c: nw[:, DEPTH * NW + c:DEPTH * NW + c + 1],
                             lambda c: yT[:, c, :], [('h', t)], ['yT'], tmp)
                src = yT
                srck = 'yT'
                sl = lambda c, j: yT[:, c, j * 128:(j + 1) * 128]
            else:
                srck = ('h', t)
                sl = lambda c, j: hT[:, c, t * 512 + j * 128:t * 512 + (j + 1) * 128]
            for j in range(4):
                tb = t * 4 + j
                s = tb % 2
                for half in range(2):
                    b = bank()
                    def fn(e, half=half, b=b, j=j, sl=sl):
                        ins = None
                        for i in range(4):
                            c = half * 4 + i
                            ins = e.transpose(ps[b][:, i * 128:(i + 1) * 128], sl(c, j), ident_f)
                        return ins
                    P.add('pe', fn, r=[srck, 'cstf'], w=[PSK(b)])
                    cp(stage[:, s, half * 512:(half + 1) * 512], ps[b][:, :], r=[PSK(b)], w=[('stg', s)],
                       eng=('act' if half else 'dve'))
                dma('sp', out_d[sq_i, tb * 128:(tb + 1) * 128, :], stage[:, s, :], r=[('stg', s)], w=[('out', sq_i, tb)])
        P.fence()

    def ffn(l, which):
        wi_d = W['ffn_%s_w_in' % which][l].rearrange('(k p) c -> p k c', p=128)
        wo_d = W['ffn_%s_w_out' % which][l].rearrange('(j p) c -> p j c', p=128)
        noff = 0 if which == 'a' else 16
        TS = min(S, 1024)
        NTS = TS // 512
        def ffn_norm(st, uT, tmp):
            for tt_ in range(NTS):
                t = st * NTS + tt_
                tok = slice(t * 512, (t + 1) * 512)
                utok = slice(tt_ * 512, (tt_ + 1) * 512)
                rmsnorm_tile(lambda c: hT[:, c, tok], 8, D, lambda c: nwcol(l, noff, c),
                             lambda c: uT[:, c, utok], [('h', t)], [('u', tt_)], tmp)

        for st in range(S // TS):
            AR.reset()
            tmp = {'sq': AR.alloc([2, 512], BF), 'rs': AR.alloc([512], F32), 'eps': epsc}
            uT = AR.alloc([8, TS], BF)
            aT = AR.alloc([NJ, TS], BF)
            wi = AR.alloc([2, 8, 512], BF)
            wo = AR.alloc([2, NJ, 128], BF)
            sg = AR.alloc([2, 512], F32)
            if st == 0:
                ffn_norm(st, uT, tmp)
            sgi = 0
            for jp in range(NJ // 2):
                s = jp % 2
                dma('pool', wi[:, s, :, 0:256], wi_d[:, :, jp * 256:(jp + 1) * 256], w=[('wi', s)])
                dma('pool', wi[:, s, :, 256:512], wi_d[:, :, DFF + jp * 256:DFF + (jp + 1) * 256], w=[('wi', s)])
                for jj in range(2):
                    j = jp * 2 + jj
                    for tt_ in range(NTS):
                        utok = slice(tt_ * 512, (tt_ + 1) * 512)
                        bg, bu = bank(), bank()
                        mm(ps[bg][:, :], [(wi[:, s, k, jj * 128:(jj + 1) * 128], uT[:, k, utok]) for k in range(8)],
                           r=[('wi', s), ('u', tt_)], w=[PSK(bg)])
                        mm(ps[bu][:, :], [(wi[:, s, k, 256 + jj * 128:256 + (jj + 1) * 128], uT[:, k, utok]) for k in range(8)],
                           r=[('wi', s), ('u', tt_)], w=[PSK(bu)])
                        q = sgi % 2
                        sgi += 1
                        act(sg[:, q, :], ps[bg][:, :], AF.Silu, r=[PSK(bg)], w=[('sg', q)])
                        tt(aT[:, j, utok], sg[:, q, :], ps[bu][:, :], ALU.mult, r=[('sg', q), PSK(bu)], w=[('a', j, tt_)])
            if st + 1 < S // TS:
                ffn_norm(st + 1, uT, tmp)
            for c in range(8):
                s = c % 2
                dma('pool', wo[:, s, :, :], wo_d[:, :, c * 128:(c + 1) * 128], w=[('wo', s)])
                for tt_ in range(NTS):
                    t = st * NTS + tt_
                    tok = slice(t * 512, (t + 1) * 512)
                    utok = slice(tt_ * 512, (tt_ + 1) * 512)
                    b = bank()
                    mm(ps[b][:, :], [(wo[:, s, j, :], aT[:, j, utok]) for j in range(NJ)],
                       r=[('wo', s)] + [('a', j, tt_) for j in range(NJ)], w=[PSK(b)])
                    stt(hT[:, c, tok], ps[b][:, :], 0.5, hT[:, c, tok], ALU.mult, ALU.add, r=[PSK(b), ('h', t)], w=[('h', t)])
        P.fence()

    def ple(l, sq_i):
        AR.reset()
        tmp = {'sq': AR.alloc([2, 512], BF), 'rs': AR.alloc([512], F32), 'eps': epsc}
        uT = AR.alloc([8, 512], BF)
        pT = AR.alloc([2, S], BF)
        wg = AR.alloc([8, D], BF)
        wp = AR.alloc([2, D], BF)
        stage = AR.alloc([2, PLE], F32)
        sgm = AR.alloc([2, 512], F32)
        dma('pool', wg, W['w_ple_gate'][l].rearrange('(k p) c -> p k c', p=128), w=['wg'])
        dma('pool', wp, W['w_ple_proj'][l].rearrange('(k p) c -> p k c', p=128), w=['wp'])
        for tb in range(NB):
            s = tb % 2
            dma('sp', stage[:, s, :], p_d[l, sq_i, tb * 128:(tb + 1) * 128, :], w=[('stg', s)])
            b = bank()
            def fn(e, s=s, b=b):
                ins = None
                for i in range(2):
                    ins = e.transpose(ps[b][:, i * 128:(i + 1) * 128], stage[:, s, i * 128:(i + 1) * 128], ident_f)
                return ins
            P.add('pe', fn, r=[('stg', s), 'cstf'], w=[PSK(b)])
            cp(pT[:, :, tb * 128:(tb + 1) * 128], ps[b][:, 0:256].rearrange('p (a b) -> p a b', a=2), r=[PSK(b)], w=[('pT', tb // 4)])
        for t in range(NT):
            tok = slice(t * 512, (t + 1) * 512)
            rmsnorm_tile(lambda c: hT[:, c, tok], 8, D, lambda c: nwcol(l, 24, c),
                         lambda c: uT[:, c, :], [('h', t)], ['u'], tmp)
            for c in range(8):
                cs = slice(c * 128, (c + 1) * 128)
                bg, bp = bank(), bank()
                mm(ps[bg][:, :], [(wg[:, k, cs], uT[:, k, :]) for k in range(8)], r=['wg', 'u'], w=[PSK(bg)])
                mm(ps[bp][:, :], [(wp[:, k, cs], pT[:, k, tok]) for k in range(2)], r=['wp', ('pT', t)], w=[PSK(bp)])
                q = c % 2
                act(sgm[:, q, :], ps[bg][:, :], AF.Sigmoid, r=[PSK(bg)], w=[('sg', q)])
                tt(sgm[:, q, :], sgm[:, q, :], ps[bp][:, :], ALU.mult, r=[('sg', q), PSK(bp)], w=[('sg', q)])
                tt(hT[:, c, tok], hT[:, c, tok], sgm[:, q, :], ALU.add, r=[('sg', q), ('h', t)], w=[('h', t)])
        P.fence()


    nuinc_b = cstb[:, C_NUINC:C_NUINC + 128]
    nones_b = cstb[:, C_NONES:C_NONES + 128]
    mmla_b = cstb[:, C_MMLA:C_MMLA + 128]
    msb_b = cstb[:, C_MSB:C_MSB + 128]
    msb_f = cstf[:, C_MSB:C_MSB + 128]
    mhg_f = cstf[:, C_MHG:C_MHG + 128]
    rst_f = cstf[:, C_RST:C_RST + 512]
    TWO_PI = 6.283185307179586
    C1 = 6.28125
    C2 = TWO_PI - C1

    def mm1(out, l, rr, start, stop, r=(), w=(), skip=False):
        return P.add('pe', lambda e: e.matmul(out, l, rr, start=start, stop=stop, skip_group_check=skip), r=r, w=w)

    def rope_tables(sq_i):
        AR.reset()
        posi = AR.alloc([S], I32 if False else F32)
        posi_i = posi.bitcast(I32)
        ang = AR.alloc([S], F32)
        kf = AR.alloc([S], F32)
        ki = AR.alloc([S], F32)
        ki_i = ki.bitcast(I32)
        dma('sp', posi_i, pos_d[sq_i:sq_i + 1, :].partition_broadcast(128), w=['posi'])
        cp(ang, posi_i, r=['posi'], w=['ang'])
        ts(ang, ang, cstf[:, C_INVF:C_INVF + 1], None, ALU.mult, r=['ang', 'cstf'], w=['ang'])
        for tab, shift, key in ((sinT, 0.0, 'sinT'), (cosT, np.pi / 2, 'cosT')):
            ts(kf, ang, shift, 1.0 / TWO_PI, ALU.add, ALU.mult, r=['ang'], w=['kf'])
            cp(ki_i, kf, r=['kf'], w=['ki'])
            cp(kf, ki_i, r=['ki'], w=['kf'])
            ts(tab[:, :], ang, shift, None, ALU.add, r=['ang'], w=[key])
            stt(tab[:, :], kf, -C1, tab[:, :], ALU.mult, ALU.add, r=['kf', key], w=[key])
            stt(tab[:, :], kf, -C2, tab[:, :], ALU.mult, ALU.add, r=['kf', key], w=[key])
            ts(kf, tab[:, :], np.pi, -TWO_PI, ALU.is_gt, ALU.mult, r=[key], w=['kf'])
            tt(tab[:, :], tab[:, :], kf, ALU.add, r=[key, 'kf'], w=[key])
            ts(kf, tab[:, :], -np.pi, TWO_PI, ALU.is_lt, ALU.mult, r=[key], w=['kf'])
            tt(tab[:, :], tab[:, :], kf, ALU.add, r=[key, 'kf'], w=[key])
            ts(tab[:, :], tab[:, :], 3.14159, -3.14159, ALU.min, ALU.max, r=[key], w=[key])
            act(tab[:, :], tab[:, :], AF.Sin, r=[key], w=[key])
        ts(sinT[:, :], sinT[:, :], cstf[:, C_SGN:C_SGN + 1], None, ALU.mult, r=['sinT', 'cstf'], w=['sinT'])
        P.fence()

    def mixer(l, sq_i, branches):
        AR.reset()
        yC = AR.alloc([4, S], BF)
        offC = AR.off
        yA = AR.alloc([4, S], BF)
        offA = AR.off
        yB = AR.alloc([4, S], BF)
        base0 = AR.off
        yv = {0: yA, 1: yB, 2: yC}
        w_in_d = W['w_in'][l].rearrange('(k p) c -> p k c', p=128)
        SC_A = 96.0 ** -0.5

        def mknorm(tmp):
            def norm_tile(t, uT):
                tok = slice(t * 512, (t + 1) * 512)
                rmsnorm_tile(lambda c: hT[:, c, tok], 8, D, lambda c: nwcol(l, 8, c),
                             lambda c: uT[:, c, :], [('h', t)], ['u'], tmp)
            return norm_tile

        def interleave(gens):
            gens = list(gens)
            while gens:
                for g_ in list(gens):
                    try:
                        next(g_)
                    except StopIteration:
                        gens.remove(g_)

        def bank_x(*excl):
            b = bank()
            while b in excl:
                b = bank()
            return b

        def branch_a():
            AR.reset(offA)
            cqn = AR.alloc([3, S], BF)
            ckvn = AR.alloc([2, S], BF)
            krT = AR.alloc([S], BF)
            base1 = AR.off
            wl = AR.alloc([8, 704], BF)
            dma('pool', wl[:, :, 0:672], w_in_d[:, :, 0:672], w=['wl'])
            dma('pool', wl[:, :, 672:688], w_in_d[:, :, 656:672], w=['wl'])
            dma('pool', wl[:, :, 688:704], w_in_d[:, :, 640:656], w=['wl'])
            NLC = 2 if NT >= 2 else 1
            CH = []
            for i in range(NLC):
                CH.append({'tmp': {'sq': AR.alloc([2, 512], BF), 'rs': AR.alloc([512], F32), 'eps': epsc, 'kp': 'L%d' % i, 'bank': 4 * i + 3},
                           'uT': AR.alloc([8, 512], BF), 't1': AR.alloc([512], F32), 't2': AR.alloc([512], F32)})

            def latent(t, i):
                c_ = CH[i]
                tmp, uT, t1, t2 = c_['tmp'], c_['uT'], c_['t1'], c_['t2']
                uk = ('uL', i)
                bb0 = 4 * i
                tok = slice(t * 512, (t + 1) * 512)
                yield from rmsnorm_gen(lambda c: hT[:, c, tok], 8, D, lambda c: nwcol(l, 8, c),
                                       lambda c: uT[:, c, :], [('h', t)], [uk], tmp)
                for (o0, nch, dim, noff, dst, dk) in ((0, 3, 384, 32, cqn, 'cqn'), (384, 2, 256, 35, ckvn, 'ckvn')):
                    bs = [bb0 + c for c in range(nch)]
                    for c in range(nch):
                        mm(ps[bs[c]][:, :], [(wl[:, k, o0 + c * 128:o0 + (c + 1) * 128], uT[:, k, :]) for k in range(8)],
                           r=['wl', uk], w=[PSK(bs[c])])
                        yield
                    yield from rmsnorm_gen(lambda c: ps[bs[c]][:, :], nch, dim, lambda c: nwcol(l, noff, c),
                                           lambda c: dst[:, c, tok], [PSK(b) for b in bs], [(dk, t)], tmp)
                ba, bb = bb0, bb0 + 1
                mm(ps[ba][0:96, :], [(wl[:, k, 576:672], uT[:, k, :]) for k in range(8)], r=['wl', uk], w=[PSK(ba)])
                mm(ps[bb][0:96, :], [(wl[:, k, 608:704], uT[:, k, :]) for k in range(8)], r=['wl', uk], w=[PSK(bb)])
                yield
                tt(t1[64:96, :], ps[ba][64:96, :], cosT[64:96, tok], ALU.mult, r=[PSK(ba), 'cosT'], w=[('t1', i)])
                tt(t2[64:96, :], ps[bb][64:96, :], sinT[64:96, tok], ALU.mult, r=[PSK(bb), 'sinT'], w=[('t2', i)])
                yield
                tt(krT[64:96, tok], t1[64:96, :], t2[64:96, :], ALU.add, r=[('t1', i), ('t2', i)], w=[('krT', t)])
                yield

            for t0 in range(0, NT, NLC):
                interleave([latent(t0 + i, i) for i in range(min(NLC, NT - t0))])
            P.fence()
            AR.reset(base1)
            wuq = AR.alloc([3, 768], BF)
            wsw = AR.alloc([3, 8, 96], BF)
            wukv = AR.alloc([2, 1024], BF)
            qh = AR.alloc([2, S], BF)
            kh = AR.alloc([2, S], BF)
            vh = AR.alloc([2, NB, 128], BF)
            pT = AR.alloc([4, 512], BF)
            t1 = AR.alloc([512], F32)
            t2 = AR.alloc([512], F32)
            wuq_d = W['mla_w_uq'][l].rearrange('(k p) c -> p k c', p=128)
            wuq_hd = W['mla_w_uq'][l].rearrange('(k p) (h d) -> p k h d', p=128, d=96)
            P.add('pool', lambda e: e.memset(wsw, 0.0), w=['wsw'])
            dma('pool', wuq, wuq_d, w=['wuq'])
            for k in range(3):
                dma('pool', wsw[:, k, :, 64:80], wuq_hd[:, k, :, 80:96], w=['wsw'])
                dma('pool', wsw[:, k, :, 80:96], wuq_hd[:, k, :, 64:80], w=['wsw'])
            dma('pool', wukv, W['mla_w_ukv'][l].rearrange('(k p) c -> p k c', p=128), w=['wukv'])
            P.add('dve', lambda e: e.memset(vh[:, :, :, 64:128], 1.0), w=[('vh', 0), ('vh', 1)])
            rec = AR.alloc([2, 512], F32)
            state = {'pti': 0, 'boi': 0, 'zi': 0}

            def proj(h):
                hs = h % 2
                b1, b2, b3, b4 = 0, 1, 2, 3
                for t in range(NT):
                    tok = slice(t * 512, (t + 1) * 512)
                    mm(ps[b1][0:96, :], [(wuq[:, k, h * 96:(h + 1) * 96], cqn[:, k, tok]) for k in range(3)],
                       r=['wuq', ('cqn', t)], w=[PSK(b1)])
                    mm(ps[b2][0:96, :], [(wsw[:, k, h, :], cqn[:, k, tok]) for k in range(3)],
                       r=['wsw', ('cqn', t)], w=[PSK(b2)])
                    mm(ps[b3][0:64, :], [(wukv[:, k, h * 128:h * 128 + 64], ckvn[:, k, tok]) for k in range(2)],
                       r=['wukv', ('ckvn', t)], w=[PSK(b3)])
                    for j in range(4):
                        blk = slice(t * 512 + j * 128, t * 512 + (j + 1) * 128)
                        mm(ps[b4][:, j * 64:(j + 1) * 64], [(ckvn[:, k, blk], wukv[:, k, h * 128 + 64:(h + 1) * 128]) for k in range(2)],
                           r=['wukv', ('ckvn', t)], w=[PSK(b4)])
                    act(qh[0:64, hs, tok], ps[b1][0:64, :], AF.Copy, r=[PSK(b1)], w=[('qh', hs)], scale=SC_A)
                    stt(t1[64:96, :], ps[b1][64:96, :], SC_A, cosT[64:96, tok], ALU.mult, ALU.mult, r=[PSK(b1), 'cosT'], w=['t1'])
                    stt(t2[64:96, :], ps[b2][64:96, :], SC_A, sinT[64:96, tok], ALU.mult, ALU.mult, r=[PSK(b2), 'sinT'], w=['t2'])
                    tt(qh[64:96, hs, tok], t1[64:96, :], t2[64:96, :], ALU.add, r=['t1', 't2'], w=[('qh', hs)])
                    cp(kh[0:64, hs, tok], ps[b3][0:64, :], r=[PSK(b3)], w=[('kh', hs)], eng='act')
                    cp(kh[64:96, hs, tok], krT[64:96, tok], r=[('krT', t)], w=[('kh', hs)], eng='pool')
                    cp(vh[:, hs, t * 4:(t + 1) * 4, 0:64], ps[b4][:, 0:256].rearrange('p (a b) -> p a b', a=4),
                       r=[PSK(b4)], w=[('vh', hs)])
                    yield

            def attn(h):
                hs = h % 2
                for tq in range(NT):
                    qtok = slice(tq * 512, (tq + 1) * 512)
                    bo = 4 + state['boi'] % 2
                    rsl = state['boi'] % 2
                    state['boi'] += 1
                    nk = 4 * (tq + 1)
                    pq = []
                    for kc in range(nk):
                        j = kc - 4 * tq
                        c0 = 128 * j if j > 0 else 0
                        bz = 6 + state['zi'] % 2
                        state['zi'] += 1
                        mm1(ps[bz][:, c0:512], kh[0:96, hs, kc * 128:(kc + 1) * 128], qh[0:96, hs, tq * 512 + c0:(tq + 1) * 512],
                            True, True, r=[('kh', hs), ('qh', hs)], w=[PSK(bz)])
                        q = state['pti'] % 4
                        state['pti'] += 1
                        act(pT[:, q, c0:512], ps[bz][:, c0:512], AF.Exp, r=[PSK(bz)], w=[('pT', q)])
                        if j >= 0:
                            tt(pT[:, q, c0:c0 + 128], pT[:, q, c0:c0 + 128], mmla_b, ALU.mult, r=[('pT', q), 'cstb'], w=[('pT', q)])
                        if len(pq) >= 2:
                            pq.pop(0)()
                        pq.append(lambda kc=kc, q=q, c0=c0: mm1(ps[bo][:, c0:512], vh[:, hs, kc, :], pT[:, q, c0:512], kc == 0, kc == nk - 1,
                                                                 r=[('vh', hs), ('pT', q)], w=[PSK(bo)]))
                        yield
                    while pq:
                        pq.pop(0)()
                    P.add('dve', lambda e, bo=bo, rsl=rsl: e.reciprocal(rec[0:64, rsl, :], ps[bo][64:128, :]), r=[PSK(bo)], w=[('rec', rsl)])
                    tt(yA[hs * 64:(hs + 1) * 64, h // 2, qtok], ps[bo][0:64, :], rec[0:64, rsl, :], ALU.mult, r=[PSK(bo), ('rec', rsl)], w=[('yT', 0)])
                    yield

            interleave([proj(0)])
            for h in range(8):
                gens = [attn(h)]
                if h < 7:
                    gens.append(proj(h + 1))
                interleave(gens)
            P.fence()

        def branch_c():
            AR.reset(offC)
            LB0 = DEPTH * NW + 8
            if l == 0:
                P.add('dve', lambda e: e.memset(lbt[:, 0:4], 0.0), w=['lbt'])
                P.add('dve', lambda e: e.memset(lbt[:, 4:8], 1.0), w=['lbt'])
                P.add('dve', lambda e: e.memset(lbt[:, 8:12], -1.0), w=['lbt'])
            else:
                tt(lbt[:, 12:16], nw[:, LB0:LB0 + 4], nw[:, LB0 + 4:LB0 + 8], ALU.subtract, r=[nwk], w=['lbt'])
                act(lbt[:, 12:16], lbt[:, 12:16], AF.Exp, r=['lbt'], w=['lbt'])
                ts(lbt[:, 12:16], lbt[:, 12:16], 1.0, None, ALU.add, r=['lbt'], w=['lbt'])
                P.add('dve', lambda e: e.reciprocal(lbt[:, 0:4], lbt[:, 12:16]), r=['lbt'], w=['lbt'])
                ts(lbt[:, 0:4], lbt[:, 0:4], 1.0 - 1e-6, 0.0, ALU.min, ALU.max, r=['lbt'], w=['lbt'])
                ts(lbt[:, 4:8], lbt[:, 0:4], -1.0, 1.0, ALU.mult, ALU.add, r=['lbt'], w=['lbt'])
                ts(lbt[:, 8:12], lbt[:, 4:8], -1.0, None, ALU.mult, r=['lbt'], w=['lbt'])
            NCH = S // 64
            uT = AR.alloc([8, S], BF)
            QF = AR.alloc([S], BF)
            KF = AR.alloc([S], BF)
            KFt = AR.alloc([NB, 128], BF)
            Vt = AR.alloc([NB, 128], BF)
            GS = AR.alloc([S], BF)
            dec = AR.alloc([NCH], F32)
            e1 = AR.alloc([NCH + 1], F32)
            e2 = AR.alloc([NCH], F32)
            St = AR.alloc([128], F32)
            Sbf = AR.alloc([2, 128], BF)
            Am = AR.alloc([2, 128], BF)
            tmp = {'sq': AR.alloc([2, 512], BF), 'rs': AR.alloc([512], F32), 'eps': epsc}
            whg = AR.alloc([8, 512], BF)
            NCHAIN = max(1, min(2, NT // 2)) if NT > 1 else 1
            osq = AR.alloc([512], BF)
            o1 = AR.alloc([512], F32)
            o2 = AR.alloc([512], F32)
            cb = [0]

            def cbank():
                b = cb[0] % 6
                cb[0] += 1
                return b
            pending_rec = None
            X = [[AR.alloc([512], F32) for _ in range(3)] for _ in range(NCHAIN)]
            B32 = [AR.alloc([512], F32) for _ in range(NCHAIN)]
            T8 = [AR.alloc([8], F32) for _ in range(NCHAIN)]
            for t in range(NT):
                tok = slice(t * 512, (t + 1) * 512)
                rmsnorm_tile(lambda c: hT[:, c, tok], 8, D, lambda c: nwcol(l, 8, c),
                             lambda c: uT[:, c, tok], [('h', t)], [('u', t)], tmp)
            for h in range(4):
                for i, o in enumerate((O_HQ, O_HF, O_HI, O_HG)):
                    dma('pool', whg[:, :, i * 128:(i + 1) * 128], w_in_d[:, :, o + h * 128:o + (h + 1) * 128], w=['whg'])
                lbc, omlc, nomlc = lbt[:, h:h + 1], lbt[:, 4 + h:5 + h], lbt[:, 8 + h:9 + h]

                def prep(t, sl):
                    x1, x2, x3 = X[sl]
                    b32 = B32[sl]
                    t8 = T8[sl]
                    xk = lambda i: ('x', sl, i)
                    tok = slice(t * 512, (t + 1) * 512)
                    ch = slice(t * 8, (t + 1) * 8)
                    b = cbank()
                    mm(ps[b][:, :], [(whg[:, k, 128:256], uT[:, k, tok]) for k in range(8)], r=['whg', ('u', t)], w=[PSK(b)])
                    act(x1, ps[b][:, :], AF.Exp, r=[PSK(b)], w=[xk(1)], scale=-1.0)
                    yield
                    b = cbank()
                    mm(ps[b][:, :], [(whg[:, k, 0:128], uT[:, k, tok]) for k in range(8)], r=['whg', ('u', t)], w=[PSK(b)])
                    act(QF[:, tok], ps[b][:, :], AF.Silu, r=[PSK(b)], w=[('QF', t)])
                    ts(x1, x1, 1.0, None, ALU.add, r=[xk(1)], w=[xk(1)])
                    yield
                    b = cbank()
                    mm(ps[b][:, :], [(whg[:, k, 384:512], uT[:, k, tok]) for k in range(8)], r=['whg', ('u', t)], w=[PSK(b)])
                    act(GS[:, tok], ps[b][:, :], AF.Silu, r=[PSK(b)], w=[('GS', t)])
                    P.add('dve', lambda e: e.reciprocal(x2, x1), r=[xk(1)], w=[xk(2)])
                    yield
                    b = cbank()
                    for j in range(4):
                        mm(ps[b][:, j * 128:(j + 1) * 128], [(uT[:, k, t * 512 + j * 128:t * 512 + (j + 1) * 128], whg[:, k, 256:384]) for k in range(8)],
                           r=['whg', ('u', t)], w=[PSK(b)])
                    cp(Vt[:, t * 4:(t + 1) * 4, :], ps[b][:, :].rearrange('p (a b) -> p a b', a=4), r=[PSK(b)], w=[('Vt', t)], eng='act')
                    ts(x1, x2, omlc, lbc, ALU.mult, ALU.add, r=[xk(2), 'lbt'], w=[xk(1)])
                    yield
                    act(x3, x1, AF.Ln, r=[xk(1)], w=[xk(3)])
                    ts(KF[:, tok], x2, nomlc, omlc, ALU.mult, ALU.add, r=[xk(2), 'lbt'], w=[('KF', t)])
                    yield
                    P.add('dve', lambda e: e.tensor_tensor_scan(b32, rst_f, x3, 0.0, ALU.mult, ALU.add),
                          r=[xk(3), 'cstf'], w=[('b32', sl)])
                    yield
                    b3 = b32.rearrange('p (c s) -> p c s', s=64)
                    tt(x1.rearrange('p (c s) -> p c s', s=64), b3, b3[:, :, 31:32].broadcast_to([128, 8, 64]), ALU.subtract,
                       r=[('b32', sl)], w=[xk(1)])
                    tt(t8, b3[:, :, 63], b3[:, :, 31], ALU.subtract, r=[('b32', sl)], w=[('t8', sl)])
                    yield
                    act(x3, x1, AF.Exp, r=[xk(1)], w=[xk(3)])
                    act(x1, x1, AF.Exp, r=[xk(1)], w=[xk(1)], scale=-1.0)
                    yield
                    act(e2[:, ch], t8, AF.Exp, r=[('t8', sl)], w=[('e2', t)])
                    tt(QF[:, tok], QF[:, tok], x3, ALU.mult, r=[('QF', t), xk(3)], w=[('QF', t)])
                    yield
                    act(dec[:, ch], b3[:, :, 63], AF.Exp, r=[('b32', sl)], w=[('dec', t)])
                    tt(KF[:, tok], KF[:, tok], x1, ALU.mult, r=[('KF', t), xk(1)], w=[('KF', t)])
                    yield
                    act(e1[:, ch], b3[:, :, 31], AF.Exp, r=[('b32', sl)], w=[('e1', t)])
                    kd = x2.bitcast(BF)
                    tt(kd[:, 0:512].rearrange('p (c s) -> p c s', s=64), KF[:, tok].rearrange('p (c s) -> p c s', s=64),
                       e2[:, ch].unsqueeze(2).broadcast_to([128, 8, 64]), ALU.mult, r=[('KF', t), ('e2', t), xk(2)], w=[xk(2)])
                    yield
                    b = cbank()
                    psb = ps[b][:, :].bitcast(BF)

                    def fn(e):
                        ins = None
                        for j in range(4):
                            ins = e.transpose(psb[:, j * 128:(j + 1) * 128], kd[:, j * 128:(j + 1) * 128], ident_b)
                        return ins
                    P.add('pe', fn, r=[xk(2), 'cstb'], w=[PSK(b)])
                    cp(KFt[:, t * 4:(t + 1) * 4, :], psb[:, 0:512].rearrange('p (a b) -> p a b', a=4), r=[PSK(b)], w=[('KFt', t)], eng='act')
                    yield

                def rec(tiles, h=h):
                    for t in tiles:
                        tok = slice(t * 512, (t + 1) * 512)
                        bo = 6 + t % 2
                        if t == 0:
                            P.add('dve', lambda e: e.memset(St, 0.0), w=['St'])
                            P.add('dve', lambda e: e.memset(Sbf[:, 0, :], 0.0), w=[('Sbf', 0)])
                        elif t == tiles[0]:
                            c0_ = 8 * t
                            ts(Sbf[:, c0_ % 2, :], St, e1[:, c0_:c0_ + 1], None, ALU.mult, r=['St', ('e1', t)], w=[('Sbf', c0_ % 2)])
                        for j in range(4):
                            m = t * 4 + j
                            blk = slice(m * 128, (m + 1) * 128)
                            cols = slice(j * 128, (j + 1) * 128)
                            ba = cbank()
                            mm1(ps[ba][:, 0:128], KF[:, blk], QF[:, blk], True, True, r=[('KF', t), ('QF', t)], w=[PSK(ba)])
                            a_ = m % 2
                            tt(Am[:, a_, :], ps[ba][:, 0:128], mhg_f, ALU.mult, r=[PSK(ba), 'cstf'], w=[('Am', a_)])
                            mm1(ps[bo][:, cols], Vt[:, m, :], Am[:, a_, :], True, False, r=[('Vt', t), ('Am', a_)], w=[PSK(bo)])
                            for cch in range(2):
                                c = 2 * m + cch
                                sl = c % 2
                                mm1(ps[bo][:, j * 128 + cch * 64:j * 128 + (cch + 1) * 64], Sbf[:, sl, :], QF[:, c * 64:(c + 1) * 64],
                                    False, cch == 1, r=[('Sbf', sl), ('QF', t)], w=[PSK(bo)])
                                if c == NCH - 1:
                                    continue
                                bs_ = cbank()
                                pr = slice(cch * 64, (cch + 1) * 64)
                                mm1(ps[bs_][:, 0:128], KFt[pr, m, :], Vt[pr, m, :], True, True, r=[('KFt', t), ('Vt', t)], w=[PSK(bs_)])
                                stt(St, St, dec[:, c:c + 1], ps[bs_][:, 0:128], ALU.mult, ALU.add, r=['St', PSK(bs_), ('dec', c // 8)], w=['St'])
                                if c != 8 * (tiles[-1] + 1) - 1:
                                    ts(Sbf[:, 1 - sl, :], St, e1[:, c + 1:c + 2], None, ALU.mult, r=['St', ('e1', (c + 1) // 8)], w=[('Sbf', 1 - sl)])
                                yield
                        act(osq[:, :], ps[bo][:, :], AF.Square, r=[PSK(bo)], w=['osq'])
                        bss = cbank()
                        mm1(ps[bss][:, :], ones_b, osq[:, :], True, True, r=['osq', 'cstb'], w=[PSK(bss)])
                        act(o1, ps[bss][:, :], AF.Ln, r=[PSK(bss)], w=['o1'], scale=1.0 / 128, bias=epsc)
                        yield
                        act(o1, o1, AF.Exp, r=['o1'], w=['o1'], scale=-0.5)
                        tt(o2, ps[bo][:, :], o1, ALU.mult, r=[PSK(bo), 'o1'], w=['o2'])
                        yield
                        stt(yC[:, h, tok], o2, nwcol(l, 37, h), GS[:, tok], ALU.mult, ALU.mult, r=['o2', ('GS', t), nwk], w=[('yT', 2)])
                        yield

                HALF = max(1, NT // 2)
                first = list(range(0, HALF))
                second = list(range(HALF, NT))
                g1 = [prep(t, i) for i, t in enumerate(first)]
                if pending_rec is not None:
                    g1.append(pending_rec)
                interleave(g1)
                g2 = [prep(t, i) for i, t in enumerate(second)]
                g2.append(rec(first))
                interleave(g2)
                pending_rec = rec(second) if second else None
            if pending_rec is not None:
                interleave([pending_rec])
            P.fence()

        if 'C' in branches:
            P.mark('mixC s%d l%d' % (sq_i, l))
            branch_c()

        if 'A' in branches:
            P.mark('mixA s%d l%d' % (sq_i, l))
            branch_a()
        def branch_b():
            for g in range(2):
                AR.reset(base0)
                qT = AR.alloc([2, S], BF)
                kT = AR.alloc([2, S], BF)
                vS = AR.alloc([NB, 256], BF)
                base1 = AR.off
                tmp = {'sq': AR.alloc([2, 512], BF), 'rs': AR.alloc([512], F32), 'eps': epsc}
                norm_tile = mknorm(tmp)
                uT = AR.alloc([8, 512], BF)
                wsb = AR.alloc([8, 768], BF)
                for i, o in enumerate((O_SBQ, O_SBK, O_SBV)):
                    dma('pool', wsb[:, :, i * 256:(i + 1) * 256], w_in_d[:, :, o + g * 256:o + (g + 1) * 256], w=['wsb'])
                for t in range(NT):
                    tok = slice(t * 512, (t + 1) * 512)
                    norm_tile(t, uT)
                    for cc in range(2):
                        bq, bk = bank(), bank()
                        mm(ps[bq][:, :], [(wsb[:, k, cc * 128:(cc + 1) * 128], uT[:, k, :]) for k in range(8)], r=['wsb', 'u'], w=[PSK(bq)])
                        mm(ps[bk][:, :], [(wsb[:, k, 256 + cc * 128:256 + (cc + 1) * 128], uT[:, k, :]) for k in range(8)], r=['wsb', 'u'], w=[PSK(bk)])
                        act(qT[:, cc, tok], ps[bq][:, :], AF.Copy, r=[PSK(bq)], w=[('qT', t)], scale=0.125)
                        cp(kT[:, cc, tok], ps[bk][:, :], r=[PSK(bk)], w=[('kT', t)])
                    for jj in range(2):
                        bv = bank()
                        for i in range(2):
                            j = jj * 2 + i
                            mm(ps[bv][:, i * 256:(i + 1) * 256], [(uT[:, k, j * 128:(j + 1) * 128], wsb[:, k, 512:768]) for k in range(8)],
                               r=['wsb', 'u'], w=[PSK(bv)])
                        cp(vS[:, t * 4 + jj * 2:t * 4 + jj * 2 + 2, :], ps[bv][:, :].rearrange('p (a b) -> p a b', a=2),
                           r=[PSK(bv)], w=[('vS', t)], eng='act')
                P.fence()
                AR.reset(base1)
                e32 = AR.alloc([2, 512], F32)
                spb = AR.alloc([3, 512], BF)
                ab = AR.alloc([3, 512], BF)
                R = AR.alloc([512], F32)
                Rb = AR.alloc([4, 512], BF)
                jobs = []
                for hl in range(4):
                    for tq in range(NT):
                        nk = 4 * (tq + 1)
                        for idx, kc in enumerate(range(nk - 1, -1, -1)):
                            jobs.append((hl, tq, nk, idx, kc))
                zb = [0]

                def zbank():
                    b = 2 + zb[0] % 6
                    zb[0] += 1
                    return b

                def mk(n, job):
                    hl, tq, nk, idx, kc = job
                    cc = hl // 2
                    pb = (hl % 2) * 64
                    tile_i = hl * NT + tq
                    bo = tile_i % 2
                    qtok = slice(tq * 512, (tq + 1) * 512)
                    j = kc - 4 * tq
                    c0 = 128 * j if j > 0 else 0
                    cs = slice(c0, 512)
                    q = n % 2
                    q3 = n % 3
                    kap = kT[pb:pb + 64, cc, kc * 128:(kc + 1) * 128]
                    qap = qT[pb:pb + 64, cc, tq * 512 + c0:(tq + 1) * 512]
                    rs_ = n % 4
                    rsl = (n - 1) % 4

                    def stage1():
                        if idx == 0:
                            P.add('pool', lambda e: e.memset(R[:, 256:512], 0.0), w=[('R', 1)])
                            P.add('dve', lambda e: e.memset(R[:, 0:256], 0.0), w=[('R', 0)])
                        bz = zbank()
                        mm1(ps[bz][:, cs], kap, qap, True, True, r=[('kT', kc // 4), ('qT', tq)], w=[PSK(bz)])
                        act(e32[:, q, cs], ps[bz][:, cs], AF.Exp, r=[PSK(bz)], w=[('e32', q)])
                        act(spb[:, q3, cs], e32[:, q, cs], AF.Ln, r=[('e32', q)], w=[('spb', q3)], bias=onec)
                        if j >= 0:
                            tt(spb[:, q3, c0:c0 + 128], spb[:, q3, c0:c0 + 128], msb_b, ALU.mult, r=[('spb', q3), 'cstb'], w=[('spb', q3)])
                        if idx < nk - 1:
                            jn = j - 1
                            c0n = 128 * jn if jn > 0 else 0
                            hi0 = max(c0, 256)
                            tt(R[:, hi0:512], R[:, hi0:512], spb[:, q3, hi0:512], ALU.add, r=[('R', 1), ('spb', q3)], w=[('R', 1)], eng='pool')
                            if c0 < 256:
                                tt(R[:, c0:256], R[:, c0:256], spb[:, q3, c0:256], ALU.add, r=[('R', 0), ('spb', q3)], w=[('R', 0)])
                            cp(Rb[:, rs_, c0n:512], R[:, c0n:512], r=[('R', 0), ('R', 1)], w=[('Rb', rs_)])

                    def stage2():
                        bc = zbank()

                        def fn(e):
                            e.matmul(ps[bc][:, cs], kap, qap, start=True, stop=False)
                            ins = e.matmul(ps[bc][:, cs], nuinc_b, spb[:, q3, cs], start=False, stop=(idx == 0))
                            if idx > 0:
                                ins = e.matmul(ps[bc][:, cs], nones_b, Rb[:, rsl, cs], start=False, stop=True)
                            return ins
                        P.add('pe', fn, r=[('kT', kc // 4), ('qT', tq), ('spb', q3), 'cstb'] + ([('Rb', rsl)] if idx > 0 else []), w=[PSK(bc)])
                        act(ab[:, q3, cs], ps[bc][:, cs], AF.Exp, r=[PSK(bc)], w=[('ab', q3)])
                        if j >= 0:
                            tt(ab[:, q3, c0:c0 + 128], ab[:, q3, c0:c0 + 128], msb_b, ALU.mult, r=[('ab', q3), 'cstb'], w=[('ab', q3)])

                    def stage3():
                        mm1(ps[bo][0:64, cs], vS[:, kc, hl * 64:(hl + 1) * 64], ab[:, q3, cs], idx == 0, idx == nk - 1,
                            r=[('vS', kc // 4), ('ab', q3)], w=[PSK(bo)], skip=True)
                        if idx == nk - 1:
                            cp(yB[pb:pb + 64, g * 2 + cc, qtok], ps[bo][0:64, :], r=[PSK(bo)], w=[('yT', 1)], eng='act')
                    return stage1, stage2, stage3

                st = [mk(n, job) for n, job in enumerate(jobs)]
                NJB = len(st)
                for n in range(NJB + 2):
                    if n < NJB:
                        st[n][0]()
                    if 0 <= n - 1 < NJB:
                        st[n - 1][1]()
                    if 0 <= n - 2 < NJB:
                        st[n - 2][2]()
                P.fence()

        if 'B' in branches:
            P.mark('mixB s%d l%d' % (sq_i, l))
            branch_b()

        def merge():
            AR.reset(base0)
            tmp = {'sq': AR.alloc([2, 512], BF), 'rs': AR.alloc([512], F32), 'eps': epsc}
            TP = min(S, 1024)
            NTP = TP // 512
            uT = AR.alloc([8, TP], BF)
            mT = AR.alloc([8, TP], BF)
            wgt = AR.alloc([2, 8, 384], BF)
            wbr = AR.alloc([2, 3, 4, 128], BF)
            wo2 = AR.alloc([2, 8, 128], BF)
            sg = AR.alloc([2, 512], F32)
            acc = AR.alloc([2, 512], F32)
            wout_d = W['w_out'][l].rearrange('(k p) c -> p k c', p=128)
            brw = [W[n][l].rearrange('(k p) c -> p k c', p=128) for n in ('w_br_mla', 'w_br_sb', 'w_br_hgrn')]
            brs = [i for i, b in enumerate('ABC') if b in branches]
            wi_ = 0
            sgi = 0
            aci = 0
            for tp in range(S // TP):
                for tt_ in range(NTP):
                    t = tp * NTP + tt_
                    tok = slice(t * 512, (t + 1) * 512)
                    utok = slice(tt_ * 512, (tt_ + 1) * 512)
                    rmsnorm_tile(lambda c: hT[:, c, tok], 8, D, lambda c: nwcol(l, 8, c),
                                 lambda c: uT[:, c, utok], [('h', t)], [('u', tt_)], tmp)
                for c in range(8):
                    s = wi_ % 2
                    wi_ += 1
                    for br in brs:
                        dma('pool', wgt[:, s, :, br * 128:(br + 1) * 128],
                            w_in_d[:, :, O_GATE + br * 1024 + c * 128:O_GATE + br * 1024 + (c + 1) * 128], w=[('wgt', s)])
                        dma('pool', wbr[:, s, br, :, :], brw[br][:, :, c * 128:(c + 1) * 128], w=[('wbr', s)])
                    for tt_ in range(NTP):
                        t = tp * NTP + tt_
                        tok = slice(t * 512, (t + 1) * 512)
                        utok = slice(tt_ * 512, (tt_ + 1) * 512)
                        a_ = aci % 2
                        aci += 1
                        for bi, br in enumerate(brs):
                            bg, bp = bank(), bank()
                            mm(ps[bg][:, :], [(wgt[:, s, k, br * 128:(br + 1) * 128], uT[:, k, utok]) for k in range(8)],
                               r=[('wgt', s), ('u', tt_)], w=[PSK(bg)])
                            mm(ps[bp][:, :], [(wbr[:, s, br, k, :], yv[br][:, k, tok]) for k in range(4)],
                               r=[('wbr', s), ('yT', br)], w=[PSK(bp)])
                            q = sgi % 2
                            sgi += 1
                            act(sg[:, q, :], ps[bg][:, :], AF.Sigmoid, r=[PSK(bg)], w=[('sg', q)])
                            last = (bi == len(brs) - 1)
                            dst = mT[:, c, utok] if last else acc[:, a_, :]
                            dk = [('mT', c, tt_)] if last else [('acc', a_)]
                            if bi == 0:
                                tt(dst, sg[:, q, :], ps[bp][:, :], ALU.mult, r=[('sg', q), PSK(bp)], w=dk)
                            else:
                                tt(sg[:, q, :], sg[:, q, :], ps[bp][:, :], ALU.mult, r=[('sg', q), PSK(bp)], w=[('sg', q)])
                                tt(dst, acc[:, a_, :], sg[:, q, :], ALU.add, r=[('sg', q), ('acc', a_)], w=dk)
                for c2 in range(8):
                    s = c2 % 2
                    dma('pool', wo2[:, s, :, :], wout_d[:, :, c2 * 128:(c2 + 1) * 128], w=[('wo2', s)])
                    for tt_ in range(NTP):
                        t = tp * NTP + tt_
                        tok = slice(t * 512, (t + 1) * 512)
                        utok = slice(tt_ * 512, (tt_ + 1) * 512)
                        b = bank()
                        mm(ps[b][:, :], [(wo2[:, s, k, :], mT[:, k, utok]) for k in range(8)],
                           r=[('wo2', s)] + [('mT', k, tt_) for k in range(8)], w=[PSK(b)])
                        tt(hT[:, c2, tok], hT[:, c2, tok], ps[b][:, :], ALU.add, r=[PSK(b), ('h', t)], w=[('h', t)])
            P.fence()

        P.mark('merge s%d l%d' % (sq_i, l))
        merge()

    epst = nc.alloc_sbuf_tensor('epst', [128, 1], F32)
    epsc = epst[:, 0:1]
    P.add('dve', lambda e: e.memset(epst[:, :], EPS), w=['eps'])
    onet = nc.alloc_sbuf_tensor('onet', [128, 1], F32)
    onec = onet[:, 0:1]
    P.add('dve', lambda e: e.memset(onet[:, :], 1.0), w=['one'])
    P.fence()

    for sq_i in range(NSEQ):
        if 'mix' in phases:
            P.mark('rope s%d' % sq_i)
            rope_tables(sq_i)
        P.mark('load s%d' % sq_i)
        load_seq(sq_i)
        for l in layers:
            if 'ffa' in phases:
                P.mark('ffa s%d l%d' % (sq_i, l))
                ffn(l, 'a')
            if 'mix' in phases:
                brs = ''.join(b for b in 'ABC' if ('mix' + b) in phases) or 'ABC'
                mixer(l, sq_i, brs)
            if 'ffb' in phases:
                P.mark('ffb s%d l%d' % (sq_i, l))
                ffn(l, 'b')
            if 'ple' in phases:
                P.mark('ple s%d l%d' % (sq_i, l))
                ple(l, sq_i)
        P.mark('store s%d' % sq_i)
        store_seq(sq_i)

    P.fence()
    P.emit()
    nc._prog = P
    return nc


C_ID, C_ONES, C_UINC, C_NUINC, C_NONES, C_MMLA, C_MSB, C_MHG, C_RST, C_INVF, C_SGN = 0, 128, 256, 384, 512, 640, 768, 896, 1024, 1536, 1537
NCST = 1540


def make_consts():
    c = np.zeros((128, NCST), np.float32)
    c[:, C_ID:C_ID + 128] = np.eye(128, dtype=np.float32)
    c[:, C_ONES:C_ONES + 128] = 1.0
    j = np.arange(128)[:, None]
    k = np.arange(128)[None, :]
    c[:, C_UINC:C_UINC + 128] = (j >= k).astype(np.float32)
    c[:, C_NUINC:C_NUINC + 128] = -(j >= k).astype(np.float32)
    c[:, C_NONES:C_NONES + 128] = -1.0
    c[:, C_MMLA:C_MMLA + 128] = 1.0 - ((j >= 64) & (k < 64)).astype(np.float32)
    c[:, C_MSB:C_MSB + 128] = (j < k).astype(np.float32)
    c[:, C_MHG:C_MHG + 128] = ((j // 64 == k // 64) & (j <= k)).astype(np.float32)
    c[:, C_RST:C_RST + 512] = (np.arange(512) % 64 != 0).astype(np.float32)[None, :]
    half = 16
    inv = (10000.0 ** (-np.arange(half, dtype=np.float32) / half)).astype(np.float32)
    pp = np.arange(128)
    c[:, C_INVF] = inv[pp % 16]
    c[:, C_SGN] = np.where((pp % 32) < 16, -1.0, 1.0)
    return c


WNAMES = ['ffn_a_norm', 'ffn_a_w_in', 'ffn_a_w_out', 'mix_norm', 'w_in', 'mla_q_norm', 'mla_w_uq', 'mla_kv_norm',
          'mla_w_ukv', 'hgrn_lower_bounds', 'hgrn_out_norm', 'w_br_mla', 'w_br_sb', 'w_br_hgrn', 'w_out',
          'ffn_b_norm', 'ffn_b_w_in', 'ffn_b_w_out', 'ple_norm', 'w_ple_gate', 'w_ple_proj', 'final_norm']


def run(inputs, n_cores=8, **bkw):
    x = np.ascontiguousarray(np.asarray(inputs['x'], dtype=np.float32))
    p = np.ascontiguousarray(np.asarray(inputs['p'], dtype=np.float32))
    pos = np.ascontiguousarray(np.asarray(inputs['positions'], dtype=np.int32))
    B, S, _ = x.shape
    nseq = B // n_cores
    nc = bass.Bass("TRN2", target_bir_lowering=False)
    build(nc, S, nseq, **bkw)
    cst = make_consts()
    wts = {k: np.ascontiguousarray(np.asarray(inputs[k], dtype=np.float32)) for k in WNAMES}
    in_maps = []
    for c in range(n_cores):
        m = {'x': x[c * nseq:(c + 1) * nseq], 'p': np.ascontiguousarray(p[:, c * nseq:(c + 1) * nseq]),
             'positions': pos[c * nseq:(c + 1) * nseq], 'cst': cst}
        m.update(wts)
        in_maps.append(m)
    res = run_bass_kernel_spmd(nc, in_maps, core_ids=list(range(n_cores)))
    return np.concatenate([np.asarray(r['out']) for r in res.results], axis=0).astype(np.float32)


def kernel(**inputs):
    return run(inputs, n_cores=8)
```

```python
import numpy as np
import concourse.bass as bass
import concourse.mybir as mybir
from concourse.bass_utils import run_bass_kernel_spmd

F32 = mybir.dt.float32
BF = mybir.dt.bfloat16
I32 = mybir.dt.int32
AF = mybir.ActivationFunctionType
ALU = mybir.AluOpType

D = 1024
DFF = 2816
NJ = DFF // 128
DEPTH = 2
PLE = 256
EPS = 1e-6
IN_W = 7328
O_CQ, O_CKV, O_KR = 0, 384, 640
O_SBQ, O_SBK, O_SBV = 672, 1184, 1696
O_HQ, O_HF, O_HI, O_HG = 2208, 2720, 3232, 3744
O_GATE = 4256

ENGS = ('pe', 'act', 'dve', 'pool', 'sp')
NDS = 8


class Op:
    __slots__ = ('eng', 'fn', 'deps', 'tok', 'dma', 'prev_dma')


class Prog:
    def __init__(self, nc):
        self.nc = nc
        self.ops = []
        self.last_w = {}
        self.readers = {}
        self.cnt = {e: 0 for e in ENGS}
        self.dcnt = {'sp': 0, 'pool': 0}
        self.dma_hist = {'sp': [], 'pool': []}
        self.last_real = {e: None for e in ENGS}
        self.pending_dma = []
        self.marks = []
        self.pe_count_at_op = {}

    def mark(self, name):
        self.marks.append((name, len(self.ops)))

    def add(self, eng, fn, r=(), w=(), dma=False):
        i = len(self.ops)
        deps = {}
        for k in r:
            lw = self.last_w.get(k)
            if lw is not None:
                deps[lw] = True
        for k in w:
            lw = self.last_w.get(k)
            if lw is not None:
                deps[lw] = True
            for rd in self.readers.get(k, ()):
                if rd not in deps:
                    deps[rd] = False
        for k in r:
            self.readers.setdefault(k, []).append(i)
        for k in w:
            self.last_w[k] = i
            self.readers[k] = []
        deps.pop(i, None)
        op = Op()
        op.eng, op.fn, op.deps, op.dma, op.prev_dma = eng, fn, deps, dma, None
        if fn is None:
            op.tok = None
        elif dma:
            n = self.dcnt[eng]
            self.dcnt[eng] += 1
            op.tok = ('d', eng, n % NDS, 16 * (n // NDS + 1))
            hist = self.dma_hist[eng]
            if n >= NDS:
                op.prev_dma = hist[n - NDS]
            hist.append(i)
            self.pending_dma.append(i)
        else:
            self.cnt[eng] += 1
            op.tok = ('c', eng, self.cnt[eng])
        if fn is not None:
            self.last_real[eng] = i
        self.ops.append(op)
        return i

    def fence(self):
        targets = [v for v in self.last_real.values() if v is not None] + list(self.pending_dma)
        self.pending_dma = []
        for e in ENGS:
            i = self.add(e, None)
            self.ops[i].deps = {t: True for t in targets}

    def emit(self):
        nc = self.nc
        sems = {e: nc.alloc_semaphore('s_' + e) for e in ENGS}
        dsems = {q: [nc.alloc_semaphore('d_%s%d' % (q, i)) for i in range(NDS)] for q in ('sp', 'pool')}
        per_eng = {e: [] for e in ENGS}
        for i, op in enumerate(self.ops):
            per_eng[op.eng].append(i)
        ops = self.ops

        def tok_sem(tok):
            if tok[0] == 'c':
                return sems[tok[1]], tok[2]
            return dsems[tok[1]][tok[2]], tok[3]

        pe_n = [0]

        class _Cnt:
            def matmul(self_, *a, **k):
                pe_n[0] += 1
                return nc.tensor.matmul(*a, **k)

            def transpose(self_, *a, **k):
                pe_n[0] += 1
                return nc.tensor.transpose(*a, **k)
        cnt_proxy = _Cnt()

        def run(ename, e):
            waited = {}
            for i in per_eng[ename]:
                op = ops[i]
                dl = sorted(op.deps.items())
                if op.prev_dma is not None:
                    dl.append((op.prev_dma, True))
                for j, true_dep in dl:
                    dj = ops[j]
                    if dj.tok is None:
                        continue
                    if (not dj.dma) and dj.eng == ename:
                        if ename in ('pe', 'sp'):
                            continue
                    s, v = tok_sem(dj.tok)
                    if waited.get(s.num, 0) >= v:
                        continue
                    e.wait_ge(s, v)
                    waited[s.num] = v
                if op.fn is None:
                    continue
                if ename == 'pe':
                    self.pe_count_at_op[i] = pe_n[0]
                    ins = op.fn(cnt_proxy)
                else:
                    ins = op.fn(e)
                s, v = tok_sem(op.tok)
                ins.then_inc(s, 16 if op.dma else 1)

        with nc.Block() as block:
            block.tensor(lambda e: run('pe', e))
            block.scalar(lambda e: run('act', e))
            block.vector(lambda e: run('dve', e))
            block.gpsimd(lambda e: run('pool', e))
            block.sync(lambda e: run('sp', e))
        self.pe_total = pe_n[0]


class Arena:
    def __init__(self, nc, nbytes):
        self.t = nc.alloc_sbuf_tensor('arena', [128, nbytes // 4], F32)
        self.size = nbytes
        self.off = 0

    def reset(self, off=0):
        self.off = off

    def alloc(self, free_shape, dtype):
        esz = 2 if dtype == BF else 4
        n = 1
        for s in free_shape:
            n *= s
        nb = (n * esz + 63) // 64 * 64
        assert self.off + nb <= self.size, ('arena overflow', self.off, nb, self.size)
        ap = self.t[:, self.off // 4:(self.off + nb) // 4]
        self.off += nb
        if dtype != F32:
            ap = ap.bitcast(dtype)
        ap = ap[:, 0:n]
        if len(free_shape) == 2:
            ap = ap.rearrange('p (a b) -> p a b', a=free_shape[0])
        elif len(free_shape) == 3:
            ap = ap.rearrange('p (a b c) -> p a b c', a=free_shape[0], b=free_shape[1])
        elif len(free_shape) == 4:
            ap = ap.rearrange('p (a b c d) -> p a b c d', a=free_shape[0], b=free_shape[1], c=free_shape[2])
        return ap


def build(nc, S, NSEQ, layers=(0, 1), phases=('ffa', 'mix', 'ffb', 'ple'), final_norm=True, dbg=None):
    NT = S // 512
    NB = S // 128
    P = Prog(nc)

    def din(name, shape, dt=F32):
        return nc.dram_tensor(name, list(shape), dt, kind='ExternalInput').ap()

    x_d = din('x', [NSEQ, S, D])
    p_d = din('p', [DEPTH, NSEQ, S, PLE])
    pos_d = din('positions', [NSEQ, S], I32)
    W = {}
    for nm, shp in [('ffn_a_norm', [DEPTH, D]), ('ffn_a_w_in', [DEPTH, D, 2 * DFF]), ('ffn_a_w_out', [DEPTH, DFF, D]),
                    ('mix_norm', [DEPTH, D]), ('w_in', [DEPTH, D, IN_W]), ('mla_q_norm', [DEPTH, 384]),
                    ('mla_w_uq', [DEPTH, 384, 768]), ('mla_kv_norm', [DEPTH, 256]), ('mla_w_ukv', [DEPTH, 256, 1024]),
                    ('hgrn_lower_bounds', [DEPTH, 512]), ('hgrn_out_norm', [DEPTH, 512]),
                    ('w_br_mla', [DEPTH, 512, D]), ('w_br_sb', [DEPTH, 512, D]), ('w_br_hgrn', [DEPTH, 512, D]),
                    ('w_out', [DEPTH, D, D]), ('ffn_b_norm', [DEPTH, D]), ('ffn_b_w_in', [DEPTH, D, 2 * DFF]),
                    ('ffn_b_w_out', [DEPTH, DFF, D]), ('ple_norm', [DEPTH, D]), ('w_ple_gate', [DEPTH, D, D]),
                    ('w_ple_proj', [DEPTH, PLE, D]), ('final_norm', [D])]:
        W[nm] = din(nm, shp)
    cst_d = din('cst', [128, NCST])
    out_d = nc.dram_tensor('out', [NSEQ, S, D], F32, kind='ExternalOutput').ap()

    hT = nc.alloc_sbuf_tensor('hT', [128, 8 * S], F32)[:, :].rearrange('p (c s) -> p c s', c=8)
    cstf = nc.alloc_sbuf_tensor('cstf', [128, NCST], F32)
    cstb = nc.alloc_sbuf_tensor('cstb', [128, NCST], BF)
    NW = 41
    nw = nc.alloc_sbuf_tensor('nw', [128, DEPTH * NW + 16], F32)
    lbt = nc.alloc_sbuf_tensor('lbt', [128, 16], F32)
    cosT = nc.alloc_sbuf_tensor('cosT', [128, S], F32)
    sinT = nc.alloc_sbuf_tensor('sinT', [128, S], F32)
    ps = [nc.alloc_psum_tensor('ps%d' % b, [128, 512], F32) for b in range(8)]
    arena_bytes = nc.sbuf_bytes_remaining - 1024
    arena_bytes = min(arena_bytes, 124 * 1024) // 64 * 64
    AR = Arena(nc, arena_bytes)

    ident_f = cstf[:, C_ID:C_ID + 128]
    ident_b = cstb[:, C_ID:C_ID + 128]
    ones_b = cstb[:, C_ONES:C_ONES + 128]
    uinc_b = cstb[:, C_UINC:C_UINC + 128]

    bank_rr = [0]

    def bank():
        b = bank_rr[0]
        bank_rr[0] = (b + 1) % 8
        return b

    def PSK(b):
        return ('ps', b)

    def dma(q, out, in_, r=(), w=()):
        eng = 'pool' if q == 'pool' else 'sp'
        return P.add(eng, lambda e: e.dma_start(out=out, in_=in_), r=r, w=w, dma=True)

    def mm(b_out, pairs, r=(), w=()):
        def fn(e):
            n = len(pairs)
            ins = None
            for i, (l, rr) in enumerate(pairs):
                ins = e.matmul(b_out, l, rr, start=(i == 0), stop=(i == n - 1))
            return ins
        return P.add('pe', fn, r=r, w=w)

    def act(out, in_, func, r=(), w=(), scale=1.0, bias=0.0):
        return P.add('act', lambda e: e.activation(out, in_, func, bias=bias, scale=scale), r=r, w=w)

    def ts(out, in0, s1, s2, op0, op1=None, r=(), w=(), eng='dve'):
        if op1 is None:
            return P.add(eng, lambda e: e.tensor_scalar(out, in0, s1, None, op0), r=r, w=w)
        return P.add(eng, lambda e: e.tensor_scalar(out, in0, s1, s2, op0, op1), r=r, w=w)

    def tt(out, in0, in1, op, r=(), w=(), eng='dve'):
        return P.add(eng, lambda e: e.tensor_tensor(out, in0, in1, op), r=r, w=w)

    def stt(out, in0, sc, in1, op0, op1, r=(), w=()):
        return P.add('dve', lambda e: e.scalar_tensor_tensor(out, in0, sc, in1, op0, op1), r=r, w=w)

    def cp(out, in_, r=(), w=(), eng='dve'):
        if eng == 'act':
            return P.add('act', lambda e: e.copy(out, in_), r=r, w=w)
        return P.add(eng, lambda e: e.tensor_copy(out, in_), r=r, w=w)

    dma('sp', cstf[:, :], cst_d, w=['cstf'])
    cp(cstb[:, :], cstf[:, :], r=['cstf'], w=['cstb'])
    nwk = 'nw'
    nwst = nc.alloc_sbuf_tensor('nwst', [128, 128], F32)
    P.add('dve', lambda e: e.memset(nwst[:, :], 0.0), w=['nwst'])
    for l in range(DEPTH):
        base = l * NW
        for nm, off, nch in [('ffn_a_norm', 0, 8), ('mix_norm', 8, 8), ('ffn_b_norm', 16, 8), ('ple_norm', 24, 8),
                             ('mla_q_norm', 32, 3), ('mla_kv_norm', 35, 2), ('hgrn_out_norm', 37, 4)]:
            dma('sp', nwst[base + off:base + off + nch, :], W[nm][l].rearrange('(c p) -> c p', p=128), w=['nwst'])
    dma('sp', nwst[DEPTH * NW:DEPTH * NW + 8, :], W['final_norm'].rearrange('(c p) -> c p', p=128), w=['nwst'])
    for l in range(DEPTH):
        r0 = DEPTH * NW + 8 + l * 4
        dma('sp', nwst[r0:r0 + 4, :], W['hgrn_lower_bounds'][l].rearrange('(c p) -> c p', p=128), w=['nwst'])
    P.add('pe', lambda e: e.transpose(ps[0][:, 0:128], nwst[:, :], ident_f), r=['nwst', 'cstf'], w=[PSK(0)])
    cp(nw[:, :], ps[0][:, 0:DEPTH * NW + 16], r=[PSK(0)], w=[nwk])

    def nwcol(l, off, c):
        i = l * NW + off + c
        return nw[:, i:i + 1]

    def rmsnorm_gen(src_fn, nch, dim, wcol_fn, dst_fn, rkeys, wkeys, tmp):
        sq = tmp['sq']
        kp = tmp.get('kp', '')
        b = tmp['bank'] if 'bank' in tmp else bank()
        rsk = kp + 'rs'
        for c in range(nch):
            s = c % 2
            act(sq[:, s, :], src_fn(c), AF.Square, r=list(rkeys), w=[(kp + 'sq', s)])
            P.add('pe', lambda e, c=c, s=s: e.matmul(ps[b][:, :], ones_b, sq[:, s, :], start=(c == 0), stop=(c == nch - 1)),
                  r=[(kp + 'sq', s), 'cstb'], w=[PSK(b)])
            yield
        rs = tmp['rs']
        act(rs, ps[b][:, :], AF.Ln, r=[PSK(b)], w=[rsk], scale=1.0 / dim, bias=tmp['eps'])
        act(rs, rs, AF.Exp, r=[rsk], w=[rsk], scale=-0.5)
        yield
        for c in range(nch):
            stt(dst_fn(c), src_fn(c), wcol_fn(c), rs, ALU.mult, ALU.mult, r=list(rkeys) + [rsk, nwk], w=list(wkeys))
            yield

    def rmsnorm_tile(src_fn, nch, dim, wcol_fn, dst_fn, rkeys, wkeys, tmp, src_is_psum=False):
        for _ in rmsnorm_gen(src_fn, nch, dim, wcol_fn, dst_fn, rkeys, wkeys, tmp):
            pass

    def load_seq(sq_i):
        AR.reset()
        stage = AR.alloc([2, D], F32)
        for tb in range(NB):
            s = tb % 2
            t = tb // 4
            dma('sp', stage[:, s, :], x_d[sq_i, tb * 128:(tb + 1) * 128, :], w=[('stg', s)])
            for half in range(2):
                b = bank()
                def fn(e, s=s, half=half, b=b):
                    ins = None
                    for i in range(4):
                        c = half * 4 + i
                        ins = e.transpose(ps[b][:, i * 128:(i + 1) * 128], stage[:, s, c * 128:(c + 1) * 128], ident_f)
                    return ins
                P.add('pe', fn, r=[('stg', s), 'cstf'], w=[PSK(b)])
                cp(hT[:, half * 4:half * 4 + 4, tb * 128:(tb + 1) * 128],
                   ps[b][:, :].rearrange('p (a b) -> p a b', a=4), r=[PSK(b)], w=[('h', t)],
                   eng=('act' if half else 'dve'))
        P.fence()

    def store_seq(sq_i):
        AR.reset()
        tmp = {'sq': AR.alloc([2, 512], BF), 'rs': AR.alloc([512], F32), 'eps': epsc}
        yT = AR.alloc([8, 512], F32)
        stage = AR.alloc([2, D], F32)
        for t in range(NT):
            tok = slice(t * 512, (t + 1) * 512)
            if final_norm:
                rmsnorm_tile(lambda c: hT[:, c, tok], 8, D, lambda c: nw[:, DEPTH * NW + c:DEPTH * NW + c + 1],
                             lambda c: yT[:, c, :], [('h', t)], ['yT'], tmp)
                src = yT
                srck = 'yT'
                sl = lambda c, j: yT[:, c, j * 128:(j + 1) * 128]
            else:
                srck = ('h', t)
                sl = lambda c, j: hT[:, c, t * 512 + j * 128:t * 512 + (j + 1) * 128]
            for j in range(4):
                tb = t * 4 + j
                s = tb % 2
                for half in range(2):
                    b = bank()
                    def fn(e, half=half, b=b, j=j, sl=sl):
                        ins = None
                        for i in range(4):
                            c = half * 4 + i
                            ins = e.transpose(ps[b][:, i * 128:(i + 1) * 128], sl(c, j), ident_f)
                        return ins
                    P.add('pe', fn, r=[srck, 'cstf'], w=[PSK(b)])
                    cp(stage[:, s, half * 512:(half + 1) * 512], ps[b][:, :], r=[PSK(b)], w=[('stg', s)],
                       eng=('act' if half else 'dve'))
                dma('sp', out_d[sq_i, tb * 128:(tb + 1) * 128, :], stage[:, s, :], r=[('stg', s)], w=[('out', sq_i, tb)])
        P.fence()

    def ffn(l, which):
        wi_d = W['ffn_%s_w_in' % which][l].rearrange('(k p) c -> p k c', p=128)
        wo_d = W['ffn_%s_w_out' % which][l].rearrange('(j p) c -> p j c', p=128)
        noff = 0 if which == 'a' else 16
        TS = min(S, 1024)
        NTS = TS // 512
        def ffn_norm(st, uT, tmp):
            for tt_ in range(NTS):
                t = st * NTS + tt_
                tok = slice(t * 512, (t + 1) * 512)
                utok = slice(tt_ * 512, (tt_ + 1) * 512)
                rmsnorm_tile(lambda c: hT[:, c, tok], 8, D, lambda c: nwcol(l, noff, c),
                             lambda c: uT[:, c, utok], [('h', t)], [('u', tt_)], tmp)

        for st in range(S // TS):
            AR.reset()
            tmp = {'sq': AR.alloc([2, 512], BF), 'rs': AR.alloc([512], F32), 'eps': epsc}
            uT = AR.alloc([8, TS], BF)
            aT = AR.alloc([NJ, TS], BF)
            wi = AR.alloc([2, 8, 512], BF)
            wo = AR.alloc([2, NJ, 128], BF)
            sg = AR.alloc([2, 512], F32)
            if st == 0:
                ffn_norm(st, uT, tmp)
            sgi = 0
            for jp in range(NJ // 2):
                s = jp % 2
                dma('pool', wi[:, s, :, 0:256], wi_d[:, :, jp * 256:(jp + 1) * 256], w=[('wi', s)])
                dma('pool', wi[:, s, :, 256:512], wi_d[:, :, DFF + jp * 256:DFF + (jp + 1) * 256], w=[('wi', s)])
                for jj in range(2):
                    j = jp * 2 + jj
                    for tt_ in range(NTS):
                        utok = slice(tt_ * 512, (tt_ + 1) * 512)
                        bg, bu = bank(), bank()
                        mm(ps[bg][:, :], [(wi[:, s, k, jj * 128:(jj + 1) * 128], uT[:, k, utok]) for k in range(8)],
                           r=[('wi', s), ('u', tt_)], w=[PSK(bg)])
                        mm(ps[bu][:, :], [(wi[:, s, k, 256 + jj * 128:256 + (jj + 1) * 128], uT[:, k, utok]) for k in range(8)],
                           r=[('wi', s), ('u', tt_)], w=[PSK(bu)])
                        q = sgi % 2
                        sgi += 1
                        act(sg[:, q, :], ps[bg][:, :], AF.Silu, r=[PSK(bg)], w=[('sg', q)])
                        tt(aT[:, j, utok], sg[:, q, :], ps[bu][:, :], ALU.mult, r=[('sg', q), PSK(bu)], w=[('a', j, tt_)])
            if st + 1 < S // TS:
                ffn_norm(st + 1, uT, tmp)
            for c in range(8):
                s = c % 2
                dma('pool', wo[:, s, :, :], wo_d[:, :, c * 128:(c + 1) * 128], w=[('wo', s)])
                for tt_ in range(NTS):
                    t = st * NTS + tt_
                    tok = slice(t * 512, (t + 1) * 512)
                    utok = slice(tt_ * 512, (tt_ + 1) * 512)
                    b = bank()
                    mm(ps[b][:, :], [(wo[:, s, j, :], aT[:, j, utok]) for j in range(NJ)],
                       r=[('wo', s)] + [('a', j, tt_) for j in range(NJ)], w=[PSK(b)])
                    stt(hT[:, c, tok], ps[b][:, :], 0.5, hT[:, c, tok], ALU.mult, ALU.add, r=[PSK(b), ('h', t)], w=[('h', t)])
        P.fence()

    def ple(l, sq_i):
        AR.reset()
        tmp = {'sq': AR.alloc([2, 512], BF), 'rs': AR.alloc([512], F32), 'eps': epsc}
        uT = AR.alloc([8, 512], BF)
        pT = AR.alloc([2, S], BF)
        wg = AR.alloc([8, D], BF)
        wp = AR.alloc([2, D], BF)
        stage = AR.alloc([2, PLE], F32)
        sgm = AR.alloc([2, 512], F32)
        dma('pool', wg, W['w_ple_gate'][l].rearrange('(k p) c -> p k c', p=128), w=['wg'])
        dma('pool', wp, W['w_ple_proj'][l].rearrange('(k p) c -> p k c', p=128), w=['wp'])
        for tb in range(NB):
            s = tb % 2
            dma('sp', stage[:, s, :], p_d[l, sq_i, tb * 128:(tb + 1) * 128, :], w=[('stg', s)])
            b = bank()
            def fn(e, s=s, b=b):
                ins = None
                for i in range(2):
                    ins = e.transpose(ps[b][:, i * 128:(i + 1) * 128], stage[:, s, i * 128:(i + 1) * 128], ident_f)
                return ins
            P.add('pe', fn, r=[('stg', s), 'cstf'], w=[PSK(b)])
            cp(pT[:, :, tb * 128:(tb + 1) * 128], ps[b][:, 0:256].rearrange('p (a b) -> p a b', a=2), r=[PSK(b)], w=[('pT', tb // 4)])
        for t in range(NT):
            tok = slice(t * 512, (t + 1) * 512)
            rmsnorm_tile(lambda c: hT[:, c, tok], 8, D, lambda c: nwcol(l, 24, c),
                         lambda c: uT[:, c, :], [('h', t)], ['u'], tmp)
            for c in range(8):
                cs = slice(c * 128, (c + 1) * 128)
                bg, bp = bank(), bank()
                mm(ps[bg][:, :], [(wg[:, k, cs], uT[:, k, :]) for k in range(8)], r=['wg', 'u'], w=[PSK(bg)])
                mm(ps[bp][:, :], [(wp[:, k, cs], pT[:, k, tok]) for k in range(2)], r=['wp', ('pT', t)], w=[PSK(bp)])
                q = c % 2
                act(sgm[:, q, :], ps[bg][:, :], AF.Sigmoid, r=[PSK(bg)], w=[('sg', q)])
                tt(sgm[:, q, :], sgm[:, q, :], ps[bp][:, :], ALU.mult, r=[('sg', q), PSK(bp)], w=[('sg', q)])
                tt(hT[:, c, tok], hT[:, c, tok], sgm[:, q, :], ALU.add, r=[('sg', q), ('h', t)], w=[('h', t)])
        P.fence()


    nuinc_b = cstb[:, C_NUINC:C_NUINC + 128]
    nones_b = cstb[:, C_NONES:C_NONES + 128]
    mmla_b = cstb[:, C_MMLA:C_MMLA + 128]
    msb_b = cstb[:, C_MSB:C_MSB + 128]
    msb_f = cstf[:, C_MSB:C_MSB + 128]
    mhg_f = cstf[:, C_MHG:C_MHG + 128]
    rst_f = cstf[:, C_RST:C_RST + 512]
    TWO_PI = 6.283185307179586
    C1 = 6.28125
    C2 = TWO_PI - C1

    def mm1(out, l, rr, start, stop, r=(), w=(), skip=False):
        return P.add('pe', lambda e: e.matmul(out, l, rr, start=start, stop=stop, skip_group_check=skip), r=r, w=w)

    def rope_tables(sq_i):
        AR.reset()
        posi = AR.alloc([S], I32 if False else F32)
        posi_i = posi.bitcast(I32)
        ang = AR.alloc([S], F32)
        kf = AR.alloc([S], F32)
        ki = AR.alloc([S], F32)
        ki_i = ki.bitcast(I32)
        dma('sp', posi_i, pos_d[sq_i:sq_i + 1, :].partition_broadcast(128), w=['posi'])
        cp(ang, posi_i, r=['posi'], w=['ang'])
        ts(ang, ang, cstf[:, C_INVF:C_INVF + 1], None, ALU.mult, r=['ang', 'cstf'], w=['ang'])
        for tab, shift, key in ((sinT, 0.0, 'sinT'), (cosT, np.pi / 2, 'cosT')):
            ts(kf, ang, shift, 1.0 / TWO_PI, ALU.add, ALU.mult, r=['ang'], w=['kf'])
            cp(ki_i, kf, r=['kf'], w=['ki'])
            cp(kf, ki_i, r=['ki'], w=['kf'])
            ts(tab[:, :], ang, shift, None, ALU.add, r=['ang'], w=[key])
            stt(tab[:, :], kf, -C1, tab[:, :], ALU.mult, ALU.add, r=['kf', key], w=[key])
            stt(tab[:, :], kf, -C2, tab[:, :], ALU.mult, ALU.add, r=['kf', key], w=[key])
            ts(kf, tab[:, :], np.pi, -TWO_PI, ALU.is_gt, ALU.mult, r=[key], w=['kf'])
            tt(tab[:, :], tab[:, :], kf, ALU.add, r=[key, 'kf'], w=[key])
            ts(kf, tab[:, :], -np.pi, TWO_PI, ALU.is_lt, ALU.mult, r=[key], w=['kf'])
            tt(tab[:, :], tab[:, :], kf, ALU.add, r=[key, 'kf'], w=[key])
            ts(tab[:, :], tab[:, :], 3.14159, -3.14159, ALU.min, ALU.max, r=[key], w=[key])
            act(tab[:, :], tab[:, :], AF.Sin, r=[key], w=[key])
        ts(sinT[:, :], sinT[:, :], cstf[:, C_SGN:C_SGN + 1], None, ALU.mult, r=['sinT', 'cstf'], w=['sinT'])
        P.fence()

    def mixer(l, sq_i, branches):
        AR.reset()
        yC = AR.alloc([4, S], BF)
        offC = AR.off
        yA = AR.alloc([4, S], BF)
        offA = AR.off
        yB = AR.alloc([4, S], BF)
        base0 = AR.off
        yv = {0: yA, 1: yB, 2: yC}
        w_in_d = W['w_in'][l].rearrange('(k p) c -> p k c', p=128)
        SC_A = 96.0 ** -0.5

        def mknorm(tmp):
            def norm_tile(t, uT):
                tok = slice(t * 512, (t + 1) * 512)
                rmsnorm_tile(lambda c: hT[:, c, tok], 8, D, lambda c: nwcol(l, 8, c),
                             lambda c: uT[:, c, :], [('h', t)], ['u'], tmp)
            return norm_tile

        def interleave(gens):
            gens = list(gens)
            while gens:
                for g_ in list(gens):
                    try:
                        next(g_)
                    except StopIteration:
                        gens.remove(g_)

        def bank_x(*excl):
            b = bank()
            while b in excl:
                b = bank()
            return b

        def branch_a():
            AR.reset(offA)
            cqn = AR.alloc([3, S], BF)
            ckvn = AR.alloc([2, S], BF)
            krT = AR.alloc([S], BF)
            base1 = AR.off
            wl = AR.alloc([8, 704], BF)
            dma('pool', wl[:, :, 0:672], w_in_d[:, :, 0:672], w=['wl'])
            dma('pool', wl[:, :, 672:688], w_in_d[:, :, 656:672], w=['wl'])
            dma('pool', wl[:, :, 688:704], w_in_d[:, :, 640:656], w=['wl'])
            NLC = 2 if NT >= 2 else 1
            CH = []
            for i in range(NLC):
                CH.append({'tmp': {'sq': AR.alloc([2, 512], BF), 'rs': AR.alloc([512], F32), 'eps': epsc, 'kp': 'L%d' % i, 'bank': 4 * i + 3},
                           'uT': AR.alloc([8, 512], BF), 't1': AR.alloc([512], F32), 't2': AR.alloc([512], F32)})

            def latent(t, i):
                c_ = CH[i]
                tmp, uT, t1, t2 = c_['tmp'], c_['uT'], c_['t1'], c_['t2']
                uk = ('uL', i)
                bb0 = 4 * i
                tok = slice(t * 512, (t + 1) * 512)
                yield from rmsnorm_gen(lambda c: hT[:, c, tok], 8, D, lambda c: nwcol(l, 8, c),
                                       lambda c: uT[:, c, :], [('h', t)], [uk], tmp)
                for (o0, nch, dim, noff, dst, dk) in ((0, 3, 384, 32, cqn, 'cqn'), (384, 2, 256, 35, ckvn, 'ckvn')):
                    bs = [bb0 + c for c in range(nch)]
                    for c in range(nch):
                        mm(ps[bs[c]][:, :], [(wl[:, k, o0 + c * 128:o0 + (c + 1) * 128], uT[:, k, :]) for k in range(8)],
                           r=['wl', uk], w=[PSK(bs[c])])
                        yield
                    yield from rmsnorm_gen(lambda c: ps[bs[c]][:, :], nch, dim, lambda c: nwcol(l, noff, c),
                                           lambda c: dst[:, c, tok], [PSK(b) for b in bs], [(dk, t)], tmp)
                ba, bb = bb0, bb0 + 1
                mm(ps[ba][0:96, :], [(wl[:, k, 576:672], uT[:, k, :]) for k in range(8)], r=['wl', uk], w=[PSK(ba)])
                mm(ps[bb][0:96, :], [(wl[:, k, 608:704], uT[:, k, :]) for k in range(8)], r=['wl', uk], w=[PSK(bb)])
                yield
                tt(t1[64:96, :], ps[ba][64:96, :], cosT[64:96, tok], ALU.mult, r=[PSK(ba), 'cosT'], w=[('t1', i)])
                tt(t2[64:96, :], ps[bb][64:96, :], sinT[64:96, tok], ALU.mult, r=[PSK(bb), 'sinT'], w=[('t2', i)])
                yield
                tt(krT[64:96, tok], t1[64:96, :], t2[64:96, :], ALU.add, r=[('t1', i), ('t2', i)], w=[('krT', t)])
                yield

            for t0 in range(0, NT, NLC):
                interleave([latent(t0 + i, i) for i in range(min(NLC, NT - t0))])
            P.fence()
            AR.reset(base1)
            wuq = AR.alloc([3, 768], BF)
            wsw = AR.alloc([3, 8, 96], BF)
            wukv = AR.alloc([2, 1024], BF)
            qh = AR.alloc([2, S], BF)
            kh = AR.alloc([2, S], BF)
            vh = AR.alloc([2, NB, 128], BF)
            pT = AR.alloc([4, 512], BF)
            t1 = AR.alloc([512], F32)
            t2 = AR.alloc([512], F32)
            wuq_d = W['mla_w_uq'][l].rearrange('(k p) c -> p k c', p=128)
            wuq_hd = W['mla_w_uq'][l].rearrange('(k p) (h d) -> p k h d', p=128, d=96)
            P.add('pool', lambda e: e.memset(wsw, 0.0), w=['wsw'])
            dma('pool', wuq, wuq_d, w=['wuq'])
            for k in range(3):
                dma('pool', wsw[:, k, :, 64:80], wuq_hd[:, k, :, 80:96], w=['wsw'])
                dma('pool', wsw[:, k, :, 80:96], wuq_hd[:, k, :, 64:80], w=['wsw'])
            dma('pool', wukv, W['mla_w_ukv'][l].rearrange('(k p) c -> p k c', p=128), w=['wukv'])
            P.add('dve', lambda e: e.memset(vh[:, :, :, 64:128], 1.0), w=[('vh', 0), ('vh', 1)])
            rec = AR.alloc([2, 512], F32)
            state = {'pti': 0, 'boi': 0, 'zi': 0}

            def proj(h):
                hs = h % 2
                b1, b2, b3, b4 = 0, 1, 2, 3
                for t in range(NT):
                    tok = slice(t * 512, (t + 1) * 512)
                    mm(ps[b1][0:96, :], [(wuq[:, k, h * 96:(h + 1) * 96], cqn[:, k, tok]) for k in range(3)],
                       r=['wuq', ('cqn', t)], w=[PSK(b1)])
                    mm(ps[b2][0:96, :], [(wsw[:, k, h, :], cqn[:, k, tok]) for k in range(3)],
                       r=['wsw', ('cqn', t)], w=[PSK(b2)])
                    mm(ps[b3][0:64, :], [(wukv[:, k, h * 128:h * 128 + 64], ckvn[:, k, tok]) for k in range(2)],
                       r=['wukv', ('ckvn', t)], w=[PSK(b3)])
                    for j in range(4):
                        blk = slice(t * 512 + j * 128, t * 512 + (j + 1) * 128)
                        mm(ps[b4][:, j * 64:(j + 1) * 64], [(ckvn[:, k, blk], wukv[:, k, h * 128 + 64:(h + 1) * 128]) for k in range(2)],
                           r=['wukv', ('ckvn', t)], w=[PSK(b4)])
                    act(qh[0:64, hs, tok], ps[b1][0:64, :], AF.Copy, r=[PSK(b1)], w=[('qh', hs)], scale=SC_A)
                    stt(t1[64:96, :], ps[b1][64:96, :], SC_A, cosT[64:96, tok], ALU.mult, ALU.mult, r=[PSK(b1), 'cosT'], w=['t1'])
                    stt(t2[64:96, :], ps[b2][64:96, :], SC_A, sinT[64:96, tok], ALU.mult, ALU.mult, r=[PSK(b2), 'sinT'], w=['t2'])
                    tt(qh[64:96, hs, tok], t1[64:96, :], t2[64:96, :], ALU.add, r=['t1', 't2'], w=[('qh', hs)])
                    cp(kh[0:64, hs, tok], ps[b3][0:64, :], r=[PSK(b3)], w=[('kh', hs)], eng='act')
                    cp(kh[64:96, hs, tok], krT[64:96, tok], r=[('krT', t)], w=[('kh', hs)], eng='pool')
                    cp(vh[:, hs, t * 4:(t + 1) * 4, 0:64], ps[b4][:, 0:256].rearrange('p (a b) -> p a b', a=4),
                       r=[PSK(b4)], w=[('vh', hs)])
                    yield

            def attn(h):
                hs = h % 2
                for tq in range(NT):
                    qtok = slice(tq * 512, (tq + 1) * 512)
                    bo = 4 + state['boi'] % 2
                    rsl = state['boi'] % 2
                    state['boi'] += 1
                    nk = 4 * (tq + 1)
                    pq = []
                    for kc in range(nk):
                        j = kc - 4 * tq
                        c0 = 128 * j if j > 0 else 0
                        bz = 6 + state['zi'] % 2
                        state['zi'] += 1
                        mm1(ps[bz][:, c0:512], kh[0:96, hs, kc * 128:(kc + 1) * 128], qh[0:96, hs, tq * 512 + c0:(tq + 1) * 512],
                            True, True, r=[('kh', hs), ('qh', hs)], w=[PSK(bz)])
                        q = state['pti'] % 4
                        state['pti'] += 1
                        act(pT[:, q, c0:512], ps[bz][:, c0:512], AF.Exp, r=[PSK(bz)], w=[('pT', q)])
                        if j >= 0:
                            tt(pT[:, q, c0:c0 + 128], pT[:, q, c0:c0 + 128], mmla_b, ALU.mult, r=[('pT', q), 'cstb'], w=[('pT', q)])
                        if len(pq) >= 2:
                            pq.pop(0)()
                        pq.append(lambda kc=kc, q=q, c0=c0: mm1(ps[bo][:, c0:512], vh[:, hs, kc, :], pT[:, q, c0:512], kc == 0, kc == nk - 1,
                                                                 r=[('vh', hs), ('pT', q)], w=[PSK(bo)]))
                        yield
                    while pq:
                        pq.pop(0)()
                    P.add('dve', lambda e, bo=bo, rsl=rsl: e.reciprocal(rec[0:64, rsl, :], ps[bo][64:128, :]), r=[PSK(bo)], w=[('rec', rsl)])
                    tt(yA[hs * 64:(hs + 1) * 64, h // 2, qtok], ps[bo][0:64, :], rec[0:64, rsl, :], ALU.mult, r=[PSK(bo), ('rec', rsl)], w=[('yT', 0)])
                    yield

            interleave([proj(0)])
            for h in range(8):
                gens = [attn(h)]
                if h < 7:
                    gens.append(proj(h + 1))
                interleave(gens)
            P.fence()

        def branch_c():
            AR.reset(offC)
            LB0 = DEPTH * NW + 8
            if l == 0:
                P.add('dve', lambda e: e.memset(lbt[:, 0:4], 0.0), w=['lbt'])
                P.add('dve', lambda e: e.memset(lbt[:, 4:8], 1.0), w=['lbt'])
                P.add('dve', lambda e: e.memset(lbt[:, 8:12], -1.0), w=['lbt'])
            else:
                tt(lbt[:, 12:16], nw[:, LB0:LB0 + 4], nw[:, LB0 + 4:LB0 + 8], ALU.subtract, r=[nwk], w=['lbt'])
                act(lbt[:, 12:16], lbt[:, 12:16], AF.Exp, r=['lbt'], w=['lbt'])
                ts(lbt[:, 12:16], lbt[:, 12:16], 1.0, None, ALU.add, r=['lbt'], w=['lbt'])
                P.add('dve', lambda e: e.reciprocal(lbt[:, 0:4], lbt[:, 12:16]), r=['lbt'], w=['lbt'])
                ts(lbt[:, 0:4], lbt[:, 0:4], 1.0 - 1e-6, 0.0, ALU.min, ALU.max, r=['lbt'], w=['lbt'])
                ts(lbt[:, 4:8], lbt[:, 0:4], -1.0, 1.0, ALU.mult, ALU.add, r=['lbt'], w=['lbt'])
                ts(lbt[:, 8:12], lbt[:, 4:8], -1.0, None, ALU.mult, r=['lbt'], w=['lbt'])
            NCH = S // 64
            uT = AR.alloc([8, S], BF)
            QF = AR.alloc([S], BF)
            KF = AR.alloc([S], BF)
            KFt = AR.alloc([NB, 128], BF)
            Vt = AR.alloc([NB, 128], BF)
            GS = AR.alloc([S], BF)
            dec = AR.alloc([NCH], F32)
            e1 = AR.alloc([NCH + 1], F32)
            e2 = AR.alloc([NCH], F32)
            St = AR.alloc([128], F32)
            Sbf = AR.alloc([2, 128], BF)
            Am = AR.alloc([2, 128], BF)
            tmp = {'sq': AR.alloc([2, 512], BF), 'rs': AR.alloc([512], F32), 'eps': epsc}
            whg = AR.alloc([8, 512], BF)
            NCHAIN = max(1, min(2, NT // 2)) if NT > 1 else 1
            osq = AR.alloc([512], BF)
            o1 = AR.alloc([512], F32)
            o2 = AR.alloc([512], F32)
            cb = [0]

            def cbank():
                b = cb[0] % 6
                cb[0] += 1
                return b
            pending_rec = None
            X = [[AR.alloc([512], F32) for _ in range(3)] for _ in range(NCHAIN)]
            B32 = [AR.alloc([512], F32) for _ in range(NCHAIN)]
            T8 = [AR.alloc([8], F32) for _ in range(NCHAIN)]
            for t in range(NT):
                tok = slice(t * 512, (t + 1) * 512)
                rmsnorm_tile(lambda c: hT[:, c, tok], 8, D, lambda c: nwcol(l, 8, c),
                             lambda c: uT[:, c, tok], [('h', t)], [('u', t)], tmp)
            for h in range(4):
                for i, o in enumerate((O_HQ, O_HF, O_HI, O_HG)):
                    dma('pool', whg[:, :, i * 128:(i + 1) * 128], w_in_d[:, :, o + h * 128:o + (h + 1) * 128], w=['whg'])
                lbc, omlc, nomlc = lbt[:, h:h + 1], lbt[:, 4 + h:5 + h], lbt[:, 8 + h:9 + h]

                def prep(t, sl):
                    x1, x2, x3 = X[sl]
                    b32 = B32[sl]
                    t8 = T8[sl]
                    xk = lambda i: ('x', sl, i)
                    tok = slice(t * 512, (t + 1) * 512)
                    ch = slice(t * 8, (t + 1) * 8)
                    b = cbank()
                    mm(ps[b][:, :], [(whg[:, k, 128:256], uT[:, k, tok]) for k in range(8)], r=['whg', ('u', t)], w=[PSK(b)])
                    act(x1, ps[b][:, :], AF.Exp, r=[PSK(b)], w=[xk(1)], scale=-1.0)
                    yield
                    b = cbank()
                    mm(ps[b][:, :], [(whg[:, k, 0:128], uT[:, k, tok]) for k in range(8)], r=['whg', ('u', t)], w=[PSK(b)])
                    act(QF[:, tok], ps[b][:, :], AF.Silu, r=[PSK(b)], w=[('QF', t)])
                    ts(x1, x1, 1.0, None, ALU.add, r=[xk(1)], w=[xk(1)])
                    yield
                    b = cbank()
                    mm(ps[b][:, :], [(whg[:, k, 384:512], uT[:, k, tok]) for k in range(8)], r=['whg', ('u', t)], w=[PSK(b)])
                    act(GS[:, tok], ps[b][:, :], AF.Silu, r=[PSK(b)], w=[('GS', t)])
                    P.add('dve', lambda e: e.reciprocal(x2, x1), r=[xk(1)], w=[xk(2)])
                    yield
                    b = cbank()
                    for j in range(4):
                        mm(ps[b][:, j * 128:(j + 1) * 128], [(uT[:, k, t * 512 + j * 128:t * 512 + (j + 1) * 128], whg[:, k, 256:384]) for k in range(8)],
                           r=['whg', ('u', t)], w=[PSK(b)])
                    cp(Vt[:, t * 4:(t + 1) * 4, :], ps[b][:, :].rearrange('p (a b) -> p a b', a=4), r=[PSK(b)], w=[('Vt', t)], eng='act')
                    ts(x1, x2, omlc, lbc, ALU.mult, ALU.add, r=[xk(2), 'lbt'], w=[xk(1)])
                    yield
                    act(x3, x1, AF.Ln, r=[xk(1)], w=[xk(3)])
                    ts(KF[:, tok], x2, nomlc, omlc, ALU.mult, ALU.add, r=[xk(2), 'lbt'], w=[('KF', t)])
                    yield
                    P.add('dve', lambda e: e.tensor_tensor_scan(b32, rst_f, x3, 0.0, ALU.mult, ALU.add),
                          r=[xk(3), 'cstf'], w=[('b32', sl)])
                    yield
                    b3 = b32.rearrange('p (c s) -> p c s', s=64)
                    tt(x1.rearrange('p (c s) -> p c s', s=64), b3, b3[:, :, 31:32].broadcast_to([128, 8, 64]), ALU.subtract,
                       r=[('b32', sl)], w=[xk(1)])
                    tt(t8, b3[:, :, 63], b3[:, :, 31], ALU.subtract, r=[('b32', sl)], w=[('t8', sl)])
                    yield
                    act(x3, x1, AF.Exp, r=[xk(1)], w=[xk(3)])
                    act(x1, x1, AF.Exp, r=[xk(1)], w=[xk(1)], scale=-1.0)
                    yield
                    act(e2[:, ch], t8, AF.Exp, r=[('t8', sl)], w=[('e2', t)])
                    tt(QF[:, tok], QF[:, tok], x3, ALU.mult, r=[('QF', t), xk(3)], w=[('QF', t)])
                    yield
                    act(dec[:, ch], b3[:, :, 63], AF.Exp, r=[('b32', sl)], w=[('dec', t)])
                    tt(KF[:, tok], KF[:, tok], x1, ALU.mult, r=[('KF', t), xk(1)], w=[('KF', t)])
                    yield
                    act(e1[:, ch], b3[:, :, 31], AF.Exp, r=[('b32', sl)], w=[('e1', t)])
                    kd = x2.bitcast(BF)
                    tt(kd[:, 0:512].rearrange('p (c s) -> p c s', s=64), KF[:, tok].rearrange('p (c s) -> p c s', s=64),
                       e2[:, ch].unsqueeze(2).broadcast_to([128, 8, 64]), ALU.mult, r=[('KF', t), ('e2', t), xk(2)], w=[xk(2)])
                    yield
                    b = cbank()
                    psb = ps[b][:, :].bitcast(BF)

                    def fn(e):
                        ins = None
                        for j in range(4):
                            ins = e.transpose(psb[:, j * 128:(j + 1) * 128], kd[:, j * 128:(j + 1) * 128], ident_b)
                        return ins
                    P.add('pe', fn, r=[xk(2), 'cstb'], w=[PSK(b)])
                    cp(KFt[:, t * 4:(t + 1) * 4, :], psb[:, 0:512].rearrange('p (a b) -> p a b', a=4), r=[PSK(b)], w=[('KFt', t)], eng='act')
                    yield

                def rec(tiles, h=h):
                    for t in tiles:
                        tok = slice(t * 512, (t + 1) * 512)
                        bo = 6 + t % 2
                        if t == 0:
                            P.add('dve', lambda e: e.memset(St, 0.0), w=['St'])
                            P.add('dve', lambda e: e.memset(Sbf[:, 0, :], 0.0), w=[('Sbf', 0)])
                        elif t == tiles[0]:
                            c0_ = 8 * t
                            ts(Sbf[:, c0_ % 2, :], St, e1[:, c0_:c0_ + 1], None, ALU.mult, r=['St', ('e1', t)], w=[('Sbf', c0_ % 2)])
                        for j in range(4):
                            m = t * 4 + j
                            blk = slice(m * 128, (m + 1) * 128)
                            cols = slice(j * 128, (j + 1) * 128)
                            ba = cbank()
                            mm1(ps[ba][:, 0:128], KF[:, blk], QF[:, blk], True, True, r=[('KF', t), ('QF', t)], w=[PSK(ba)])
                            a_ = m % 2
                            tt(Am[:, a_, :], ps[ba][:, 0:128], mhg_f, ALU.mult, r=[PSK(ba), 'cstf'], w=[('Am', a_)])
                            mm1(ps[bo][:, cols], Vt[:, m, :], Am[:, a_, :], True, False, r=[('Vt', t), ('Am', a_)], w=[PSK(bo)])
                            for cch in range(2):
                                c = 2 * m + cch
                                sl = c % 2
                                mm1(ps[bo][:, j * 128 + cch * 64:j * 128 + (cch + 1) * 64], Sbf[:, sl, :], QF[:, c * 64:(c + 1) * 64],
                                    False, cch == 1, r=[('Sbf', sl), ('QF', t)], w=[PSK(bo)])
                                if c == NCH - 1:
                                    continue
                                bs_ = cbank()
                                pr = slice(cch * 64, (cch + 1) * 64)
                                mm1(ps[bs_][:, 0:128], KFt[pr, m, :], Vt[pr, m, :], True, True, r=[('KFt', t), ('Vt', t)], w=[PSK(bs_)])
                                stt(St, St, dec[:, c:c + 1], ps[bs_][:, 0:128], ALU.mult, ALU.add, r=['St', PSK(bs_), ('dec', c // 8)], w=['St'])
                                if c != 8 * (tiles[-1] + 1) - 1:
                                    ts(Sbf[:, 1 - sl, :], St, e1[:, c + 1:c + 2], None, ALU.mult, r=['St', ('e1', (c + 1) // 8)], w=[('Sbf', 1 - sl)])
                                yield
                        act(osq[:, :], ps[bo][:, :], AF.Square, r=[PSK(bo)], w=['osq'])
                        bss = cbank()
                        mm1(ps[bss][:, :], ones_b, osq[:, :], True, True, r=['osq', 'cstb'], w=[PSK(bss)])
                        act(o1, ps[bss][:, :], AF.Ln, r=[PSK(bss)], w=['o1'], scale=1.0 / 128, bias=epsc)
                        yield
                        act(o1, o1, AF.Exp, r=['o1'], w=['o1'], scale=-0.5)
                        tt(o2, ps[bo][:, :], o1, ALU.mult, r=[PSK(bo), 'o1'], w=['o2'])
                        yield
                        stt(yC[:, h, tok], o2, nwcol(l, 37, h), GS[:, tok], ALU.mult, ALU.mult, r=['o2', ('GS', t), nwk], w=[('yT', 2)])
                        yield

                HALF = max(1, NT // 2)
                first = list(range(0, HALF))
                second = list(range(HALF, NT))
                g1 = [prep(t, i) for i, t in enumerate(first)]
                if pending_rec is not None:
                    g1.append(pending_rec)
                interleave(g1)
                g2 = [prep(t, i) for i, t in enumerate(second)]
                g2.append(rec(first))
                interleave(g2)
                pending_rec = rec(second) if second else None
            if pending_rec is not None:
                interleave([pending_rec])
            P.fence()

        if 'C' in branches:
            P.mark('mixC s%d l%d' % (sq_i, l))
            branch_c()

        if 'A' in branches:
            P.mark('mixA s%d l%d' % (sq_i, l))
            branch_a()
        def branch_b():
            for g in range(2):
                AR.reset(base0)
                qz = AR.alloc([4, S], BF)
                kT = AR.alloc([2, S], BF)
                vS = AR.alloc([NB, 256], BF)
                base1 = AR.off
                tmp = {'sq': AR.alloc([2, 512], BF), 'rs': AR.alloc([512], F32), 'eps': epsc}
                norm_tile = mknorm(tmp)
                uT = AR.alloc([8, 512], BF)
                wsb = AR.alloc([8, 768], BF)
                for i, o in enumerate((O_SBQ, O_SBK, O_SBV)):
                    dma('pool', wsb[:, :, i * 256:(i + 1) * 256], w_in_d[:, :, o + g * 256:o + (g + 1) * 256], w=['wsb'])
                P.add('pool', lambda e: e.memset(qz, 0.0), w=[('qT', x) for x in range(NT)])
                for t in range(NT):
                    tok = slice(t * 512, (t + 1) * 512)
                    norm_tile(t, uT)
                    for cc in range(2):
                        bq, bk = bank(), bank()
                        mm(ps[bq][:, :], [(wsb[:, k, cc * 128:(cc + 1) * 128], uT[:, k, :]) for k in range(8)], r=['wsb', 'u'], w=[PSK(bq)])
                        mm(ps[bk][:, :], [(wsb[:, k, 256 + cc * 128:256 + (cc + 1) * 128], uT[:, k, :]) for k in range(8)], r=['wsb', 'u'], w=[PSK(bk)])
                        act(qz[0:64, 2 * cc, tok], ps[bq][0:64, :], AF.Copy, r=[PSK(bq)], w=[('qT', t)], scale=0.125)
                        act(qz[64:128, 2 * cc + 1, tok], ps[bq][64:128, :], AF.Copy, r=[PSK(bq)], w=[('qT', t)], scale=0.125)
                        cp(kT[:, cc, tok], ps[bk][:, :], r=[PSK(bk)], w=[('kT', t)])
                    for jj in range(2):
                        bv = bank()
                        for i in range(2):
                            j = jj * 2 + i
                            mm(ps[bv][:, i * 256:(i + 1) * 256], [(uT[:, k, j * 128:(j + 1) * 128], wsb[:, k, 512:768]) for k in range(8)],
                               r=['wsb', 'u'], w=[PSK(bv)])
                        cp(vS[:, t * 4 + jj * 2:t * 4 + jj * 2 + 2, :], ps[bv][:, :].rearrange('p (a b) -> p a b', a=2),
                           r=[PSK(bv)], w=[('vS', t)], eng='act')
                P.fence()
                AR.reset(base1)
                e32 = AR.alloc([2, 512], F32)
                spb = AR.alloc([3, 512], BF)
                ab = AR.alloc([3, 512], BF)
                R = AR.alloc([512], F32)
                Rb = AR.alloc([4, 512], BF)
                jobs = []
                for hl in range(4):
                    for tq in range(NT):
                        nk = 4 * (tq + 1)
                        for idx, kc in enumerate(range(nk - 1, -1, -1)):
                            jobs.append((hl, tq, nk, idx, kc))
                zb = [0]

                def zbank():
                    b = 2 + zb[0] % 6
                    zb[0] += 1
                    return b

                def mk(n, job):
                    hl, tq, nk, idx, kc = job
                    cc = hl // 2
                    pb = (hl % 2) * 64
                    tile_i = hl * NT + tq
                    bo = tile_i % 2
                    qtok = slice(tq * 512, (tq + 1) * 512)
                    j = kc - 4 * tq
                    c0 = 128 * j if j > 0 else 0
                    cs = slice(c0, 512)
                    q = n % 2
                    q3 = n % 3
                    kap = kT[:, cc, kc * 128:(kc + 1) * 128]
                    qap = qz[:, hl, tq * 512 + c0:(tq + 1) * 512]
                    rs_ = n % 4
                    rsl = (n - 1) % 4

                    def stage1():
                        if idx == 0:
                            P.add('pool', lambda e: e.memset(R[:, 256:512], 0.0), w=[('R', 1)])
                            P.add('dve', lambda e: e.memset(R[:, 0:256], 0.0), w=[('R', 0)])
                        bz = zbank()
                        mm1(ps[bz][:, cs], kap, qap, True, True, r=[('kT', kc // 4), ('qT', tq)], w=[PSK(bz)])
                        act(e32[:, q, cs], ps[bz][:, cs], AF.Exp, r=[PSK(bz)], w=[('e32', q)])
                        act(spb[:, q3, cs], e32[:, q, cs], AF.Ln, r=[('e32', q)], w=[('spb', q3)], bias=onec)
                        if j >= 0:
                            tt(spb[:, q3, c0:c0 + 128], spb[:, q3, c0:c0 + 128], msb_b, ALU.mult, r=[('spb', q3), 'cstb'], w=[('spb', q3)])
                        if idx < nk - 1:
                            jn = j - 1
                            c0n = 128 * jn if jn > 0 else 0
                            hi0 = max(c0, 256)
                            tt(R[:, hi0:512], R[:, hi0:512], spb[:, q3, hi0:512], ALU.add, r=[('R', 1), ('spb', q3)], w=[('R', 1)], eng='pool')
                            if c0 < 256:
                                tt(R[:, c0:256], R[:, c0:256], spb[:, q3, c0:256], ALU.add, r=[('R', 0), ('spb', q3)], w=[('R', 0)])
                            cp(Rb[:, rs_, c0n:512], R[:, c0n:512], r=[('R', 0), ('R', 1)], w=[('Rb', rs_)])

                    def stage2():
                        bc = zbank()

                        def fn(e):
                            e.matmul(ps[bc][:, cs], kap, qap, start=True, stop=False)
                            ins = e.matmul(ps[bc][:, cs], nuinc_b, spb[:, q3, cs], start=False, stop=(idx == 0))
                            if idx > 0:
                                ins = e.matmul(ps[bc][:, cs], nones_b, Rb[:, rsl, cs], start=False, stop=True)
                            return ins
                        P.add('pe', fn, r=[('kT', kc // 4), ('qT', tq), ('spb', q3), 'cstb'] + ([('Rb', rsl)] if idx > 0 else []), w=[PSK(bc)])
                        act(ab[:, q3, cs], ps[bc][:, cs], AF.Exp, r=[PSK(bc)], w=[('ab', q3)])
                        if j >= 0:
                            tt(ab[:, q3, c0:c0 + 128], ab[:, q3, c0:c0 + 128], msb_b, ALU.mult, r=[('ab', q3), 'cstb'], w=[('ab', q3)])

                    def stage3():
                        mm1(ps[bo][:, cs], vS[:, kc, cc * 128:(cc + 1) * 128], ab[:, q3, cs], idx == 0, idx == nk - 1,
                            r=[('vS', kc // 4), ('ab', q3)], w=[PSK(bo)], skip=True)
                        if idx == nk - 1:
                            cp(yB[pb:pb + 64, g * 2 + cc, qtok], ps[bo][pb:pb + 64, :], r=[PSK(bo)], w=[('yT', 1)], eng='act')
                    return stage1, stage2, stage3

                st = [mk(n, job) for n, job in enumerate(jobs)]
                NJB = len(st)
                for n in range(NJB + 2):
                    if n < NJB:
                        st[n][0]()
                    if 0 <= n - 1 < NJB:
                        st[n - 1][1]()
                    if 0 <= n - 2 < NJB:
                        st[n - 2][2]()
                P.fence()

        if 'B' in branches:
            P.mark('mixB s%d l%d' % (sq_i, l))
            branch_b()

        def merge():
            AR.reset(base0)
            tmp = {'sq': AR.alloc([2, 512], BF), 'rs': AR.alloc([512], F32), 'eps': epsc}
            TP = min(S, 1024)
            NTP = TP // 512
            uT = AR.alloc([8, TP], BF)
            mT = AR.alloc([8, TP], BF)
            wgt = AR.alloc([2, 8, 384], BF)
            wbr = AR.alloc([2, 3, 4, 128], BF)
            wo2 = AR.alloc([2, 8, 128], BF)
            sg = AR.alloc([2, 512], F32)
            acc = AR.alloc([2, 512], F32)
            wout_d = W['w_out'][l].rearrange('(k p) c -> p k c', p=128)
            brw = [W[n][l].rearrange('(k p) c -> p k c', p=128) for n in ('w_br_mla', 'w_br_sb', 'w_br_hgrn')]
            brs = [i for i, b in enumerate('ABC') if b in branches]
            wi_ = 0
            sgi = 0
            aci = 0
            for tp in range(S // TP):
                for tt_ in range(NTP):
                    t = tp * NTP + tt_
                    tok = slice(t * 512, (t + 1) * 512)
                    utok = slice(tt_ * 512, (tt_ + 1) * 512)
                    rmsnorm_tile(lambda c: hT[:, c, tok], 8, D, lambda c: nwcol(l, 8, c),
                                 lambda c: uT[:, c, utok], [('h', t)], [('u', tt_)], tmp)
                for c in range(8):
                    s = wi_ % 2
                    wi_ += 1
                    for br in brs:
                        dma('pool', wgt[:, s, :, br * 128:(br + 1) * 128],
                            w_in_d[:, :, O_GATE + br * 1024 + c * 128:O_GATE + br * 1024 + (c + 1) * 128], w=[('wgt', s)])
                        dma('pool', wbr[:, s, br, :, :], brw[br][:, :, c * 128:(c + 1) * 128], w=[('wbr', s)])
                    for tt_ in range(NTP):
                        t = tp * NTP + tt_
                        tok = slice(t * 512, (t + 1) * 512)
                        utok = slice(tt_ * 512, (tt_ + 1) * 512)
                        a_ = aci % 2
                        aci += 1
                        for bi, br in enumerate(brs):
                            bg, bp = bank(), bank()
                            mm(ps[bg][:, :], [(wgt[:, s, k, br * 128:(br + 1) * 128], uT[:, k, utok]) for k in range(8)],
                               r=[('wgt', s), ('u', tt_)], w=[PSK(bg)])
                            mm(ps[bp][:, :], [(wbr[:, s, br, k, :], yv[br][:, k, tok]) for k in range(4)],
                               r=[('wbr', s), ('yT', br)], w=[PSK(bp)])
                            q = sgi % 2
                            sgi += 1
                            act(sg[:, q, :], ps[bg][:, :], AF.Sigmoid, r=[PSK(bg)], w=[('sg', q)])
                            last = (bi == len(brs) - 1)
                            dst = mT[:, c, utok] if last else acc[:, a_, :]
                            dk = [('mT', c, tt_)] if last else [('acc', a_)]
                            if bi == 0:
                                tt(dst, sg[:, q, :], ps[bp][:, :], ALU.mult, r=[('sg', q), PSK(bp)], w=dk)
                            else:
                                tt(sg[:, q, :], sg[:, q, :], ps[bp][:, :], ALU.mult, r=[('sg', q), PSK(bp)], w=[('sg', q)])
                                tt(dst, acc[:, a_, :], sg[:, q, :], ALU.add, r=[('sg', q), ('acc', a_)], w=dk)
                for c2 in range(8):
                    s = c2 % 2
                    dma('pool', wo2[:, s, :, :], wout_d[:, :, c2 * 128:(c2 + 1) * 128], w=[('wo2', s)])
                    for tt_ in range(NTP):
                        t = tp * NTP + tt_
                        tok = slice(t * 512, (t + 1) * 512)
                        utok = slice(tt_ * 512, (tt_ + 1) * 512)
                        b = bank()
                        mm(ps[b][:, :], [(wo2[:, s, k, :], mT[:, k, utok]) for k in range(8)],
                           r=[('wo2', s)] + [('mT', k, tt_) for k in range(8)], w=[PSK(b)])
                        tt(hT[:, c2, tok], hT[:, c2, tok], ps[b][:, :], ALU.add, r=[PSK(b), ('h', t)], w=[('h', t)])
            P.fence()

        P.mark('merge s%d l%d' % (sq_i, l))
        merge()

    epst = nc.alloc_sbuf_tensor('epst', [128, 1], F32)
    epsc = epst[:, 0:1]
    P.add('dve', lambda e: e.memset(epst[:, :], EPS), w=['eps'])
    onet = nc.alloc_sbuf_tensor('onet', [128, 1], F32)
    onec = onet[:, 0:1]
    P.add('dve', lambda e: e.memset(onet[:, :], 1.0), w=['one'])
    P.fence()

    for sq_i in range(NSEQ):
        if 'mix' in phases:
            P.mark('rope s%d' % sq_i)
            rope_tables(sq_i)
        P.mark('load s%d' % sq_i)
        load_seq(sq_i)
        for l in layers:
            if 'ffa' in phases:
                P.mark('ffa s%d l%d' % (sq_i, l))
                ffn(l, 'a')
            if 'mix' in phases:
                brs = ''.join(b for b in 'ABC' if ('mix' + b) in phases) or 'ABC'
                mixer(l, sq_i, brs)
            if 'ffb' in phases:
                P.mark('ffb s%d l%d' % (sq_i, l))
                ffn(l, 'b')
            if 'ple' in phases:
                P.mark('ple s%d l%d' % (sq_i, l))
                ple(l, sq_i)
        P.mark('store s%d' % sq_i)
        store_seq(sq_i)

    P.fence()
    P.emit()
    nc._prog = P
    return nc


C_ID, C_ONES, C_UINC, C_NUINC, C_NONES, C_MMLA, C_MSB, C_MHG, C_RST, C_INVF, C_SGN = 0, 128, 256, 384, 512, 640, 768, 896, 1024, 1536, 1537
NCST = 1540


def make_consts():
    c = np.zeros((128, NCST), np.float32)
    c[:, C_ID:C_ID + 128] = np.eye(128, dtype=np.float32)
    c[:, C_ONES:C_ONES + 128] = 1.0
    j = np.arange(128)[:, None]
    k = np.arange(128)[None, :]
    c[:, C_UINC:C_UINC + 128] = (j >= k).astype(np.float32)
    c[:, C_NUINC:C_NUINC + 128] = -(j >= k).astype(np.float32)
    c[:, C_NONES:C_NONES + 128] = -1.0
    c[:, C_MMLA:C_MMLA + 128] = 1.0 - ((j >= 64) & (k < 64)).astype(np.float32)
    c[:, C_MSB:C_MSB + 128] = (j < k).astype(np.float32)
    c[:, C_MHG:C_MHG + 128] = ((j // 64 == k // 64) & (j <= k)).astype(np.float32)
    c[:, C_RST:C_RST + 512] = (np.arange(512) % 64 != 0).astype(np.float32)[None, :]
    half = 16
    inv = (10000.0 ** (-np.arange(half, dtype=np.float32) / half)).astype(np.float32)
    pp = np.arange(128)
    c[:, C_INVF] = inv[pp % 16]
    c[:, C_SGN] = np.where((pp % 32) < 16, -1.0, 1.0)
    return c


WNAMES = ['ffn_a_norm', 'ffn_a_w_in', 'ffn_a_w_out', 'mix_norm', 'w_in', 'mla_q_norm', 'mla_w_uq', 'mla_kv_norm',
          'mla_w_ukv', 'hgrn_lower_bounds', 'hgrn_out_norm', 'w_br_mla', 'w_br_sb', 'w_br_hgrn', 'w_out',
          'ffn_b_norm', 'ffn_b_w_in', 'ffn_b_w_out', 'ple_norm', 'w_ple_gate', 'w_ple_proj', 'final_norm']


def run(inputs, n_cores=8, **bkw):
    x = np.ascontiguousarray(np.asarray(inputs['x'], dtype=np.float32))
    p = np.ascontiguousarray(np.asarray(inputs['p'], dtype=np.float32))
    pos = np.ascontiguousarray(np.asarray(inputs['positions'], dtype=np.int32))
    B, S, _ = x.shape
    nseq = B // n_cores
    nc = bass.Bass("TRN2", target_bir_lowering=False)
    build(nc, S, nseq, **bkw)
    cst = make_consts()
    wts = {k: np.ascontiguousarray(np.asarray(inputs[k], dtype=np.float32)) for k in WNAMES}
    in_maps = []
    for c in range(n_cores):
        m = {'x': x[c * nseq:(c + 1) * nseq], 'p': np.ascontiguousarray(p[:, c * nseq:(c + 1) * nseq]),
             'positions': pos[c * nseq:(c + 1) * nseq], 'cst': cst}
        m.update(wts)
        in_maps.append(m)
    res = run_bass_kernel_spmd(nc, in_maps, core_ids=list(range(n_cores)))
    return np.concatenate([np.asarray(r['out']) for r in res.results], axis=0).astype(np.float32)


def kernel(**inputs):
    return run(inputs, n_cores=8)
```

```python
import numpy as np
import concourse.bass as bass
import concourse.mybir as mybir
from concourse.bass_utils import run_bass_kernel_spmd

F32 = mybir.dt.float32
BF = mybir.dt.bfloat16
I32 = mybir.dt.int32
AF = mybir.ActivationFunctionType
ALU = mybir.AluOpType

D = 1024
DFF = 2816
NJ = DFF // 128
DEPTH = 2
PLE = 256
EPS = 1e-6
IN_W = 7328
O_CQ, O_CKV, O_KR = 0, 384, 640
O_SBQ, O_SBK, O_SBV = 672, 1184, 1696
O_HQ, O_HF, O_HI, O_HG = 2208, 2720, 3232, 3744
O_GATE = 4256

ENGS = ('pe', 'act', 'dve', 'pool', 'sp')
NDS = 8


class Op:
    __slots__ = ('eng', 'fn', 'deps', 'tok', 'dma', 'prev_dma')


class Prog:
    def __init__(self, nc):
        self.nc = nc
        self.ops = []
        self.last_w = {}
        self.readers = {}
        self.cnt = {e: 0 for e in ENGS}
        self.dcnt = {'sp': 0, 'pool': 0}
        self.dma_hist = {'sp': [], 'pool': []}
        self.last_real = {e: None for e in ENGS}
        self.pending_dma = []
        self.marks = []
        self.pe_count_at_op = {}

    def mark(self, name):
        self.marks.append((name, len(self.ops)))

    def add(self, eng, fn, r=(), w=(), dma=False):
        i = len(self.ops)
        deps = {}
        for k in r:
            lw = self.last_w.get(k)
            if lw is not None:
                deps[lw] = True
        for k in w:
            lw = self.last_w.get(k)
            if lw is not None:
                deps[lw] = True
            for rd in self.readers.get(k, ()):
                if rd not in deps:
                    deps[rd] = False
        for k in r:
            self.readers.setdefault(k, []).append(i)
        for k in w:
            self.last_w[k] = i
            self.readers[k] = []
        deps.pop(i, None)
        op = Op()
        op.eng, op.fn, op.deps, op.dma, op.prev_dma = eng, fn, deps, dma, None
        if fn is None:
            op.tok = None
        elif dma:
            n = self.dcnt[eng]
            self.dcnt[eng] += 1
            op.tok = ('d', eng, n % NDS, 16 * (n // NDS + 1))
            hist = self.dma_hist[eng]
            if n >= NDS:
                op.prev_dma = hist[n - NDS]
            hist.append(i)
            self.pending_dma.append(i)
        else:
            self.cnt[eng] += 1
            op.tok = ('c', eng, self.cnt[eng])
        if fn is not None:
            self.last_real[eng] = i
        self.ops.append(op)
        return i

    def fence(self):
        targets = [v for v in self.last_real.values() if v is not None] + list(self.pending_dma)
        self.pending_dma = []
        for e in ENGS:
            i = self.add(e, None)
            self.ops[i].deps = {t: True for t in targets}

    def emit(self):
        nc = self.nc
        sems = {e: nc.alloc_semaphore('s_' + e) for e in ENGS}
        dsems = {q: [nc.alloc_semaphore('d_%s%d' % (q, i)) for i in range(NDS)] for q in ('sp', 'pool')}
        per_eng = {e: [] for e in ENGS}
        for i, op in enumerate(self.ops):
            per_eng[op.eng].append(i)
        ops = self.ops

        def tok_sem(tok):
            if tok[0] == 'c':
                return sems[tok[1]], tok[2]
            return dsems[tok[1]][tok[2]], tok[3]

        pe_n = [0]

        class _Cnt:
            def matmul(self_, *a, **k):
                pe_n[0] += 1
                return nc.tensor.matmul(*a, **k)

            def transpose(self_, *a, **k):
                pe_n[0] += 1
                return nc.tensor.transpose(*a, **k)
        cnt_proxy = _Cnt()

        def run(ename, e):
            waited = {}
            for i in per_eng[ename]:
                op = ops[i]
                dl = sorted(op.deps.items())
                if op.prev_dma is not None:
                    dl.append((op.prev_dma, True))
                for j, true_dep in dl:
                    dj = ops[j]
                    if dj.tok is None:
                        continue
                    if (not dj.dma) and dj.eng == ename:
                        if ename in ('pe', 'sp'):
                            continue
                    s, v = tok_sem(dj.tok)
                    if waited.get(s.num, 0) >= v:
                        continue
                    e.wait_ge(s, v)
                    waited[s.num] = v
                if op.fn is None:
                    continue
                if ename == 'pe':
                    self.pe_count_at_op[i] = pe_n[0]
                    ins = op.fn(cnt_proxy)
                else:
                    ins = op.fn(e)
                s, v = tok_sem(op.tok)
                ins.then_inc(s, 16 if op.dma else 1)

        with nc.Block() as block:
            block.tensor(lambda e: run('pe', e))
            block.scalar(lambda e: run('act', e))
            block.vector(lambda e: run('dve', e))
            block.gpsimd(lambda e: run('pool', e))
            block.sync(lambda e: run('sp', e))
        self.pe_total = pe_n[0]


class Arena:
    def __init__(self, nc, nbytes):
        self.t = nc.alloc_sbuf_tensor('arena', [128, nbytes // 4], F32)
        self.size = nbytes
        self.off = 0

    def reset(self, off=0):
        self.off = off

    def alloc(self, free_shape, dtype):
        esz = 2 if dtype == BF else 4
        n = 1
        for s in free_shape:
            n *= s
        nb = (n * esz + 63) // 64 * 64
        assert self.off + nb <= self.size, ('arena overflow', self.off, nb, self.size)
        ap = self.t[:, self.off // 4:(self.off + nb) // 4]
        self.off += nb
        if dtype != F32:
            ap = ap.bitcast(dtype)
        ap = ap[:, 0:n]
        if len(free_shape) == 2:
            ap = ap.rearrange('p (a b) -> p a b', a=free_shape[0])
        elif len(free_shape) == 3:
            ap = ap.rearrange('p (a b c) -> p a b c', a=free_shape[0], b=free_shape[1])
        elif len(free_shape) == 4:
            ap = ap.rearrange('p (a b c d) -> p a b c d', a=free_shape[0], b=free_shape[1], c=free_shape[2])
        return ap


def build(nc, S, NSEQ, layers=(0, 1), phases=('ffa', 'mix', 'ffb', 'ple'), final_norm=True, dbg=None):
    NT = S // 512
    NB = S // 128
    P = Prog(nc)

    def din(name, shape, dt=F32):
        return nc.dram_tensor(name, list(shape), dt, kind='ExternalInput').ap()

    x_d = din('x', [NSEQ, S, D])
    p_d = din('p', [DEPTH, NSEQ, S, PLE])
    pos_d = din('positions', [NSEQ, S], I32)
    W = {}
    for nm, shp in [('ffn_a_norm', [DEPTH, D]), ('ffn_a_w_in', [DEPTH, D, 2 * DFF]), ('ffn_a_w_out', [DEPTH, DFF, D]),
                    ('mix_norm', [DEPTH, D]), ('w_in', [DEPTH, D, IN_W]), ('mla_q_norm', [DEPTH, 384]),
                    ('mla_w_uq', [DEPTH, 384, 768]), ('mla_kv_norm', [DEPTH, 256]), ('mla_w_ukv', [DEPTH, 256, 1024]),
                    ('hgrn_lower_bounds', [DEPTH, 512]), ('hgrn_out_norm', [DEPTH, 512]),
                    ('w_br_mla', [DEPTH, 512, D]), ('w_br_sb', [DEPTH, 512, D]), ('w_br_hgrn', [DEPTH, 512, D]),
                    ('w_out', [DEPTH, D, D]), ('ffn_b_norm', [DEPTH, D]), ('ffn_b_w_in', [DEPTH, D, 2 * DFF]),
                    ('ffn_b_w_out', [DEPTH, DFF, D]), ('ple_norm', [DEPTH, D]), ('w_ple_gate', [DEPTH, D, D]),
                    ('w_ple_proj', [DEPTH, PLE, D]), ('final_norm', [D])]:
        W[nm] = din(nm, shp)
    cst_d = din('cst', [128, NCST])
    out_d = nc.dram_tensor('out', [NSEQ, S, D], F32, kind='ExternalOutput').ap()

    hT = nc.alloc_sbuf_tensor('hT', [128, 8 * S], F32)[:, :].rearrange('p (c s) -> p c s', c=8)
    cstf = nc.alloc_sbuf_tensor('cstf', [128, NCST], F32)
    cstb = nc.alloc_sbuf_tensor('cstb', [128, NCST], BF)
    NW = 41
    nw = nc.alloc_sbuf_tensor('nw', [128, DEPTH * NW + 16], F32)
    lbt = nc.alloc_sbuf_tensor('lbt', [128, 16], F32)
    cosT = nc.alloc_sbuf_tensor('cosT', [128, S], F32)
    sinT = nc.alloc_sbuf_tensor('sinT', [128, S], F32)
    ps = [nc.alloc_psum_tensor('ps%d' % b, [128, 512], F32) for b in range(8)]
    arena_bytes = nc.sbuf_bytes_remaining - 1024
    arena_bytes = min(arena_bytes, 124 * 1024) // 64 * 64
    AR = Arena(nc, arena_bytes)

    ident_f = cstf[:, C_ID:C_ID + 128]
    ident_b = cstb[:, C_ID:C_ID + 128]
    ones_b = cstb[:, C_ONES:C_ONES + 128]
    uinc_b = cstb[:, C_UINC:C_UINC + 128]

    bank_rr = [0]

    def bank():
        b = bank_rr[0]
        bank_rr[0] = (b + 1) % 8
        return b

    def PSK(b):
        return ('ps', b)

    def dma(q, out, in_, r=(), w=()):
        eng = 'pool' if q == 'pool' else 'sp'
        return P.add(eng, lambda e: e.dma_start(out=out, in_=in_), r=r, w=w, dma=True)

    def mm(b_out, pairs, r=(), w=()):
        def fn(e):
            n = len(pairs)
            ins = None
            for i, (l, rr) in enumerate(pairs):
                ins = e.matmul(b_out, l, rr, start=(i == 0), stop=(i == n - 1))
            return ins
        return P.add('pe', fn, r=r, w=w)

    def act(out, in_, func, r=(), w=(), scale=1.0, bias=0.0):
        return P.add('act', lambda e: e.activation(out, in_, func, bias=bias, scale=scale), r=r, w=w)

    def ts(out, in0, s1, s2, op0, op1=None, r=(), w=(), eng='dve'):
        if op1 is None:
            return P.add(eng, lambda e: e.tensor_scalar(out, in0, s1, None, op0), r=r, w=w)
        return P.add(eng, lambda e: e.tensor_scalar(out, in0, s1, s2, op0, op1), r=r, w=w)

    def tt(out, in0, in1, op, r=(), w=(), eng='dve'):
        return P.add(eng, lambda e: e.tensor_tensor(out, in0, in1, op), r=r, w=w)

    def stt(out, in0, sc, in1, op0, op1, r=(), w=()):
        return P.add('dve', lambda e: e.scalar_tensor_tensor(out, in0, sc, in1, op0, op1), r=r, w=w)

    def cp(out, in_, r=(), w=(), eng='dve'):
        if eng == 'act':
            return P.add('act', lambda e: e.copy(out, in_), r=r, w=w)
        return P.add(eng, lambda e: e.tensor_copy(out, in_), r=r, w=w)

    dma('sp', cstf[:, :], cst_d, w=['cstf'])
    cp(cstb[:, :], cstf[:, :], r=['cstf'], w=['cstb'])
    nwk = 'nw'
    nwst = nc.alloc_sbuf_tensor('nwst', [128, 128], F32)
    P.add('dve', lambda e: e.memset(nwst[:, :], 0.0), w=['nwst'])
    for l in range(DEPTH):
        base = l * NW
        for nm, off, nch in [('ffn_a_norm', 0, 8), ('mix_norm', 8, 8), ('ffn_b_norm', 16, 8), ('ple_norm', 24, 8),
                             ('mla_q_norm', 32, 3), ('mla_kv_norm', 35, 2), ('hgrn_out_norm', 37, 4)]:
            dma('sp', nwst[base + off:base + off + nch, :], W[nm][l].rearrange('(c p) -> c p', p=128), w=['nwst'])
    dma('sp', nwst[DEPTH * NW:DEPTH * NW + 8, :], W['final_norm'].rearrange('(c p) -> c p', p=128), w=['nwst'])
    for l in range(DEPTH):
        r0 = DEPTH * NW + 8 + l * 4
        dma('sp', nwst[r0:r0 + 4, :], W['hgrn_lower_bounds'][l].rearrange('(c p) -> c p', p=128), w=['nwst'])
    P.add('pe', lambda e: e.transpose(ps[0][:, 0:128], nwst[:, :], ident_f), r=['nwst', 'cstf'], w=[PSK(0)])
    cp(nw[:, :], ps[0][:, 0:DEPTH * NW + 16], r=[PSK(0)], w=[nwk])

    def nwcol(l, off, c):
        i = l * NW + off + c
        return nw[:, i:i + 1]

    def rmsnorm_gen(src_fn, nch, dim, wcol_fn, dst_fn, rkeys, wkeys, tmp):
        sq = tmp['sq']
        kp = tmp.get('kp', '')
        b = tmp['bank'] if 'bank' in tmp else bank()
        rsk = kp + 'rs'
        for c in range(nch):
            s = c % 2
            act(sq[:, s, :], src_fn(c), AF.Square, r=list(rkeys), w=[(kp + 'sq', s)])
            P.add('pe', lambda e, c=c, s=s: e.matmul(ps[b][:, :], ones_b, sq[:, s, :], start=(c == 0), stop=(c == nch - 1)),
                  r=[(kp + 'sq', s), 'cstb'], w=[PSK(b)])
            yield
        rs = tmp['rs']
        act(rs, ps[b][:, :], AF.Ln, r=[PSK(b)], w=[rsk], scale=1.0 / dim, bias=tmp['eps'])
        act(rs, rs, AF.Exp, r=[rsk], w=[rsk], scale=-0.5)
        yield
        for c in range(nch):
            stt(dst_fn(c), src_fn(c), wcol_fn(c), rs, ALU.mult, ALU.mult, r=list(rkeys) + [rsk, nwk], w=list(wkeys))
            yield

    def rmsnorm_tile(src_fn, nch, dim, wcol_fn, dst_fn, rkeys, wkeys, tmp, src_is_psum=False):
        for _ in rmsnorm_gen(src_fn, nch, dim, wcol_fn, dst_fn, rkeys, wkeys, tmp):
            pass

    def load_seq(sq_i):
        AR.reset()
        stage = AR.alloc([2, D], F32)
        for tb in range(NB):
            s = tb % 2
            t = tb // 4
            dma('sp', stage[:, s, :], x_d[sq_i, tb * 128:(tb + 1) * 128, :], w=[('stg', s)])
            for half in range(2):
                b = bank()
                def fn(e, s=s, half=half, b=b):
                    ins = None
                    for i in range(4):
                        c = half * 4 + i
                        ins = e.transpose(ps[b][:, i * 128:(i + 1) * 128], stage[:, s, c * 128:(c + 1) * 128], ident_f)
                    return ins
                P.add('pe', fn, r=[('stg', s), 'cstf'], w=[PSK(b)])
                cp(hT[:, half * 4:half * 4 + 4, tb * 128:(tb + 1) * 128],
                   ps[b][:, :].rearrange('p (a b) -> p a b', a=4), r=[PSK(b)], w=[('h', t)],
                   eng=('act' if half else 'dve'))
        P.fence()

    def store_seq(sq_i):
        AR.reset()
        tmp = {'sq': AR.alloc([2, 512], BF), 'rs': AR.alloc([512], F32), 'eps': epsc}
        yT = AR.alloc([8, 512], F32)
        stage = AR.alloc([2, D], F32)
        for t in range(NT):
            tok = slice(t * 512, (t + 1) * 512)
            if final_norm:
                rmsnorm_tile(lambda c: hT[:, c, tok], 8, D, lambda c: nw[:, DEPTH * NW + c:DEPTH * NW + c + 1],
                             lambda c: yT[:, c, :], [('h', t)], ['yT'], tmp)
                src = yT
                srck = 'yT'
                sl = lambda c, j: yT[:, c, j * 128:(j + 1) * 128]
            else:
                srck = ('h', t)
                sl = lambda c, j: hT[:, c, t * 512 + j * 128:t * 512 + (j + 1) * 128]
            for j in range(4):
                tb = t * 4 + j
                s = tb % 2
                for half in range(2):
                    b = bank()
                    def fn(e, half=half, b=b, j=j, sl=sl):
                        ins = None
                        for i in range(4):
                            c = half * 4 + i
                            ins = e.transpose(ps[b][:, i * 128:(i + 1) * 128], sl(c, j), ident_f)
                        return ins
                    P.add('pe', fn, r=[srck, 'cstf'], w=[PSK(b)])
                    cp(stage[:, s, half * 512:(half + 1) * 512], ps[b][:, :], r=[PSK(b)], w=[('stg', s)],
                       eng=('act' if half else 'dve'))
                dma('sp', out_d[sq_i, tb * 128:(tb + 1) * 128, :], stage[:, s, :], r=[('stg', s)], w=[('out', sq_i, tb)])
        P.fence()

    def ffn(l, which):
        wi_d = W['ffn_%s_w_in' % which][l].rearrange('(k p) c -> p k c', p=128)
        wo_d = W['ffn_%s_w_out' % which][l].rearrange('(j p) c -> p j c', p=128)
        noff = 0 if which == 'a' else 16
        TS = min(S, 1024)
        NTS = TS // 512
        def ffn_norm(st, uT, tmp):
            for tt_ in range(NTS):
                t = st * NTS + tt_
                tok = slice(t * 512, (t + 1) * 512)
                utok = slice(tt_ * 512, (tt_ + 1) * 512)
                rmsnorm_tile(lambda c: hT[:, c, tok], 8, D, lambda c: nwcol(l, noff, c),
                             lambda c: uT[:, c, utok], [('h', t)], [('u', tt_)], tmp)

        for st in range(S // TS):
            AR.reset()
            tmp = {'sq': AR.alloc([2, 512], BF), 'rs': AR.alloc([512], F32), 'eps': epsc}
            uT = AR.alloc([8, TS], BF)
            aT = AR.alloc([NJ, TS], BF)
            wi = AR.alloc([2, 8, 512], BF)
            wo = AR.alloc([2, NJ, 128], BF)
            sg = AR.alloc([2, 512], F32)
            if st == 0:
                ffn_norm(st, uT, tmp)
            sgi = 0
            for jp in range(NJ // 2):
                s = jp % 2
                dma('pool', wi[:, s, :, 0:256], wi_d[:, :, jp * 256:(jp + 1) * 256], w=[('wi', s)])
                dma('pool', wi[:, s, :, 256:512], wi_d[:, :, DFF + jp * 256:DFF + (jp + 1) * 256], w=[('wi', s)])
                for jj in range(2):
                    j = jp * 2 + jj
                    for tt_ in range(NTS):
                        utok = slice(tt_ * 512, (tt_ + 1) * 512)
                        bg, bu = bank(), bank()
                        mm(ps[bg][:, :], [(wi[:, s, k, jj * 128:(jj + 1) * 128], uT[:, k, utok]) for k in range(8)],
                           r=[('wi', s), ('u', tt_)], w=[PSK(bg)])
                        mm(ps[bu][:, :], [(wi[:, s, k, 256 + jj * 128:256 + (jj + 1) * 128], uT[:, k, utok]) for k in range(8)],
                           r=[('wi', s), ('u', tt_)], w=[PSK(bu)])
                        q = sgi % 2
                        sgi += 1
                        act(sg[:, q, :], ps[bg][:, :], AF.Silu, r=[PSK(bg)], w=[('sg', q)])
                        tt(aT[:, j, utok], sg[:, q, :], ps[bu][:, :], ALU.mult, r=[('sg', q), PSK(bu)], w=[('a', j, tt_)])
            if st + 1 < S // TS:
                ffn_norm(st + 1, uT, tmp)
            for c in range(8):
                s = c % 2
                dma('pool', wo[:, s, :, :], wo_d[:, :, c * 128:(c + 1) * 128], w=[('wo', s)])
                for tt_ in range(NTS):
                    t = st * NTS + tt_
                    tok = slice(t * 512, (t + 1) * 512)
                    utok = slice(tt_ * 512, (tt_ + 1) * 512)
                    b = bank()
                    mm(ps[b][:, :], [(wo[:, s, j, :], aT[:, j, utok]) for j in range(NJ)],
                       r=[('wo', s)] + [('a', j, tt_) for j in range(NJ)], w=[PSK(b)])
                    stt(hT[:, c, tok], ps[b][:, :], 0.5, hT[:, c, tok], ALU.mult, ALU.add, r=[PSK(b), ('h', t)], w=[('h', t)])
        P.fence()

    def ple(l, sq_i):
        AR.reset()
        tmp = {'sq': AR.alloc([2, 512], BF), 'rs': AR.alloc([512], F32), 'eps': epsc}
        uT = AR.alloc([8, 512], BF)
        pT = AR.alloc([2, S], BF)
        wg = AR.alloc([8, D], BF)
        wp = AR.alloc([2, D], BF)
        stage = AR.alloc([2, PLE], F32)
        sgm = AR.alloc([2, 512], F32)
        dma('pool', wg, W['w_ple_gate'][l].rearrange('(k p) c -> p k c', p=128), w=['wg'])
        dma('pool', wp, W['w_ple_proj'][l].rearrange('(k p) c -> p k c', p=128), w=['wp'])
        for tb in range(NB):
            s = tb % 2
            dma('sp', stage[:, s, :], p_d[l, sq_i, tb * 128:(tb + 1) * 128, :], w=[('stg', s)])
            b = bank()
            def fn(e, s=s, b=b):
                ins = None
                for i in range(2):
                    ins = e.transpose(ps[b][:, i * 128:(i + 1) * 128], stage[:, s, i * 128:(i + 1) * 128], ident_f)
                return ins
            P.add('pe', fn, r=[('stg', s), 'cstf'], w=[PSK(b)])
            cp(pT[:, :, tb * 128:(tb + 1) * 128], ps[b][:, 0:256].rearrange('p (a b) -> p a b', a=2), r=[PSK(b)], w=[('pT', tb // 4)])
        for t in range(NT):
            tok = slice(t * 512, (t + 1) * 512)
            rmsnorm_tile(lambda c: hT[:, c, tok], 8, D, lambda c: nwcol(l, 24, c),
                         lambda c: uT[:, c, :], [('h', t)], ['u'], tmp)
            for c in range(8):
                cs = slice(c * 128, (c + 1) * 128)
                bg, bp = bank(), bank()
                mm(ps[bg][:, :], [(wg[:, k, cs], uT[:, k, :]) for k in range(8)], r=['wg', 'u'], w=[PSK(bg)])
                mm(ps[bp][:, :], [(wp[:, k, cs], pT[:, k, tok]) for k in range(2)], r=['wp', ('pT', t)], w=[PSK(bp)])
                q = c % 2
                act(sgm[:, q, :], ps[bg][:, :], AF.Sigmoid, r=[PSK(bg)], w=[('sg', q)])
                tt(sgm[:, q, :], sgm[:, q, :], ps[bp][:, :], ALU.mult, r=[('sg', q), PSK(bp)], w=[('sg', q)])
                tt(hT[:, c, tok], hT[:, c, tok], sgm[:, q, :], ALU.add, r=[('sg', q), ('h', t)], w=[('h', t)])
        P.fence()


    nuinc_b = cstb[:, C_NUINC:C_NUINC + 128]
    nones_b = cstb[:, C_NONES:C_NONES + 128]
    mmla_b = cstb[:, C_MMLA:C_MMLA + 128]
    msb_b = cstb[:, C_MSB:C_MSB + 128]
    msb_f = cstf[:, C_MSB:C_MSB + 128]
    mhg_f = cstf[:, C_MHG:C_MHG + 128]
    rst_f = cstf[:, C_RST:C_RST + 512]
    TWO_PI = 6.283185307179586
    C1 = 6.28125
    C2 = TWO_PI - C1

    def mm1(out, l, rr, start, stop, r=(), w=(), skip=False):
        return P.add('pe', lambda e: e.matmul(out, l, rr, start=start, stop=stop, skip_group_check=skip), r=r, w=w)

    def rope_tables(sq_i):
        AR.reset()
        posi = AR.alloc([S], I32 if False else F32)
        posi_i = posi.bitcast(I32)
        ang = AR.alloc([S], F32)
        kf = AR.alloc([S], F32)
        ki = AR.alloc([S], F32)
        ki_i = ki.bitcast(I32)
        dma('sp', posi_i, pos_d[sq_i:sq_i + 1, :].partition_broadcast(128), w=['posi'])
        cp(ang, posi_i, r=['posi'], w=['ang'])
        ts(ang, ang, cstf[:, C_INVF:C_INVF + 1], None, ALU.mult, r=['ang', 'cstf'], w=['ang'])
        for tab, shift, key in ((sinT, 0.0, 'sinT'), (cosT, np.pi / 2, 'cosT')):
            ts(kf, ang, shift, 1.0 / TWO_PI, ALU.add, ALU.mult, r=['ang'], w=['kf'])
            cp(ki_i, kf, r=['kf'], w=['ki'])
            cp(kf, ki_i, r=['ki'], w=['kf'])
            ts(tab[:, :], ang, shift, None, ALU.add, r=['ang'], w=[key])
            stt(tab[:, :], kf, -C1, tab[:, :], ALU.mult, ALU.add, r=['kf', key], w=[key])
            stt(tab[:, :], kf, -C2, tab[:, :], ALU.mult, ALU.add, r=['kf', key], w=[key])
            ts(kf, tab[:, :], np.pi, -TWO_PI, ALU.is_gt, ALU.mult, r=[key], w=['kf'])
            tt(tab[:, :], tab[:, :], kf, ALU.add, r=[key, 'kf'], w=[key])
            ts(kf, tab[:, :], -np.pi, TWO_PI, ALU.is_lt, ALU.mult, r=[key], w=['kf'])
            tt(tab[:, :], tab[:, :], kf, ALU.add, r=[key, 'kf'], w=[key])
            ts(tab[:, :], tab[:, :], 3.14159, -3.14159, ALU.min, ALU.max, r=[key], w=[key])
            act(tab[:, :], tab[:, :], AF.Sin, r=[key], w=[key])
        ts(sinT[:, :], sinT[:, :], cstf[:, C_SGN:C_SGN + 1], None, ALU.mult, r=['sinT', 'cstf'], w=['sinT'])
        P.fence()

    def mixer(l, sq_i, branches):
        AR.reset()
        yC = AR.alloc([4, S], BF)
        offC = AR.off
        yA = AR.alloc([4, S], BF)
        offA = AR.off
        yB = AR.alloc([4, S], BF)
        base0 = AR.off
        yv = {0: yA, 1: yB, 2: yC}
        w_in_d = W['w_in'][l].rearrange('(k p) c -> p k c', p=128)
        SC_A = 96.0 ** -0.5

        def mknorm(tmp):
            def norm_tile(t, uT):
                tok = slice(t * 512, (t + 1) * 512)
                rmsnorm_tile(lambda c: hT[:, c, tok], 8, D, lambda c: nwcol(l, 8, c),
                             lambda c: uT[:, c, :], [('h', t)], ['u'], tmp)
            return norm_tile

        def interleave(gens):
            gens = list(gens)
            while gens:
                for g_ in list(gens):
                    try:
                        next(g_)
                    except StopIteration:
                        gens.remove(g_)

        def bank_x(*excl):
            b = bank()
            while b in excl:
                b = bank()
            return b

        def branch_a():
            AR.reset(offA)
            cqn = AR.alloc([3, S], BF)
            ckvn = AR.alloc([2, S], BF)
            krT = AR.alloc([S], BF)
            base1 = AR.off
            wl = AR.alloc([8, 704], BF)
            dma('pool', wl[:, :, 0:672], w_in_d[:, :, 0:672], w=['wl'])
            dma('pool', wl[:, :, 672:688], w_in_d[:, :, 656:672], w=['wl'])
            dma('pool', wl[:, :, 688:704], w_in_d[:, :, 640:656], w=['wl'])
            NLC = 2 if NT >= 2 else 1
            CH = []
            for i in range(NLC):
                CH.append({'tmp': {'sq': AR.alloc([2, 512], BF), 'rs': AR.alloc([512], F32), 'eps': epsc, 'kp': 'L%d' % i, 'bank': 4 * i + 3},
                           'uT': AR.alloc([8, 512], BF), 't1': AR.alloc([512], F32), 't2': AR.alloc([512], F32)})

            def latent(t, i):
                c_ = CH[i]
                tmp, uT, t1, t2 = c_['tmp'], c_['uT'], c_['t1'], c_['t2']
                uk = ('uL', i)
                bb0 = 4 * i
                tok = slice(t * 512, (t + 1) * 512)
                yield from rmsnorm_gen(lambda c: hT[:, c, tok], 8, D, lambda c: nwcol(l, 8, c),
                                       lambda c: uT[:, c, :], [('h', t)], [uk], tmp)
                for (o0, nch, dim, noff, dst, dk) in ((0, 3, 384, 32, cqn, 'cqn'), (384, 2, 256, 35, ckvn, 'ckvn')):
                    bs = [bb0 + c for c in range(nch)]
                    for c in range(nch):
                        mm(ps[bs[c]][:, :], [(wl[:, k, o0 + c * 128:o0 + (c + 1) * 128], uT[:, k, :]) for k in range(8)],
                           r=['wl', uk], w=[PSK(bs[c])])
                        yield
                    yield from rmsnorm_gen(lambda c: ps[bs[c]][:, :], nch, dim, lambda c: nwcol(l, noff, c),
                                           lambda c: dst[:, c, tok], [PSK(b) for b in bs], [(dk, t)], tmp)
                ba, bb = bb0, bb0 + 1
                mm(ps[ba][0:96, :], [(wl[:, k, 576:672], uT[:, k, :]) for k in range(8)], r=['wl', uk], w=[PSK(ba)])
                mm(ps[bb][0:96, :], [(wl[:, k, 608:704], uT[:, k, :]) for k in range(8)], r=['wl', uk], w=[PSK(bb)])
                yield
                tt(t1[64:96, :], ps[ba][64:96, :], cosT[64:96, tok], ALU.mult, r=[PSK(ba), 'cosT'], w=[('t1', i)])
                tt(t2[64:96, :], ps[bb][64:96, :], sinT[64:96, tok], ALU.mult, r=[PSK(bb), 'sinT'], w=[('t2', i)])
                yield
                tt(krT[64:96, tok], t1[64:96, :], t2[64:96, :], ALU.add, r=[('t1', i), ('t2', i)], w=[('krT', t)])
                yield

            for t0 in range(0, NT, NLC):
                interleave([latent(t0 + i, i) for i in range(min(NLC, NT - t0))])
            P.fence()
            AR.reset(base1)
            wuq = AR.alloc([3, 768], BF)
            wsw = AR.alloc([3, 8, 96], BF)
            wukv = AR.alloc([2, 1024], BF)
            qh = AR.alloc([2, S], BF)
            kh = AR.alloc([2, S], BF)
            vh = AR.alloc([2, NB, 128], BF)
            pT = AR.alloc([4, 512], BF)
            t1 = AR.alloc([512], F32)
            t2 = AR.alloc([512], F32)
            wuq_d = W['mla_w_uq'][l].rearrange('(k p) c -> p k c', p=128)
            wuq_hd = W['mla_w_uq'][l].rearrange('(k p) (h d) -> p k h d', p=128, d=96)
            P.add('pool', lambda e: e.memset(wsw, 0.0), w=['wsw'])
            dma('pool', wuq, wuq_d, w=['wuq'])
            for k in range(3):
                dma('pool', wsw[:, k, :, 64:80], wuq_hd[:, k, :, 80:96], w=['wsw'])
                dma('pool', wsw[:, k, :, 80:96], wuq_hd[:, k, :, 64:80], w=['wsw'])
            dma('pool', wukv, W['mla_w_ukv'][l].rearrange('(k p) c -> p k c', p=128), w=['wukv'])
            P.add('dve', lambda e: e.memset(vh[:, :, :, 64:128], 1.0), w=[('vh', 0), ('vh', 1)])
            rec = AR.alloc([2, 512], F32)
            state = {'pti': 0, 'boi': 0, 'zi': 0}

            def proj(h):
                hs = h % 2
                b1, b2, b3, b4 = 0, 1, 2, 3
                for t in range(NT):
                    tok = slice(t * 512, (t + 1) * 512)
                    mm(ps[b1][0:96, :], [(wuq[:, k, h * 96:(h + 1) * 96], cqn[:, k, tok]) for k in range(3)],
                       r=['wuq', ('cqn', t)], w=[PSK(b1)])
                    mm(ps[b2][0:96, :], [(wsw[:, k, h, :], cqn[:, k, tok]) for k in range(3)],
                       r=['wsw', ('cqn', t)], w=[PSK(b2)])
                    mm(ps[b3][:, :], [(wukv[:, k, h * 128:(h + 1) * 128], ckvn[:, k, tok]) for k in range(2)],
                       r=['wukv', ('ckvn', t)], w=[PSK(b3)])
                    for j in range(4):
                        blk = slice(t * 512 + j * 128, t * 512 + (j + 1) * 128)
                        mm(ps[b4][:, j * 64:(j + 1) * 64], [(ckvn[:, k, blk], wukv[:, k, h * 128 + 64:(h + 1) * 128]) for k in range(2)],
                           r=['wukv', ('ckvn', t)], w=[PSK(b4)])
                    act(qh[0:64, hs, tok], ps[b1][0:64, :], AF.Copy, r=[PSK(b1)], w=[('qh', hs)], scale=SC_A)
                    stt(t1[64:96, :], ps[b1][64:96, :], SC_A, cosT[64:96, tok], ALU.mult, ALU.mult, r=[PSK(b1), 'cosT'], w=['t1'])
                    stt(t2[64:96, :], ps[b2][64:96, :], SC_A, sinT[64:96, tok], ALU.mult, ALU.mult, r=[PSK(b2), 'sinT'], w=['t2'])
                    tt(qh[64:96, hs, tok], t1[64:96, :], t2[64:96, :], ALU.add, r=['t1', 't2'], w=[('qh', hs)])
                    cp(kh[0:64, hs, tok], ps[b3][0:64, :], r=[PSK(b3)], w=[('kh', hs)], eng='act')
                    cp(kh[64:96, hs, tok], krT[64:96, tok], r=[('krT', t)], w=[('kh', hs)], eng='pool')
                    cp(vh[:, hs, t * 4:(t + 1) * 4, 0:64], ps[b4][:, 0:256].rearrange('p (a b) -> p a b', a=4),
                       r=[PSK(b4)], w=[('vh', hs)])
                    yield

            def attn(h):
                hs = h % 2
                for tq in range(NT):
                    qtok = slice(tq * 512, (tq + 1) * 512)
                    bo = 4 + state['boi'] % 2
                    rsl = state['boi'] % 2
                    state['boi'] += 1
                    nk = 4 * (tq + 1)
                    pq = []
                    for kc in range(nk):
                        j = kc - 4 * tq
                        c0 = 128 * j if j > 0 else 0
                        bz = 6 + state['zi'] % 2
                        state['zi'] += 1
                        mm1(ps[bz][:, c0:512], kh[0:96, hs, kc * 128:(kc + 1) * 128], qh[0:96, hs, tq * 512 + c0:(tq + 1) * 512],
                            True, True, r=[('kh', hs), ('qh', hs)], w=[PSK(bz)])
                        q = state['pti'] % 4
                        state['pti'] += 1
                        act(pT[:, q, c0:512], ps[bz][:, c0:512], AF.Exp, r=[PSK(bz)], w=[('pT', q)])
                        if j >= 0:
                            tt(pT[:, q, c0:c0 + 128], pT[:, q, c0:c0 + 128], mmla_b, ALU.mult, r=[('pT', q), 'cstb'], w=[('pT', q)])
                        if len(pq) >= 2:
                            pq.pop(0)()
                        pq.append(lambda kc=kc, q=q, c0=c0: mm1(ps[bo][:, c0:512], vh[:, hs, kc, :], pT[:, q, c0:512], kc == 0, kc == nk - 1,
                                                                 r=[('vh', hs), ('pT', q)], w=[PSK(bo)]))
                        yield
                    while pq:
                        pq.pop(0)()
                    P.add('dve', lambda e, bo=bo, rsl=rsl: e.reciprocal(rec[0:64, rsl, :], ps[bo][64:128, :]), r=[PSK(bo)], w=[('rec', rsl)])
                    tt(yA[hs * 64:(hs + 1) * 64, h // 2, qtok], ps[bo][0:64, :], rec[0:64, rsl, :], ALU.mult, r=[PSK(bo), ('rec', rsl)], w=[('yT', 0)])
                    yield

            interleave([proj(0)])
            for h in range(8):
                gens = [attn(h)]
                if h < 7:
                    gens.append(proj(h + 1))
                interleave(gens)
            P.fence()

        def branch_c():
            AR.reset(offC)
            LB0 = DEPTH * NW + 8
            if l == 0:
                P.add('dve', lambda e: e.memset(lbt[:, 0:4], 0.0), w=['lbt'])
                P.add('dve', lambda e: e.memset(lbt[:, 4:8], 1.0), w=['lbt'])
                P.add('dve', lambda e: e.memset(lbt[:, 8:12], -1.0), w=['lbt'])
            else:
                tt(lbt[:, 12:16], nw[:, LB0:LB0 + 4], nw[:, LB0 + 4:LB0 + 8], ALU.subtract, r=[nwk], w=['lbt'])
                act(lbt[:, 12:16], lbt[:, 12:16], AF.Exp, r=['lbt'], w=['lbt'])
                ts(lbt[:, 12:16], lbt[:, 12:16], 1.0, None, ALU.add, r=['lbt'], w=['lbt'])
                P.add('dve', lambda e: e.reciprocal(lbt[:, 0:4], lbt[:, 12:16]), r=['lbt'], w=['lbt'])
                ts(lbt[:, 0:4], lbt[:, 0:4], 1.0 - 1e-6, 0.0, ALU.min, ALU.max, r=['lbt'], w=['lbt'])
                ts(lbt[:, 4:8], lbt[:, 0:4], -1.0, 1.0, ALU.mult, ALU.add, r=['lbt'], w=['lbt'])
                ts(lbt[:, 8:12], lbt[:, 4:8], -1.0, None, ALU.mult, r=['lbt'], w=['lbt'])
            NCH = S // 64
            uT = AR.alloc([8, S], BF)
            QF = AR.alloc([S], BF)
            KF = AR.alloc([S], BF)
            KFt = AR.alloc([NB, 128], BF)
            Vt = AR.alloc([NB, 128], BF)
            GS = AR.alloc([S], BF)
            dec = AR.alloc([NCH], F32)
            e1 = AR.alloc([NCH + 1], F32)
            e2 = AR.alloc([NCH], F32)
            St = AR.alloc([128], F32)
            Sbf = AR.alloc([2, 128], BF)
            Am = AR.alloc([2, 128], BF)
            tmp = {'sq': AR.alloc([2, 512], BF), 'rs': AR.alloc([512], F32), 'eps': epsc}
            whg = AR.alloc([8, 512], BF)
            NCHAIN = max(1, min(2, NT // 2)) if NT > 1 else 1
            osq = AR.alloc([512], BF)
            o1 = AR.alloc([512], F32)
            o2 = AR.alloc([512], F32)
            cb = [0]

            def cbank():
                b = cb[0] % 6
                cb[0] += 1
                return b
            pending_rec = None
            X = [[AR.alloc([512], F32) for _ in range(3)] for _ in range(NCHAIN)]
            B32 = [AR.alloc([512], F32) for _ in range(NCHAIN)]
            T8 = [AR.alloc([8], F32) for _ in range(NCHAIN)]
            for t in range(NT):
                tok = slice(t * 512, (t + 1) * 512)
                rmsnorm_tile(lambda c: hT[:, c, tok], 8, D, lambda c: nwcol(l, 8, c),
                             lambda c: uT[:, c, tok], [('h', t)], [('u', t)], tmp)
            for h in range(4):
                for i, o in enumerate((O_HQ, O_HF, O_HI, O_HG)):
                    dma('pool', whg[:, :, i * 128:(i + 1) * 128], w_in_d[:, :, o + h * 128:o + (h + 1) * 128], w=['whg'])
                lbc, omlc, nomlc = lbt[:, h:h + 1], lbt[:, 4 + h:5 + h], lbt[:, 8 + h:9 + h]

                def prep(t, sl):
                    x1, x2, x3 = X[sl]
                    b32 = B32[sl]
                    t8 = T8[sl]
                    xk = lambda i: ('x', sl, i)
                    tok = slice(t * 512, (t + 1) * 512)
                    ch = slice(t * 8, (t + 1) * 8)
                    b = cbank()
                    mm(ps[b][:, :], [(whg[:, k, 128:256], uT[:, k, tok]) for k in range(8)], r=['whg', ('u', t)], w=[PSK(b)])
                    act(x1, ps[b][:, :], AF.Exp, r=[PSK(b)], w=[xk(1)], scale=-1.0)
                    yield
                    b = cbank()
                    mm(ps[b][:, :], [(whg[:, k, 0:128], uT[:, k, tok]) for k in range(8)], r=['whg', ('u', t)], w=[PSK(b)])
                    act(QF[:, tok], ps[b][:, :], AF.Silu, r=[PSK(b)], w=[('QF', t)])
                    ts(x1, x1, 1.0, None, ALU.add, r=[xk(1)], w=[xk(1)])
                    yield
                    b = cbank()
                    mm(ps[b][:, :], [(whg[:, k, 384:512], uT[:, k, tok]) for k in range(8)], r=['whg', ('u', t)], w=[PSK(b)])
                    act(GS[:, tok], ps[b][:, :], AF.Silu, r=[PSK(b)], w=[('GS', t)])
                    P.add('dve', lambda e: e.reciprocal(x2, x1), r=[xk(1)], w=[xk(2)])
                    yield
                    b = cbank()
                    for j in range(4):
                        mm(ps[b][:, j * 128:(j + 1) * 128], [(uT[:, k, t * 512 + j * 128:t * 512 + (j + 1) * 128], whg[:, k, 256:384]) for k in range(8)],
                           r=['whg', ('u', t)], w=[PSK(b)])
                    cp(Vt[:, t * 4:(t + 1) * 4, :], ps[b][:, :].rearrange('p (a b) -> p a b', a=4), r=[PSK(b)], w=[('Vt', t)], eng='act')
                    ts(x1, x2, omlc, lbc, ALU.mult, ALU.add, r=[xk(2), 'lbt'], w=[xk(1)])
                    yield
                    act(x3, x1, AF.Ln, r=[xk(1)], w=[xk(3)])
                    ts(KF[:, tok], x2, nomlc, omlc, ALU.mult, ALU.add, r=[xk(2), 'lbt'], w=[('KF', t)])
                    yield
                    P.add('dve', lambda e: e.tensor_tensor_scan(b32, rst_f, x3, 0.0, ALU.mult, ALU.add),
                          r=[xk(3), 'cstf'], w=[('b32', sl)])
                    yield
                    b3 = b32.rearrange('p (c s) -> p c s', s=64)
                    tt(x1.rearrange('p (c s) -> p c s', s=64), b3, b3[:, :, 31:32].broadcast_to([128, 8, 64]), ALU.subtract,
                       r=[('b32', sl)], w=[xk(1)])
                    tt(t8, b3[:, :, 63], b3[:, :, 31], ALU.subtract, r=[('b32', sl)], w=[('t8', sl)])
                    yield
                    act(x3, x1, AF.Exp, r=[xk(1)], w=[xk(3)])
                    act(x1, x1, AF.Exp, r=[xk(1)], w=[xk(1)], scale=-1.0)
                    yield
                    act(e2[:, ch], t8, AF.Exp, r=[('t8', sl)], w=[('e2', t)])
                    tt(QF[:, tok], QF[:, tok], x3, ALU.mult, r=[('QF', t), xk(3)], w=[('QF', t)])
                    yield
                    act(dec[:, ch], b3[:, :, 63], AF.Exp, r=[('b32', sl)], w=[('dec', t)])
                    tt(KF[:, tok], KF[:, tok], x1, ALU.mult, r=[('KF', t), xk(1)], w=[('KF', t)])
                    yield
                    act(e1[:, ch], b3[:, :, 31], AF.Exp, r=[('b32', sl)], w=[('e1', t)])
                    kd = x2.bitcast(BF)
                    tt(kd[:, 0:512].rearrange('p (c s) -> p c s', s=64), KF[:, tok].rearrange('p (c s) -> p c s', s=64),
                       e2[:, ch].unsqueeze(2).broadcast_to([128, 8, 64]), ALU.mult, r=[('KF', t), ('e2', t), xk(2)], w=[xk(2)])
                    yield
                    b = cbank()
                    psb = ps[b][:, :].bitcast(BF)

                    def fn(e):
                        ins = None
                        for j in range(4):
                            ins = e.transpose(psb[:, j * 128:(j + 1) * 128], kd[:, j * 128:(j + 1) * 128], ident_b)
                        return ins
                    P.add('pe', fn, r=[xk(2), 'cstb'], w=[PSK(b)])
                    cp(KFt[:, t * 4:(t + 1) * 4, :], psb[:, 0:512].rearrange('p (a b) -> p a b', a=4), r=[PSK(b)], w=[('KFt', t)], eng='act')
                    yield

                def rec(tiles, h=h):
                    for t in tiles:
                        tok = slice(t * 512, (t + 1) * 512)
                        bo = 6 + t % 2
                        if t == 0:
                            P.add('dve', lambda e: e.memset(St, 0.0), w=['St'])
                            P.add('dve', lambda e: e.memset(Sbf[:, 0, :], 0.0), w=[('Sbf', 0)])
                        elif t == tiles[0]:
                            c0_ = 8 * t
                            ts(Sbf[:, c0_ % 2, :], St, e1[:, c0_:c0_ + 1], None, ALU.mult, r=['St', ('e1', t)], w=[('Sbf', c0_ % 2)])
                        for j in range(4):
                            m = t * 4 + j
                            blk = slice(m * 128, (m + 1) * 128)
                            cols = slice(j * 128, (j + 1) * 128)
                            ba = cbank()
                            mm1(ps[ba][:, 0:128], KF[:, blk], QF[:, blk], True, True, r=[('KF', t), ('QF', t)], w=[PSK(ba)])
                            a_ = m % 2
                            tt(Am[:, a_, :], ps[ba][:, 0:128], mhg_f, ALU.mult, r=[PSK(ba), 'cstf'], w=[('Am', a_)])
                            mm1(ps[bo][:, cols], Vt[:, m, :], Am[:, a_, :], True, False, r=[('Vt', t), ('Am', a_)], w=[PSK(bo)])
                            for cch in range(2):
                                c = 2 * m + cch
                                sl = c % 2
                                mm1(ps[bo][:, j * 128 + cch * 64:j * 128 + (cch + 1) * 64], Sbf[:, sl, :], QF[:, c * 64:(c + 1) * 64],
                                    False, cch == 1, r=[('Sbf', sl), ('QF', t)], w=[PSK(bo)])
                                if c == NCH - 1:
                                    continue
                                bs_ = cbank()
                                pr = slice(cch * 64, (cch + 1) * 64)
                                mm1(ps[bs_][:, 0:128], KFt[pr, m, :], Vt[pr, m, :], True, True, r=[('KFt', t), ('Vt', t)], w=[PSK(bs_)])
                                stt(St, St, dec[:, c:c + 1], ps[bs_][:, 0:128], ALU.mult, ALU.add, r=['St', PSK(bs_), ('dec', c // 8)], w=['St'])
                                if c != 8 * (tiles[-1] + 1) - 1:
                                    ts(Sbf[:, 1 - sl, :], St, e1[:, c + 1:c + 2], None, ALU.mult, r=['St', ('e1', (c + 1) // 8)], w=[('Sbf', 1 - sl)])
                                yield
                        act(osq[:, :], ps[bo][:, :], AF.Square, r=[PSK(bo)], w=['osq'])
                        bss = cbank()
                        mm1(ps[bss][:, :], ones_b, osq[:, :], True, True, r=['osq', 'cstb'], w=[PSK(bss)])
                        act(o1, ps[bss][:, :], AF.Ln, r=[PSK(bss)], w=['o1'], scale=1.0 / 128, bias=epsc)
                        yield
                        act(o1, o1, AF.Exp, r=['o1'], w=['o1'], scale=-0.5)
                        tt(o2, ps[bo][:, :], o1, ALU.mult, r=[PSK(bo), 'o1'], w=['o2'])
                        yield
                        stt(yC[:, h, tok], o2, nwcol(l, 37, h), GS[:, tok], ALU.mult, ALU.mult, r=['o2', ('GS', t), nwk], w=[('yT', 2)])
                        yield

                HALF = max(1, NT // 2)
                first = list(range(0, HALF))
                second = list(range(HALF, NT))
                g1 = [prep(t, i) for i, t in enumerate(first)]
                if pending_rec is not None:
                    g1.append(pending_rec)
                interleave(g1)
                g2 = [prep(t, i) for i, t in enumerate(second)]
                g2.append(rec(first))
                interleave(g2)
                pending_rec = rec(second) if second else None
            if pending_rec is not None:
                interleave([pending_rec])
            P.fence()

        if 'C' in branches:
            P.mark('mixC s%d l%d' % (sq_i, l))
            branch_c()

        if 'A' in branches:
            P.mark('mixA s%d l%d' % (sq_i, l))
            branch_a()
        def branch_b():
            for g in range(2):
                AR.reset(base0)
                qz = AR.alloc([4, S], BF)
                kT = AR.alloc([2, S], BF)
                vS = AR.alloc([NB, 256], BF)
                base1 = AR.off
                tmp = {'sq': AR.alloc([2, 512], BF), 'rs': AR.alloc([512], F32), 'eps': epsc}
                norm_tile = mknorm(tmp)
                uT = AR.alloc([8, 512], BF)
                wsb = AR.alloc([8, 768], BF)
                for i, o in enumerate((O_SBQ, O_SBK, O_SBV)):
                    dma('pool', wsb[:, :, i * 256:(i + 1) * 256], w_in_d[:, :, o + g * 256:o + (g + 1) * 256], w=['wsb'])
                P.add('pool', lambda e: e.memset(qz, 0.0), w=[('qT', x) for x in range(NT)])
                for t in range(NT):
                    tok = slice(t * 512, (t + 1) * 512)
                    norm_tile(t, uT)
                    for cc in range(2):
                        bq, bk = bank(), bank()
                        mm(ps[bq][:, :], [(wsb[:, k, cc * 128:(cc + 1) * 128], uT[:, k, :]) for k in range(8)], r=['wsb', 'u'], w=[PSK(bq)])
                        mm(ps[bk][:, :], [(wsb[:, k, 256 + cc * 128:256 + (cc + 1) * 128], uT[:, k, :]) for k in range(8)], r=['wsb', 'u'], w=[PSK(bk)])
                        act(qz[0:64, 2 * cc, tok], ps[bq][0:64, :], AF.Copy, r=[PSK(bq)], w=[('qT', t)], scale=0.125)
                        act(qz[64:128, 2 * cc + 1, tok], ps[bq][64:128, :], AF.Copy, r=[PSK(bq)], w=[('qT', t)], scale=0.125)
                        cp(kT[:, cc, tok], ps[bk][:, :], r=[PSK(bk)], w=[('kT', t)])
                    for jj in range(2):
                        bv = bank()
                        for i in range(2):
                            j = jj * 2 + i
                            mm(ps[bv][:, i * 256:(i + 1) * 256], [(uT[:, k, j * 128:(j + 1) * 128], wsb[:, k, 512:768]) for k in range(8)],
                               r=['wsb', 'u'], w=[PSK(bv)])
                        cp(vS[:, t * 4 + jj * 2:t * 4 + jj * 2 + 2, :], ps[bv][:, :].rearrange('p (a b) -> p a b', a=2),
                           r=[PSK(bv)], w=[('vS', t)], eng='act')
                P.fence()
                AR.reset(base1)
                e32 = AR.alloc([2, 512], F32)
                spb = AR.alloc([3, 512], BF)
                ab = AR.alloc([3, 512], BF)
                R = AR.alloc([512], F32)
                Rb = AR.alloc([4, 512], BF)
                jobs = []
                for hl in range(4):
                    for tq in range(NT):
                        nk = 4 * (tq + 1)
                        for idx, kc in enumerate(range(nk - 1, -1, -1)):
                            jobs.append((hl, tq, nk, idx, kc))
                zb = [0]

                def zbank():
                    b = 2 + zb[0] % 6
                    zb[0] += 1
                    return b

                def mk(n, job):
                    hl, tq, nk, idx, kc = job
                    cc = hl // 2
                    pb = (hl % 2) * 64
                    tile_i = hl * NT + tq
                    bo = tile_i % 2
                    qtok = slice(tq * 512, (tq + 1) * 512)
                    j = kc - 4 * tq
                    c0 = 128 * j if j > 0 else 0
                    cs = slice(c0, 512)
                    q = n % 2
                    q3 = n % 3
                    kap = kT[:, cc, kc * 128:(kc + 1) * 128]
                    qap = qz[:, hl, tq * 512 + c0:(tq + 1) * 512]
                    rs_ = n % 4
                    rsl = (n - 1) % 4

                    def stage1():
                        if idx == 0:
                            P.add('pool', lambda e: e.memset(R[:, 256:512], 0.0), w=[('R', 1)])
                            P.add('dve', lambda e: e.memset(R[:, 0:256], 0.0), w=[('R', 0)])
                        bz = zbank()
                        mm1(ps[bz][:, cs], kap, qap, True, True, r=[('kT', kc // 4), ('qT', tq)], w=[PSK(bz)])
                        act(e32[:, q, cs], ps[bz][:, cs], AF.Exp, r=[PSK(bz)], w=[('e32', q)])
                        act(spb[:, q3, cs], e32[:, q, cs], AF.Ln, r=[('e32', q)], w=[('spb', q3)], bias=onec)
                        if j >= 0:
                            tt(spb[:, q3, c0:c0 + 128], spb[:, q3, c0:c0 + 128], msb_b, ALU.mult, r=[('spb', q3), 'cstb'], w=[('spb', q3)])
                        if idx < nk - 1:
                            jn = j - 1
                            c0n = 128 * jn if jn > 0 else 0
                            hi0 = max(c0, 256)
                            tt(R[:, hi0:512], R[:, hi0:512], spb[:, q3, hi0:512], ALU.add, r=[('R', 1), ('spb', q3)], w=[('R', 1)], eng='pool')
                            if c0 < 256:
                                tt(R[:, c0:256], R[:, c0:256], spb[:, q3, c0:256], ALU.add, r=[('R', 0), ('spb', q3)], w=[('R', 0)])
                            cp(Rb[:, rs_, c0n:512], R[:, c0n:512], r=[('R', 0), ('R', 1)], w=[('Rb', rs_)])

                    def stage2():
                        bc = zbank()

                        def fn(e):
                            e.matmul(ps[bc][:, cs], kap, qap, start=True, stop=False)
                            ins = e.matmul(ps[bc][:, cs], nuinc_b, spb[:, q3, cs], start=False, stop=(idx == 0))
                            if idx > 0:
                                ins = e.matmul(ps[bc][:, cs], nones_b, Rb[:, rsl, cs], start=False, stop=True)
                            return ins
                        P.add('pe', fn, r=[('kT', kc // 4), ('qT', tq), ('spb', q3), 'cstb'] + ([('Rb', rsl)] if idx > 0 else []), w=[PSK(bc)])
                        act(ab[:, q3, cs], ps[bc][:, cs], AF.Exp, r=[PSK(bc)], w=[('ab', q3)])
                        if j >= 0:
                            tt(ab[:, q3, c0:c0 + 128], ab[:, q3, c0:c0 + 128], msb_b, ALU.mult, r=[('ab', q3), 'cstb'], w=[('ab', q3)])

                    def stage3():
                        mm1(ps[bo][:, cs], vS[:, kc, cc * 128:(cc + 1) * 128], ab[:, q3, cs], idx == 0, idx == nk - 1,
                            r=[('vS', kc // 4), ('ab', q3)], w=[PSK(bo)], skip=True)
                        if idx == nk - 1:
                            cp(yB[pb:pb + 64, g * 2 + cc, qtok], ps[bo][pb:pb + 64, :], r=[PSK(bo)], w=[('yT', 1)], eng='act')
                    return stage1, stage2, stage3

                st = [mk(n, job) for n, job in enumerate(jobs)]
                NJB = len(st)
                for n in range(NJB + 2):
                    if n < NJB:
                        st[n][0]()
                    if 0 <= n - 1 < NJB:
                        st[n - 1][1]()
                    if 0 <= n - 2 < NJB:
                        st[n - 2][2]()
                P.fence()

        if 'B' in branches:
            P.mark('mixB s%d l%d' % (sq_i, l))
            branch_b()

        def merge():
            AR.reset(base0)
            tmp = {'sq': AR.alloc([2, 512], BF), 'rs': AR.alloc([512], F32), 'eps': epsc}
            TP = min(S, 1024)
            NTP = TP // 512
            uT = AR.alloc([8, TP], BF)
            mT = AR.alloc([8, TP], BF)
            wgt = AR.alloc([2, 8, 384], BF)
            wbr = AR.alloc([2, 3, 4, 128], BF)
            wo2 = AR.alloc([2, 8, 128], BF)
            sg = AR.alloc([2, 512], F32)
            acc = AR.alloc([2, 512], F32)
            wout_d = W['w_out'][l].rearrange('(k p) c -> p k c', p=128)
            brw = [W[n][l].rearrange('(k p) c -> p k c', p=128) for n in ('w_br_mla', 'w_br_sb', 'w_br_hgrn')]
            brs = [i for i, b in enumerate('ABC') if b in branches]
            wi_ = 0
            sgi = 0
            aci = 0
            for tp in range(S // TP):
                for tt_ in range(NTP):
                    t = tp * NTP + tt_
                    tok = slice(t * 512, (t + 1) * 512)
                    utok = slice(tt_ * 512, (tt_ + 1) * 512)
                    rmsnorm_tile(lambda c: hT[:, c, tok], 8, D, lambda c: nwcol(l, 8, c),
                                 lambda c: uT[:, c, utok], [('h', t)], [('u', tt_)], tmp)
                for c in range(8):
                    s = wi_ % 2
                    wi_ += 1
                    for br in brs:
                        dma('pool', wgt[:, s, :, br * 128:(br + 1) * 128],
                            w_in_d[:, :, O_GATE + br * 1024 + c * 128:O_GATE + br * 1024 + (c + 1) * 128], w=[('wgt', s)])
                        dma('pool', wbr[:, s, br, :, :], brw[br][:, :, c * 128:(c + 1) * 128], w=[('wbr', s)])
                    for tt_ in range(NTP):
                        t = tp * NTP + tt_
                        tok = slice(t * 512, (t + 1) * 512)
                        utok = slice(tt_ * 512, (tt_ + 1) * 512)
                        a_ = aci % 2
                        aci += 1
                        for bi, br in enumerate(brs):
                            bg, bp = bank(), bank()
                            mm(ps[bg][:, :], [(wgt[:, s, k, br * 128:(br + 1) * 128], uT[:, k, utok]) for k in range(8)],
                               r=[('wgt', s), ('u', tt_)], w=[PSK(bg)])
                            mm(ps[bp][:, :], [(wbr[:, s, br, k, :], yv[br][:, k, tok]) for k in range(4)],
                               r=[('wbr', s), ('yT', br)], w=[PSK(bp)])
                            q = sgi % 2
                            sgi += 1
                            act(sg[:, q, :], ps[bg][:, :], AF.Sigmoid, r=[PSK(bg)], w=[('sg', q)])
                            last = (bi == len(brs) - 1)
                            dst = mT[:, c, utok] if last else acc[:, a_, :]
                            dk = [('mT', c, tt_)] if last else [('acc', a_)]
                            if bi == 0:
                                tt(dst, sg[:, q, :], ps[bp][:, :], ALU.mult, r=[('sg', q), PSK(bp)], w=dk)
                            else:
                                tt(sg[:, q, :], sg[:, q, :], ps[bp][:, :], ALU.mult, r=[('sg', q), PSK(bp)], w=[('sg', q)])
                                tt(dst, acc[:, a_, :], sg[:, q, :], ALU.add, r=[('sg', q), ('acc', a_)], w=dk)
                for c2 in range(8):
                    s = c2 % 2
                    dma('pool', wo2[:, s, :, :], wout_d[:, :, c2 * 128:(c2 + 1) * 128], w=[('wo2', s)])
                    for tt_ in range(NTP):
                        t = tp * NTP + tt_
                        tok = slice(t * 512, (t + 1) * 512)
                        utok = slice(tt_ * 512, (tt_ + 1) * 512)
                        b = bank()
                        mm(ps[b][:, :], [(wo2[:, s, k, :], mT[:, k, utok]) for k in range(8)],
                           r=[('wo2', s)] + [('mT', k, tt_) for k in range(8)], w=[PSK(b)])
                        tt(hT[:, c2, tok], hT[:, c2, tok], ps[b][:, :], ALU.add, r=[PSK(b), ('h', t)], w=[('h', t)])
            P.fence()

        P.mark('merge s%d l%d' % (sq_i, l))
        merge()

    epst = nc.alloc_sbuf_tensor('epst', [128, 1], F32)
    epsc = epst[:, 0:1]
    P.add('dve', lambda e: e.memset(epst[:, :], EPS), w=['eps'])
    onet = nc.alloc_sbuf_tensor('onet', [128, 1], F32)
    onec = onet[:, 0:1]
    P.add('dve', lambda e: e.memset(onet[:, :], 1.0), w=['one'])
    P.fence()

    for sq_i in range(NSEQ):
        if 'mix' in phases:
            P.mark('rope s%d' % sq_i)
            rope_tables(sq_i)
        P.mark('load s%d' % sq_i)
        load_seq(sq_i)
        for l in layers:
            if 'ffa' in phases:
                P.mark('ffa s%d l%d' % (sq_i, l))
                ffn(l, 'a')
            if 'mix' in phases:
                brs = ''.join(b for b in 'ABC' if ('mix' + b) in phases) or 'ABC'
                mixer(l, sq_i, brs)
            if 'ffb' in phases:
                P.mark('ffb s%d l%d' % (sq_i, l))
                ffn(l, 'b')
            if 'ple' in phases:
                P.mark('ple s%d l%d' % (sq_i, l))
                ple(l, sq_i)
        P.mark('store s%d' % sq_i)
        store_seq(sq_i)

    P.fence()
    P.emit()
    nc._prog = P
    return nc


C_ID, C_ONES, C_UINC, C_NUINC, C_NONES, C_MMLA, C_MSB, C_MHG, C_RST, C_INVF, C_SGN = 0, 128, 256, 384, 512, 640, 768, 896, 1024, 1536, 1537
NCST = 1540


def make_consts():
    c = np.zeros((128, NCST), np.float32)
    c[:, C_ID:C_ID + 128] = np.eye(128, dtype=np.float32)
    c[:, C_ONES:C_ONES + 128] = 1.0
    j = np.arange(128)[:, None]
    k = np.arange(128)[None, :]
    c[:, C_UINC:C_UINC + 128] = (j >= k).astype(np.float32)
    c[:, C_NUINC:C_NUINC + 128] = -(j >= k).astype(np.float32)
    c[:, C_NONES:C_NONES + 128] = -1.0
    c[:, C_MMLA:C_MMLA + 128] = 1.0 - ((j >= 64) & (k < 64)).astype(np.float32)
    c[:, C_MSB:C_MSB + 128] = (j < k).astype(np.float32)
    c[:, C_MHG:C_MHG + 128] = ((j // 64 == k // 64) & (j <= k)).astype(np.float32)
    c[:, C_RST:C_RST + 512] = (np.arange(512) % 64 != 0).astype(np.float32)[None, :]
    half = 16
    inv = (10000.0 ** (-np.arange(half, dtype=np.float32) / half)).astype(np.float32)
    pp = np.arange(128)
    c[:, C_INVF] = inv[pp % 16]
    c[:, C_SGN] = np.where((pp % 32) < 16, -1.0, 1.0)
    return c


WNAMES = ['ffn_a_norm', 'ffn_a_w_in', 'ffn_a_w_out', 'mix_norm', 'w_in', 'mla_q_norm', 'mla_w_uq', 'mla_kv_norm',
          'mla_w_ukv', 'hgrn_lower_bounds', 'hgrn_out_norm', 'w_br_mla', 'w_br_sb', 'w_br_hgrn', 'w_out',
          'ffn_b_norm', 'ffn_b_w_in', 'ffn_b_w_out', 'ple_norm', 'w_ple_gate', 'w_ple_proj', 'final_norm']


def run(inputs, n_cores=8, **bkw):
    x = np.ascontiguousarray(np.asarray(inputs['x'], dtype=np.float32))
    p = np.ascontiguousarray(np.asarray(inputs['p'], dtype=np.float32))
    pos = np.ascontiguousarray(np.asarray(inputs['positions'], dtype=np.int32))
    B, S, _ = x.shape
    nseq = B // n_cores
    nc = bass.Bass("TRN2", target_bir_lowering=False)
    build(nc, S, nseq, **bkw)
    cst = make_consts()
    wts = {k: np.ascontiguousarray(np.asarray(inputs[k], dtype=np.float32)) for k in WNAMES}
    in_maps = []
    for c in range(n_cores):
        m = {'x': x[c * nseq:(c + 1) * nseq], 'p': np.ascontiguousarray(p[:, c * nseq:(c + 1) * nseq]),
             'positions': pos[c * nseq:(c + 1) * nseq], 'cst': cst}
        m.update(wts)
        in_maps.append(m)
    res = run_bass_kernel_spmd(nc, in_maps, core_ids=list(range(n_cores)))
    return np.concatenate([np.asarray(r['out']) for r in res.results], axis=0).astype(np.float32)


def kernel(**inputs):
    return run(inputs, n_cores=8)
```
